# Optimizing a Trainium2 kernel written in Bass

```python
import jax, jax.numpy as jnp
from jax import lax
import numpy as np

D_MODEL = 1024
BATCH = 16
SEQ = 2048
DEPTH = 2
DEC_BATCH = 128
DEC_SEQ = 8
PAST_LEN = 16384
PAGE_SIZE = 128

HEAD_DIM = 64
N_HEADS = D_MODEL // 128
KV_HEADS = N_HEADS // 4
Q_W = N_HEADS * HEAD_DIM
KV_W = KV_HEADS * HEAD_DIM
WINDOW = 128
CHUNK = 128
GM_GROUPS = 4
GM_W = D_MODEL // 2
GM_GW = GM_W // GM_GROUPS
SC_W = D_MODEL // 2
CONV_W = 3
D_FF = 2816
N_BRANCH = 3
IN_COLS = Q_W + 2 * KV_W + 2 * GM_W + 3 * SC_W
ALPHA = (2.0 * DEPTH) ** 0.25
BETA = (8.0 * DEPTH) ** -0.25
LN_EPS = 1e-5

kernel_name = "hybrid_swa_gmlp_shortconv_convffn_deepnorm_step"


def layer_norm(x, g, b):
    xf = x.astype(jnp.float32)
    mu = xf.mean(-1, keepdims=True)
    var = jnp.square(xf - mu).mean(-1, keepdims=True)
    return ((xf - mu) * lax.rsqrt(var + LN_EPS) * g.astype(jnp.float32) + b.astype(jnp.float32)).astype(x.dtype)


def causal_dwconv(x, w, past):
    T = x.shape[1]
    xp = jnp.concatenate([past.astype(x.dtype), x], axis=1)
    y = w[0] * xp[:, 0:T]
    for k in range(1, CONV_W):
        y = y + w[k] * xp[:, k:k + T]
    return y, xp[:, -(CONV_W - 1):]


def alibi_slopes():
    return 2.0 ** (-8.0 * jnp.arange(1, N_HEADS + 1, dtype=jnp.float32) / N_HEADS)


def swa_attend(q, kk, vv, key_valid, sinks):
    B, N, T = q.shape[:3]
    S = kk.shape[2]
    G = N_HEADS // KV_HEADS
    qg = q.reshape(B, N, T, KV_HEADS, G, HEAD_DIM)
    s = jnp.einsum('bntkgd,bnskd->bnkgts', qg, kk).astype(jnp.float32) * (HEAD_DIM ** -0.5)
    dist_i = jnp.arange(T)[:, None] + WINDOW - jnp.arange(S)[None, :]
    slopes = alibi_slopes().reshape(KV_HEADS, G)
    s = s - slopes[:, :, None, None] * dist_i.astype(jnp.float32)
    allowed = (dist_i >= 0) & (dist_i <= WINDOW)
    mask = allowed[None, :, :] & key_valid[:, None, :]
    s = jnp.where(mask[None, :, None, None], s, -jnp.inf)
    sink = sinks.astype(jnp.float32).reshape(KV_HEADS, G)[:, :, None, None]
    m = jnp.maximum(s.max(-1, keepdims=True), sink)
    p = jnp.exp(s - m)
    denom = p.sum(-1, keepdims=True) + jnp.exp(sink - m)
    o = jnp.einsum('bnkgts,bnskd->bntkgd', (p / denom).astype(vv.dtype), vv)
    return o.reshape(B, N * T, Q_W)


def token_mixer(x, prompt, k_buf, v_buf, conv_past, w_in, w_gate, b_gate, gmlp_ln_g, gmlp_ln_b,
                gmlp_ws, gmlp_bs, mixconv_w, sinks, p_attn, p_gmlp, p_conv, w_o):
    B, T, _ = x.shape
    h = x @ w_in
    offs = np.cumsum([Q_W, KV_W, KV_W, GM_W, GM_W, SC_W, SC_W]).tolist()
    q, k, v, gu, gv, sb, sc, sh = jnp.split(h, offs, axis=-1)

    q = q.reshape(B, T, N_HEADS, HEAD_DIM)
    k = k.reshape(B, T, KV_HEADS, HEAD_DIM)
    v = v.reshape(B, T, KV_HEADS, HEAD_DIM)
    if prompt:
        N = T // WINDOW
        qb = q.reshape(B, N, WINDOW, N_HEADS, HEAD_DIM)
        kb = k.reshape(B, N, WINDOW, KV_HEADS, HEAD_DIM)
        vb = v.reshape(B, N, WINDOW, KV_HEADS, HEAD_DIM)
        kk = jnp.concatenate([jnp.concatenate([jnp.zeros_like(kb[:, :1]), kb[:, :-1]], axis=1), kb], axis=2)
        vv = jnp.concatenate([jnp.concatenate([jnp.zeros_like(vb[:, :1]), vb[:, :-1]], axis=1), vb], axis=2)
        key_valid = (jnp.arange(N)[:, None] * WINDOW + jnp.arange(2 * WINDOW)[None, :] - WINDOW) >= 0
        attn = swa_attend(qb, kk, vv, key_valid, sinks)
        new_k, new_v = k[:, -WINDOW:], v[:, -WINDOW:]
    else:
        kk = jnp.concatenate([k_buf.astype(k.dtype), k], axis=1)
        vv = jnp.concatenate([v_buf.astype(v.dtype), v], axis=1)
        key_valid = jnp.ones((1, WINDOW + T), dtype=bool)
        attn = swa_attend(q[:, None], kk[:, None], vv[:, None], key_valid, sinks)
        new_k, new_v = kk[:, -WINDOW:], vv[:, -WINDOW:]

    gv = layer_norm(gv, gmlp_ln_g, gmlp_ln_b)
    Tc = CHUNK if prompt else T
    N = T // Tc
    vr = gv.reshape(B, N, Tc, GM_GROUPS, GM_GW)
    ws = jnp.tril(gmlp_ws[:, :Tc, :Tc])
    sv = jnp.einsum('gts,bnsgc->bntgc', ws, vr) + gmlp_bs[:, :Tc].T[None, None, :, :, None]
    gm = gu * sv.reshape(B, T, GM_W)

    past = jnp.zeros((B, CONV_W - 1, SC_W), x.dtype) if prompt else conv_past
    cz, new_conv = causal_dwconv(sc * sh, mixconv_w, past)
    scv = sb * cz

    gates = jax.nn.sigmoid(x @ w_gate + b_gate).reshape(B, T, N_BRANCH, D_MODEL)
    merged = (gates[:, :, 0] * (attn @ p_attn) + gates[:, :, 1] * (gm @ p_gmlp)
              + gates[:, :, 2] * (scv @ p_conv))
    return merged @ w_o, new_k, new_v, new_conv, gv


def conv_ffn(x, prompt, past, w_up, conv_w, conv_b, w_down):
    B = x.shape[0]
    up = x @ w_up
    if prompt:
        past = jnp.zeros((B, CONV_W - 1, 2 * D_FF), x.dtype)
    c, new_past = causal_dwconv(up, conv_w, past)
    c = c + conv_b
    a, g = jnp.split(c, [D_FF], axis=-1)
    return (jax.nn.silu(g) * a) @ w_down, new_past


def setup_inputs(seed: int = 0) -> dict:
    key = jax.random.key(seed)
    ks = iter(jax.random.split(key, 32))

    def nrm(shape, scale):
        return jax.random.normal(next(ks), shape, jnp.float32) * scale

    L, D = DEPTH, D_MODEL
    return {
        "x_prompt": nrm((BATCH, SEQ, D), 1.0),
        "x_sample": nrm((DEC_BATCH, DEC_SEQ, D), 1.0),
        "cache_k_win": nrm((L, DEC_BATCH, WINDOW, KV_HEADS, HEAD_DIM), 1.0),
        "cache_v_win": nrm((L, DEC_BATCH, WINDOW, KV_HEADS, HEAD_DIM), 1.0),
        "state_mixconv": nrm((L, DEC_BATCH, CONV_W - 1, SC_W), 1.0),
        "state_ffnconv": nrm((L, DEC_BATCH, CONV_W - 1, 2 * D_FF), 1.0),
        "w_in": nrm((L, D, IN_COLS), D ** -0.5),
        "w_gate": nrm((L, D, N_BRANCH * D), D ** -0.5),
        "b_gate": nrm((L, N_BRANCH * D), 0.1),
        "gmlp_ln_g": 1.0 + nrm((L, GM_W), 0.1),
        "gmlp_ln_b": nrm((L, GM_W), 0.1),
        "gmlp_ws": nrm((L, GM_GROUPS, CHUNK, CHUNK), CHUNK ** -0.5),
        "gmlp_bs": 1.0 + nrm((L, GM_GROUPS, CHUNK), 0.1),
        "mixconv_w": nrm((L, CONV_W, SC_W), CONV_W ** -0.5),
        "attn_sinks": nrm((L, N_HEADS), 0.5),
        "p_attn": nrm((L, Q_W, D), BETA * Q_W ** -0.5),
        "p_gmlp": nrm((L, GM_W, D), BETA * GM_W ** -0.5),
        "p_conv": nrm((L, SC_W, D), BETA * SC_W ** -0.5),
        "w_o": nrm((L, D, D), BETA * D ** -0.5),
        "ln1_g": 1.0 + nrm((L, D), 0.1),
        "ln1_b": nrm((L, D), 0.1),
        "w_up": nrm((L, D, 2 * D_FF), D ** -0.5),
        "ffn_conv_w": nrm((L, CONV_W, 2 * D_FF), CONV_W ** -0.5),
        "ffn_conv_b": nrm((L, 2 * D_FF), 0.02),
        "w_down": nrm((L, D_FF, D), BETA * D_FF ** -0.5),
        "ln2_g": 1.0 + nrm((L, D), 0.1),
        "ln2_b": nrm((L, D), 0.1),
    }


def reference(x_prompt, x_sample, cache_k_win, cache_v_win, state_mixconv, state_ffnconv,
              w_in, w_gate, b_gate, gmlp_ln_g, gmlp_ln_b, gmlp_ws, gmlp_bs, mixconv_w, attn_sinks,
              p_attn, p_gmlp, p_conv, w_o, ln1_g, ln1_b, w_up, ffn_conv_w, ffn_conv_b, w_down,
              ln2_g, ln2_b):
    xp, xs = x_prompt, x_sample
    kp, vp, mcp, fcp = [], [], [], []
    ksm, vsm, mcs, fcs, gvs = [], [], [], [], []
    for l in range(DEPTH):
        mix_w = (w_in[l], w_gate[l], b_gate[l], gmlp_ln_g[l], gmlp_ln_b[l], gmlp_ws[l], gmlp_bs[l],
                 mixconv_w[l], attn_sinks[l], p_attn[l], p_gmlp[l], p_conv[l], w_o[l])
        m, nk, nv, nc, _ = token_mixer(xp, True, None, None, None, *mix_w)
        xp = layer_norm(ALPHA * xp + m, ln1_g[l], ln1_b[l])
        f, nf = conv_ffn(xp, True, None, w_up[l], ffn_conv_w[l], ffn_conv_b[l], w_down[l])
        xp = layer_norm(ALPHA * xp + f, ln2_g[l], ln2_b[l])
        kp.append(nk); vp.append(nv); mcp.append(nc); fcp.append(nf)
        m, nk, nv, nc, gv = token_mixer(xs, False, cache_k_win[l], cache_v_win[l], state_mixconv[l], *mix_w)
        xs = layer_norm(ALPHA * xs + m, ln1_g[l], ln1_b[l])
        f, nf = conv_ffn(xs, False, state_ffnconv[l], w_up[l], ffn_conv_w[l], ffn_conv_b[l], w_down[l])
        xs = layer_norm(ALPHA * xs + f, ln2_g[l], ln2_b[l])
        ksm.append(nk); vsm.append(nv); mcs.append(nc); fcs.append(nf); gvs.append(gv)
    return (xp, xs,
            jnp.stack(kp), jnp.stack(vp), jnp.stack(mcp), jnp.stack(fcp),
            jnp.stack(ksm), jnp.stack(vsm), jnp.stack(mcs), jnp.stack(fcs), jnp.stack(gvs))
```

```python
import numpy as np
from contextlib import ExitStack
import concourse.bass as bass
import concourse.mybir as mybir
from concourse.bass_utils import run_bass_kernel_spmd

F32 = mybir.dt.float32
BF16 = mybir.dt.bfloat16
AF = mybir.ActivationFunctionType
ALU = mybir.AluOpType

ENGS = ['pe', 'act', 'dve', 'pool', 'sp']


class Sched:
    def __init__(self, nc, es):
        self.nc = nc
        self.es = es
        self.q = {e: [] for e in ENGS}
        self.esem = {}
        for e in ENGS:
            if e != 'sp':
                self.esem[e] = es.enter_context(nc.semaphore("prog_" + e))
        self.ecnt = {e: 0 for e in ENGS}
        self.seen = {e: {} for e in ENGS}
        self.lastw = {}
        self.readers = {}
        self.chans = {}
        self.out_chans = set()
        self.semname = {}
        self.ntask = 0

    def sb(self, name, shape, dtype):
        return self.es.enter_context(self.nc.sbuf_tensor(name, shape, dtype))

    def ps(self, name, shape, dtype):
        return self.es.enter_context(self.nc.psum_tensor(name, shape, dtype))

    def _deps(self, eng, reads, writes):
        deps = []
        for k in reads:
            t = self.lastw.get(k)
            if t is not None:
                deps.append(t)
            if isinstance(k, tuple) and k[0] == 'pb':
                r = self.readers.get(k)
                if r:
                    deps.extend(v for s_, v in r.items() if s_ != eng)
        for k in writes:
            t = self.lastw.get(k)
            if t is not None:
                deps.append(t)
            r = self.readers.get(k)
            if r:
                deps.extend(r.values())
        waits = {}
        seen = self.seen[eng]
        for (sid, sem, v) in deps:
            if eng == 'pe' and sid == 'pe':
                continue
            if seen.get(sid, 0) >= v:
                continue
            if sid not in waits or waits[sid][1] < v:
                waits[sid] = (sem, v)
        for sid, (sem, v) in waits.items():
            seen[sid] = v
        return list(waits.values())

    def _commit(self, tok, reads, writes):
        sid = tok[0]
        for k in reads:
            r = self.readers.setdefault(k, {})
            r[sid] = tok
        for k in writes:
            self.lastw[k] = tok
            self.readers[k] = {}

    def op(self, eng, fn, reads=(), writes=()):
        waits = self._deps(eng, reads, writes)
        self.ecnt[eng] += 1
        tok = (eng, self.esem[eng], self.ecnt[eng])
        self.q[eng].append((waits, fn, tok, 1))
        self._commit(tok, reads, writes)
        self.ntask += 1
        return tok

    def dma(self, eng, out, in_, chan, reads=(), writes=(), out_final=False, n=1, fn=None, **kw):
        if chan not in self.chans:
            self.chans[chan] = [self.es.enter_context(self.nc.semaphore("c_" + chan)), 0]
        c = self.chans[chan]
        waits = self._deps(eng, reads, writes)
        c[1] += 16 * n
        tok = ('c_' + chan, c[0], c[1])
        if fn is None:
            def fn(e, out=out, in_=in_, kw=kw):
                return [e.dma_start(out=out, in_=in_, **kw)]
        self.q[eng].append((waits, fn, tok, 16))
        self._commit(tok, reads, writes)
        if out_final:
            self.out_chans.add(chan)
        self.ntask += 1
        return tok

    def finish(self):
        nc = self.nc
        engmap = {'pe': 'tensor', 'act': 'scalar', 'dve': 'vector', 'pool': 'gpsimd', 'sp': 'sync'}
        finals = [(self.chans[c][0], self.chans[c][1]) for c in sorted(self.chans)]

        def run(e, name):
            for waits, fn, tok, inc in self.q[name]:
                for (sem, v) in waits:
                    e.wait_ge(sem, v)
                r = fn(e)
                if inc == 16:
                    for ins in r:
                        ins.then_inc(tok[1], 16)
                else:
                    r.then_inc(tok[1], 1)
            if name == 'sp':
                for (sem, v) in finals:
                    e.wait_ge(sem, v)

        with nc.Block() as block:
            for name in ENGS:
                getattr(block, engmap[name])(lambda e, name=name: run(e, name))


D = 1024
KC = 8
DFF = 2816
NPAIR = 22
NUP = 44
L = 2
ALPHA = float((2.0 * L) ** 0.25)
EPS = 1e-5
C_Q, C_K, C_V, C_GU, C_GV, C_SB, C_SC, C_SH = 0, 512, 640, 768, 1280, 1792, 2304, 2816
NSLOT = 3
SLOT_EL = 4096


class Builder:
    def __init__(self, nseq=2, seqlen=2048, with_sample=True):
        self.nseq = nseq
        self.seqlen = seqlen
        self.ngrp = seqlen // 1024
        self.with_sample = with_sample
        self.nc = bass.Bass("TRN2", target_bir_lowering=False)
        self.nb = 0
        self.ntf = 0
        self.ntb = 0
        self.nln = 0

    def declare(self):
        nc = self.nc

        def din(name, shape):
            return nc.dram_tensor(name, list(shape), F32, kind="ExternalInput").ap()

        def dout(name, shape):
            return nc.dram_tensor(name, list(shape), F32, kind="ExternalOutput").ap()

        ns, sl = self.nseq, self.seqlen
        self.xp = din("xp", [ns, sl, D])
        self.xs = din("xs", [128, D])
        self.ck = din("ck", [L, 16, 128, 128])
        self.cv = din("cv", [L, 16, 128, 128])
        self.smix = din("smix", [L, 32, 512])
        self.sffn = din("sffn", [L, 32, 2 * DFF])
        self.w_in = din("w_in", [L, D, 3328])
        self.w_gate = din("w_gate", [L, D, 3072])
        self.b_gate = din("b_gate", [L, 24, 128])
        self.gln_g = din("gln_g", [L, 512])
        self.gln_b = din("gln_b", [L, 512])
        self.gws = din("gws", [L, 4, 128, 128])
        self.gbs = din("gbs", [L, 512])
        self.mixw = din("mixw", [L, 12, 128])
        self.sinks = din("sinks", [L, 8])
        self.p_br = [din("p_attn", [L, 512, D]), din("p_gmlp", [L, 512, D]), din("p_conv", [L, 512, D])]
        self.w_o = din("w_o", [L, D, D])
        self.ln1_g = din("ln1_g", [L, D])
        self.ln1_b = din("ln1_b", [L, D])
        self.w_up = din("w_up", [L, D, 2 * DFF])
        self.fcw = din("fcw", [L, 132, 128])
        self.fcb = din("fcb", [L, 44, 128])
        self.w_down = din("w_down", [L, DFF, D])
        self.ln2_g = din("ln2_g", [L, D])
        self.ln2_b = din("ln2_b", [L, D])
        self.scr = nc.dram_tensor("wscr", [L, 33, 128, SLOT_EL], BF16, kind="Internal").ap()
        self.scr_wd = nc.dram_tensor("wdscr", [L, 2, 128, 11 * 1024], BF16, kind="Internal").ap()
        self.tabs = din("tabs", [4, 128, 1024])
        self.masks = din("masks", [2, 128, 128])
        self.y_p = dout("y_p", [ns, sl, D])
        self.y_s = dout("y_s", [128, D])
        self.kwp = dout("kwp", [L, ns, 128, 128])
        self.vwp = dout("vwp", [L, ns, 128, 128])
        self.mcp = dout("mcp", [L, ns, 2, 512])
        self.fcp = dout("fcp", [L, ns, 2, 2 * DFF])
        self.kws = dout("kws", [L, 16, 128, 128])
        self.vws = dout("vws", [L, 16, 128, 128])
        self.mcs = dout("mcs", [L, 32, 512])
        self.fcs = dout("fcs", [L, 32, 2 * DFF])
        self.gvs = dout("gvs", [L, 128, 512])

    def alloc(self, S):
        sb = S.sb
        self.x_tok = sb("x_tok", [128, 8, D], F32)
        self.xT = sb("xT", [128, 8, 1024], BF16)
        self.ar = sb("ar", [128, 24, 1024], BF16)
        self.uT = sb("uT", [128, 4, 1026], BF16)
        self.kT = sb("kT", [128, 1152], BF16)
        self.vaug = sb("vaug", [128, 9, 2, 65], BF16)
        self.wr = [sb("wr%d" % i, [128, SLOT_EL], BF16) for i in range(NSLOT)]
        self.tf = sb("tf", [128, 8, 512], F32)
        self.tb = sb("tb", [128, 10, 512], BF16)
        self.macc = sb("macc", [128, 4, 512], F32)
        self.lnbc = sb("lnbc", [128, 2, 1024], F32)
        self.glnbc = sb("glnbc", [128, 2, 512], F32)
        self.E = sb("E", [128, 4, 1024], BF16)
        self.wst = sb("wst", [128, 4, 128], BF16)
        self.gbias = sb("gbias", [128, 4, 128], F32)
        self.identf = sb("identf", [128, 128], F32)
        self.identb = sb("identb", [128, 128], BF16)
        self.maskt = sb("maskt", [128, 2, 128], F32)
        self.prm = sb("prm", [128, L, 212], F32)
        self.esink = sb("esink", [128, L, 8], F32)
        self.lnst = sb("lnst", [128, 4, 16], F32)
        self.att_s = sb("att_s", [128, 4, 16], F32)
        self.kcar = sb("kcar", [128, L, 128], BF16)
        self.vcar = sb("vcar", [128, L, 2, 65], BF16)
        self.ucar = sb("ucar", [128, L, 4, 2], BF16)
        self.upc = sb("upc", [128, L, NUP, 2, 3], F32)
        self.stg0 = sb("stg0", [128, 256], F32)
        self.stg1 = sb("stg1", [128, 512], F32)
        self.kcT = sb("kcT", [128, 16, 128], BF16)
        self.vaugc = sb("vaugc", [128, 16, 2, 65], BF16)
        self.pb = [S.ps("pb%d" % i, [128, 512], F32) for i in range(8)]

    def bank(self):
        b = self.nb % 8
        self.nb += 1
        return b

    def tfs(self):
        i = self.ntf % 8
        self.ntf += 1
        return i

    def ffs(self):
        i = getattr(self, '_nff', 0) % 12
        self._nff = getattr(self, '_nff', 0) + 1
        if i < 8:
            return self.tf[:, i, :], ('tf', i)
        return self.macc[:, i - 8, :], ('macc', i - 8)

    def tbs(self):
        i = self.ntb % 10
        self.ntb += 1
        return i

    @staticmethod
    def akeys(u0, u1, c0, n):
        return [('ar', u, cb) for u in range(u0, u1) for cb in range(c0 // 128, (c0 + n + 127) // 128)]

    def make_pieces(self, groups):
        pcs = []
        for gi_, grp in enumerate(groups):
            for l in range(L):
                base_ = len(pcs)
                def win(c0, n, l=l):
                    return [(0, (8, n), self.w_in[l, :, c0:c0 + n].rearrange("(k p) c -> p k c", p=128))]
                pcs.append(win(C_Q, 512))
                pcs.append(win(C_K, 256))
                pcs.append(win(C_GU, 512))
                pcs.append(win(C_GV, 512))
                pcs.append(win(C_SB, 512))
                pcs.append(win(C_SC, 512))
                pcs.append(win(C_SH, 512))
                for dp in range(4):
                    for i in range(3):
                        c0 = i * 1024 + dp * 256
                        pcs.append([
                            (0, (8, 256), self.w_gate[l, :, c0:c0 + 256].rearrange("(k p) c -> p k c", p=128)),
                            (2048, (4, 256), self.p_br[i][l, :, dp * 256:(dp + 1) * 256].rearrange("(k p) c -> p k c", p=128)),
                        ])
                for ch in range(2):
                    pcs.append([(0, (8, 512), self.w_o[l, :, ch * 512:(ch + 1) * 512].rearrange("(k p) c -> p k c", p=128))])
                for hf in range(2):
                    for (j0, nj) in self.up_pieces(hf):
                        n = nj * 128
                        pcs.append([
                            (0, (8, n), self.w_up[l, :, j0 * 128:j0 * 128 + n].rearrange("(k p) c -> p k c", p=128)),
                            (8 * n, (8, n), self.w_up[l, :, DFF + j0 * 128:DFF + j0 * 128 + n].rearrange("(k p) c -> p k c", p=128)),
                        ])
                for i_ in range(base_, len(pcs)):
                    parts_ = pcs[i_]
                    nel_ = max(off + k * n for (off, (k, n), _) in parts_)
                    pcs[i_] = dict(parts=parts_, l=l, pidx=i_ - base_, nel=nel_, first=(gi_ == 0), multi=(len(groups) > 1))
                assert len(pcs) - base_ == 33
        return pcs

    @staticmethod
    def up_pieces(hf):
        j = hf * 11
        out = []
        for nj in (2, 2, 2, 2, 2, 1):
            out.append((j, nj))
            j += nj
        return out

    def w_init(self, S, pieces):
        self.pieces = pieces
        self.p_loaded = 0
        self.p_next = 0
        self.slot_free = [True] * NSLOT
        self._w_pump(S)

    def _w_pump(self, S):
        while self.p_loaded < len(self.pieces):
            slot = self.p_loaded % NSLOT
            if not self.slot_free[slot]:
                break
            pc = self.pieces[self.p_loaded]
            parts = pc['parts']
            wr = self.wr[slot]
            skey = ('scr', pc['l'], pc['pidx'])
            sap = self.scr[pc['l'], pc['pidx'], :, 0:pc['nel']]
            if pc['first'] or not pc['multi']:
                def fn(e, parts=parts, wr=wr):
                    r = []
                    for (off, (k, n), src) in parts:
                        dst = wr[:, off:off + k * n].rearrange("p (k c) -> p k c", k=k)
                        r.append(e.dma_start(out=dst, in_=src))
                    return r
                S.dma('pool', None, None, chan='wr%d' % slot, writes=[('wr', slot)], n=len(parts), fn=fn)
                if pc['multi']:
                    S.dma('sp', sap, wr[:, 0:pc['nel']], chan='wb%d' % slot, reads=[('wr', slot)], writes=[skey])
            else:
                S.dma('pool', wr[:, 0:pc['nel']], sap, chan='wr%d' % slot, reads=[skey], writes=[('wr', slot)])
            self.slot_free[slot] = False
            self.p_loaded += 1

    def w_get(self, S):
        assert self.p_next < self.p_loaded, "weight piece not loaded (ring too small)"
        slot = self.p_next % NSLOT
        self.p_next += 1
        return slot

    def w_done(self, S, slot):
        import os
        if 'nopump' in os.environ.get('KDBG', ''):
            return
        self.slot_free[slot] = True
        self._w_pump(S)

    def wv(self, slot, off, k, n):
        return self.wr[slot][:, off:off + k * n].rearrange("p (k c) -> p k c", k=k)

    def _patch(self, S, keys, tok):
        for k in keys:
            S.lastw[k] = tok

    def init_consts(self, S):
        S.op('pool', lambda e: e.memset(self.identf[:], 0.0), writes=['identf'])
        S.op('pool', lambda e: e.affine_select(out=self.identf[:], in_=self.identf[:], pattern=[[-1, 128]],
                                               compare_op=ALU.not_equal, fill=1.0, base=0, channel_multiplier=1),
             reads=['identf'], writes=['identf'])
        S.op('dve', lambda e: e.tensor_copy(self.identb[:], self.identf[:]), reads=['identf'], writes=['identb'])
        S.op('dve', lambda e: e.memset(self.vaug[:, :, :, 64:65], 1.0), writes=['vaug_ones'])
        S.op('dve', lambda e: e.memset(self.vaugc[:, :, :, 64:65], 1.0), writes=['vaugc_ones'])
        S.op('dve', lambda e: e.memset(self.vcar[:, :, :, 64:65], 1.0), writes=['vcar_ones'])
        keys = ['maskt', 'esink']
        tok = S.dma('sp', self.maskt[:], self.masks.rearrange("m s t -> s m t"), chan='init', writes=['maskt'])
        tmps = []
        for i in range(4):
            s0 = self.tfs(); s1 = self.tfs()
            assert s1 == s0 + 1
            tmp = self.tf[:, s0:s0 + 2, :].rearrange("p a b -> p (a b)")
            tok = S.dma('sp', tmp, self.tabs[i], chan='init', writes=[('tf', s0), ('tf', s1)])
            keys += [('tf', s0), ('tf', s1)]
            tmps.append((tmp, s0, s1))
        tok = S.dma('sp', self.esink[:].rearrange("p l h -> p (l h)"),
                    self.sinks.rearrange("l h -> (l h)").partition_broadcast(128), chan='init', writes=['esink'])
        self._patch(S, keys, tok)
        for i in range(4):
            tmp, s0, s1 = tmps[i]
            S.op('act', lambda e, i=i, tmp=tmp: e.activation(self.E[:, i, :], tmp, AF.Exp),
                 reads=[('tf', s0), ('tf', s1)], writes=[('E', i)])
        S.op('act', lambda e: e.activation(self.esink[:], self.esink[:], AF.Exp), reads=['esink'], writes=['esink'])
        sts = []
        keys = []
        for l in range(L):
            s0 = self.tfs()
            st = self.tf[:, s0, :]
            k = [('tf', s0)]
            S.dma('sp', st[0:24, 0:128], self.b_gate[l], chan='init2', writes=k)
            S.dma('sp', st[24:36, 0:128], self.mixw[l], chan='init2', writes=k)
            S.dma('sp', st[36:80, 0:128], self.fcb[l], chan='init2', writes=k)
            S.dma('sp', st[0:128, 128:256], self.fcw[l, 0:128, :], chan='init2', writes=k)
            tok = S.dma('sp', st[0:4, 256:384], self.fcw[l, 128:132, :], chan='init2', writes=k)
            keys += k
            sts.append((st, s0))
        self._patch(S, keys, tok)
        for l in range(L):
            st, s0 = sts[l]
            b = self.bank()
            pb = self.pb[b]

            def tr(e, st=st, pb=pb):
                e.transpose(pb[:, 0:80], st[0:80, 0:128], self.identf[0:80, 0:80])
                e.transpose(pb[:, 80:208], st[0:128, 128:256], self.identf[:])
                return e.transpose(pb[:, 208:212], st[0:4, 256:384], self.identf[0:4, 0:4])
            S.op('pe', tr, reads=[('tf', s0), 'identf'], writes=[('pb', b)])
            S.op('dve', lambda e, l=l, pb=pb: e.tensor_copy(self.prm[:, l, :], pb[:, 0:212]),
                 reads=[('pb', b)], writes=[('prm', l)])

    def p_bg(self, l, j):
        return self.prm[:, l, j:j + 1]

    def p_mixw(self, l, k, c):
        return self.prm[:, l, 24 + k * 4 + c:24 + k * 4 + c + 1]

    def p_fcb(self, l, col):
        return self.prm[:, l, 36 + col:36 + col + 1]

    def p_fcw(self, l, k, col):
        return self.prm[:, l, 80 + k * 44 + col:80 + k * 44 + col + 1]

    def fm(self, S, slot, wview, c0, src_fn, nk, halves, rkeys, evac, mcols=128, prow=None):
        banks = [self.bank() for _ in halves]

        import os
        seq = 'seq' in os.environ.get('KDBG', '')

        def mm(e):
            r = None
            if seq:
                for hi, (h0, n) in enumerate(halves):
                    for kc in range(nk):
                        r = e.matmul(self.pb[banks[hi]][0:mcols, 0:n], lhsT=wview[:, kc, c0:c0 + mcols],
                                     rhs=src_fn(kc, h0, n), start=(kc == 0), stop=(kc == nk - 1))
                return r
            for kc in range(nk):
                for hi, (h0, n) in enumerate(halves):
                    r = e.matmul(self.pb[banks[hi]][0:mcols, 0:n], lhsT=wview[:, kc, c0:c0 + mcols],
                                 rhs=src_fn(kc, h0, n), start=(kc == 0), stop=(kc == nk - 1))
            return r
        S.op('pe', mm, reads=[('wr', slot)] + rkeys, writes=[('pb', b) for b in banks])
        import os
        if 'noevac' in os.environ.get('KDBG', ''):
            return
        for hi, (h0, n) in enumerate(halves):
            evac(hi, h0, n, banks[hi])

    def sub(self):
        import os
        self._subc = getattr(self, '_subc', 0) + 1
        if self._subc > int(os.environ.get("KSUB", "999")):
            raise StopIteration

    def xt_keys(self, h0, n):
        return [('xT', t) for t in range(h0 // 128, (h0 + n) // 128)]

    def load_x(self, S, grp):
        if grp.get('preloaded'):
            return
        if grp['kind'] == 'p':
            b, g = grp['b'], grp['g']
            tok = None
            for t in range(8):
                r0 = g * 1024 + t * 128
                tok = S.dma('sp', self.x_tok[:, t, :], self.xp[b, r0:r0 + 128, :], chan='xin', writes=[('xtok', t)])
            self._patch(S, [('xtok', t) for t in range(8)], tok)
        else:
            S.dma('sp', self.x_tok[:, 0, :], self.xs, chan='xin', writes=[('xtok', 0)])

    def build_xT(self, S, t):
        for kh in range(2):
            b = self.bank()
            pb = self.pb[b]

            def tr(e, kh=kh, pb=pb):
                r = None
                for k in range(4):
                    kc = kh * 4 + k
                    r = e.transpose(pb[:, k * 128:(k + 1) * 128], self.x_tok[:, t, kc * 128:(kc + 1) * 128], self.identf[:])
                return r
            S.op('pe', tr, reads=[('xtok', t), 'identf'], writes=[('pb', b)])
            S.op('act', lambda e, kh=kh, pb=pb: e.copy(self.xT[:, kh * 4:(kh + 1) * 4, t * 128:(t + 1) * 128],
                                                      pb[:].rearrange("p (k t) -> p k t", k=4)),
                 reads=[('pb', b)], writes=[('xT', t)])

    def layer_norm(self, S, src, dst, width, gam, bet, rkeys, wkeys, bckeys, src_keys_extra=(), recip=True):
        i = self.nln % 4
        self.nln += 1
        st = self.lnst[:, i, :]
        nchunk = width // 512
        tfi = None

        def bn(e):
            r = None
            for c in range(nchunk):
                r = e.bn_stats(st[:, c * 6:(c + 1) * 6], src[:, c * 512:(c + 1) * 512])
            return r
        k = ('lnst', i)
        S.op('dve', bn, reads=rkeys, writes=[k])
        S.op('dve', lambda e: e.bn_aggr(st[:, 12:14], st[:, 0:6 * nchunk]), reads=[k], writes=[k])
        S.op('act', lambda e: e.activation(st[:, 14:15], st[:, 13:14], AF.Sqrt, bias=EPS, scale=1.0), reads=[k], writes=[k])
        if recip:
            S.op('dve', lambda e: e.reciprocal(st[:, 15:16], st[:, 14:15]), reads=[k], writes=[k])
        return st, k

    def proj(self, S, grp, l):
        GT, halves, NTL = grp['GT'], grp['halves'], grp['NTL']
        kind = grp['kind']
        xsrc = lambda kc, h0, n: self.xT[:, kc, h0:h0 + n]
        allx = [('xT', t) for t in range(NTL)]

        def evac_copy(unit):
            def ev(hi, h0, n, b):
                S.op('act', lambda e: e.copy(self.ar[:, unit, h0:h0 + n], self.pb[b][:, 0:n]),
                     reads=[('pb', b)], writes=self.akeys(unit, unit + 1, h0, n))
            return ev
        slot = self.w_get(S)
        W = self.wv(slot, 0, 8, 512)
        import os
        dbg = os.environ.get('KDBG', '')
        for c in range(4):
            self.fm(S, slot, W, c * 128, xsrc, 8, halves, allx, evac_copy(c))
            yield None
        self.w_done(S, slot)
        slot = self.w_get(S)
        W = self.wv(slot, 0, 8, 256)

        def ev_k(hi, h0, n, b):
            S.op('act', lambda e: e.copy(self.kT[:, 128 + h0:128 + h0 + n], self.pb[b][:, 0:n]),
                 reads=[('pb', b)], writes=[('kT', 1 + t) for t in range(h0 // 128, (h0 + n) // 128)])
        self.fm(S, slot, W, 0, xsrc, 8, halves, allx, ev_k)
        yield None
        for t in range(NTL):
            b = self.bank()
            pb = self.pb[b]

            def mm(e, t=t, pb=pb, W=W):
                r = None
                for kc in range(8):
                    r = e.matmul(pb[:, 0:256], lhsT=self.xT[:, kc, t * 128:(t + 1) * 128], rhs=W[:, kc, 0:256],
                                 start=(kc == 0), stop=(kc == 7))
                return r
            S.op('pe', mm, reads=[('wr', slot), ('xT', t)], writes=[('pb', b)])
            if 'novaug' not in dbg:
              S.op(os.environ.get('KVENG', 'dve'), lambda e, t=t, pb=pb: (e.tensor_copy if os.environ.get('KVENG', 'dve') == 'dve' else e.copy)(self.vaug[:, t + 1, :, 0:64],
                                                             pb[:, 128:256].rearrange("p (k d) -> p k d", k=2)),
                 reads=[('pb', b)], writes=[('vaug', t + 1)])
            if 'nostg' in dbg:
                continue
            if (kind == 'p' and grp['last'] and t == NTL - 1) or kind == 's':
                S.op('act', lambda e, pb=pb: e.copy(self.stg0[:, 0:256], pb[:, 0:256]), reads=[('pb', b)], writes=[('stg', 0)])
                if kind == 'p':
                    S.dma('sp', self.kwp[l, grp['b']], self.stg0[:, 0:128], chan='st0', reads=[('stg', 0)], out_final=True)
                    S.dma('sp', self.vwp[l, grp['b']], self.stg0[:, 128:256], chan='st0b', reads=[('stg', 0)], out_final=True)
                else:
                    S.dma('sp', self.kws[l][:, 120:128, :], self.stg0[:, 0:128], chan='st0', reads=[('stg', 0)], out_final=True)
                    S.dma('sp', self.vws[l][:, 120:128, :], self.stg0[:, 128:256], chan='st0b', reads=[('stg', 0)], out_final=True)
                    S.dma('sp', self.kws[l][:, 0:120, :], self.ck[l][:, 8:128, :], chan='cc0', out_final=True)
                    S.dma('sp', self.vws[l][:, 0:120, :], self.cv[l][:, 8:128, :], chan='cc1', out_final=True)
            yield None
        self.w_done(S, slot)
        yield 'KV_DONE'
        slot = self.w_get(S)
        W = self.wv(slot, 0, 8, 512)
        for c in range(4):
            self.fm(S, slot, W, c * 128, xsrc, 8, halves, allx, evac_copy(4 + c))
            yield None
        self.w_done(S, slot)
        slot = self.w_get(S)
        W = self.wv(slot, 0, 8, 512)
        for t in range(NTL):
            b = self.bank()
            pb = self.pb[b]

            def mm(e, t=t, pb=pb, W=W):
                r = None
                for kc in range(8):
                    r = e.matmul(pb[:, 0:512], lhsT=self.xT[:, kc, t * 128:(t + 1) * 128], rhs=W[:, kc, 0:512],
                                 start=(kc == 0), stop=(kc == 7))
                return r
            S.op('pe', mm, reads=[('wr', slot), ('xT', t)], writes=[('pb', b)])
            st, k = self.layer_norm(S, pb[:, 0:512], None, 512, None, None, [('pb', b)], None, None)
            ti = self.tfs()
            tmp = self.tf[:, ti, :]
            S.op('dve', lambda e, pb=pb, tmp=tmp, st=st: e.scalar_tensor_tensor(tmp, pb[:, 0:512], st[:, 12:13], self.glnbc[:, 0, :], ALU.subtract, ALU.mult),
                 reads=[('pb', b), k, ('glnbc', l)], writes=[('tf', ti)])
            u, c0 = 12 + t // 2, (t % 2) * 512
            gv_dst = self.ar[:, u, c0:c0 + 512]
            if kind == 's':
                S.op('dve', lambda e, tmp=tmp, st=st: e.scalar_tensor_tensor(tmp, tmp, st[:, 15:16], self.glnbc[:, 1, :], ALU.mult, ALU.add),
                     reads=[('tf', ti), ('glnbc', l), k], writes=[('tf', ti)])
                S.op('act', lambda e, tmp=tmp, gv_dst=gv_dst: e.copy(gv_dst, tmp), reads=[('tf', ti)], writes=self.akeys(u, u + 1, c0, 512))
                S.dma('sp', self.gvs[l], tmp, chan='gvs', reads=[('tf', ti)], out_final=True)
            else:
                S.op('dve', lambda e, tmp=tmp, gv_dst=gv_dst, st=st: e.scalar_tensor_tensor(gv_dst, tmp, st[:, 15:16], self.glnbc[:, 1, :], ALU.mult, ALU.add),
                     reads=[('tf', ti), ('glnbc', l), k], writes=self.akeys(u, u + 1, c0, 512))
            if t >= 2:
                self.gmlp_block(S, grp, l, t - 2)
            yield None
        self.w_done(S, slot)
        slot = self.w_get(S)
        W = self.wv(slot, 0, 8, 512)
        for c in range(4):
            self.fm(S, slot, W, c * 128, xsrc, 8, halves, allx, evac_copy(8 + c))
            if c < 2 and NTL - 2 + c >= 0 and NTL >= 2:
                self.gmlp_block(S, grp, l, NTL - 2 + c)
            yield None
        if NTL < 2:
            self.gmlp_block(S, grp, l, 0)
        self.w_done(S, slot)
        self.conv_carry_in(S, grp, l)
        slot_c = self.w_get(S)
        slot_h = self.w_get(S)
        Wc = self.wv(slot_c, 0, 8, 512)
        Wh = self.wv(slot_h, 0, 8, 512)
        for c in range(4):
            tmps = {}

            def ev_c(hi, h0, n, b, tmps=tmps):
                ti = self.tfs()
                tmps[hi] = ti
                S.op('act', lambda e: e.copy(self.tf[:, ti, 0:n], self.pb[b][:, 0:n]), reads=[('pb', b)], writes=[('tf', ti)])

            def ev_h(hi, h0, n, b, tmps=tmps, c=c):
                ti = tmps[hi]
                if kind == 'p':
                    dst = self.uT[:, c, 2 + h0:2 + h0 + n]
                    in0 = self.pb[b][:, 0:n]
                    in1 = self.tf[:, ti, 0:n]
                else:
                    dst = self.uT[:, c, 0:160].rearrange("p (s j) -> p s j", j=10)[:, :, 2:10]
                    in0 = self.pb[b][:, 0:n].rearrange("p (s j) -> p s j", j=8)
                    in1 = self.tf[:, ti, 0:n].rearrange("p (s j) -> p s j", j=8)
                S.op('dve', lambda e: e.tensor_tensor(dst, in0, in1, ALU.mult),
                     reads=[('pb', b), ('tf', ti)], writes=[('uT', c, hi)])
            self.fm(S, slot_c, Wc, c * 128, xsrc, 8, halves, allx, ev_c)
            yield None
            self.fm(S, slot_h, Wh, c * 128, xsrc, 8, halves, allx, ev_h)
            self.conv_chunk(S, grp, l, c)
            yield None
        self.conv_carry_out(S, grp, l)
        if (kind == 'p' and grp['last']) or kind == 's':
            t = NTL - 1
            bc = self.bank(); bh = self.bank()

            def mm(e, t=t, bc=bc, bh=bh, Wc=Wc, Wh=Wh):
                r = None
                for (W_, bb) in ((Wc, bc), (Wh, bh)):
                    for kc in range(8):
                        r = e.matmul(self.pb[bb][:, 0:512], lhsT=self.xT[:, kc, t * 128:(t + 1) * 128], rhs=W_[:, kc, 0:512],
                                     start=(kc == 0), stop=(kc == 7))
                return r
            S.op('pe', mm, reads=[('wr', slot_c), ('wr', slot_h), ('xT', t)], writes=[('pb', bc), ('pb', bh)])
            ti = self.tfs()
            S.op('act', lambda e, ti=ti, bc=bc: e.copy(self.tf[:, ti, :], self.pb[bc][:, 0:512]), reads=[('pb', bc)], writes=[('tf', ti)])
            S.op('dve', lambda e, ti=ti, bh=bh: e.tensor_tensor(self.stg1[:, :], self.pb[bh][:, 0:512], self.tf[:, ti, :], ALU.mult),
                 reads=[('pb', bh), ('tf', ti)], writes=[('stg', 1)])
            if kind == 'p':
                S.dma('sp', self.mcp[l, grp['b']], self.stg1[126:128, :], chan='st1', reads=[('stg', 1)], out_final=True)
            else:
                for r in range(2):
                    S.dma('sp', self.mcs[l].rearrange("(s r) c -> s r c", r=2)[:, r, :],
                          self.stg1[:, :], chan='st1%d' % r, reads=[('stg', 1)], out_final=True,
                          fn=(lambda e, r=r: [e.dma_start(out=self.mcs[l].rearrange("(s r) c -> s r c", r=2)[:, r, :],
                                                          in_=self.sel_rows(self.stg1[:, :], 6 + r))]))
        self.w_done(S, slot_c)
        self.w_done(S, slot_h)

    def sel_rows(self, ap2d, r):
        return ap2d[r:128:8, :]

    def attn_pv_norm(self, S, l, t, pv_fn, extra_reads, outcols):
        bA = self.bank(); bB = self.bank()
        banks = (bA, bB)
        S.op('pe', lambda e: pv_fn(e, self.pb[bA], self.pb[bB]), reads=extra_reads, writes=[('pb', bA), ('pb', bB)])
        i = self.nln % 4
        self.nln += 1
        sc = self.att_s[:, i, :]
        k = ('atts', i)
        for gi in range(2):
            pv = self.pb[banks[gi]][:, 0:260].rearrange("p (h d) -> p h d", h=4)
            S.op('dve', lambda e, pv=pv, gi=gi: e.tensor_tensor(sc[:, gi * 4:(gi + 1) * 4].unsqueeze(2), pv[:, :, 64:65],
                                                               self.esink[:, l, gi * 4:(gi + 1) * 4].unsqueeze(2), ALU.add),
                 reads=[('pb', banks[gi]), 'esink'], writes=[k])
        S.op('dve', lambda e: e.reciprocal(sc[:, 8:16], sc[:, 0:8]), reads=[k], writes=[k])
        tb = self.tbs()
        at = self.tb[:, tb, :]
        for gi in range(2):
            pv = self.pb[banks[gi]][:, 0:260].rearrange("p (h d) -> p h d", h=4)
            S.op('dve', lambda e, pv=pv, gi=gi: e.tensor_tensor(
                at[:, gi * 256:(gi + 1) * 256].rearrange("p (h d) -> p h d", h=4), pv[:, :, 0:64],
                sc[:, 8 + gi * 4:8 + (gi + 1) * 4].unsqueeze(2).to_broadcast([128, 4, 64]), ALU.mult),
                reads=[('pb', banks[gi]), k], writes=[('tb', tb)])
        return at, tb

    def attn_tr(self, S, at, tb, outcols):
        bT = self.bank()
        pbt = self.pb[bT][:].bitcast(BF16)

        def tr(e):
            r = None
            for c in range(4):
                r = e.transpose(pbt[:, c * 128:(c + 1) * 128], at[:, c * 128:(c + 1) * 128], self.identb[:])
            return r
        S.op('pe', tr, reads=[('tb', tb), 'identb'], writes=[('pb', bT)])
        S.op('act', lambda e: e.copy(self.ar[:, 0:4, outcols:outcols + 128], pbt[:, 0:512].rearrange("p (c t) -> p c t", c=4)),
             reads=[('pb', bT)], writes=self.akeys(0, 4, outcols, 128))

    def attn_scores(self, S, t, gI, kT_ap, kkeys, E_ap, ekey, qcols=128, mrows=128):
        b = self.bank()
        pb = self.pb[b]
        lo, hi = gI * 64, (gI + 1) * 64
        S.op('pe', lambda e: e.matmul(pb[0:mrows, 0:512], lhsT=kT_ap[lo:hi, :],
                                      rhs=self.ar[lo:hi, 0:4, t * 128:t * 128 + 128], start=True, stop=True),
             reads=kkeys + self.akeys(0, 4, t * 128, 128), writes=[('pb', b)])
        tb = self.tbs()
        S.op('act', lambda e: e.activation(self.tb[0:mrows, tb, :], pb[0:mrows, 0:512], AF.Exp, scale=0.125),
             reads=[('pb', b)], writes=[('tb', tb)])
        S.op('dve', lambda e: e.tensor_tensor(self.tb[0:mrows, tb, :], self.tb[0:mrows, tb, :], E_ap, ALU.mult),
             reads=[('tb', tb), ekey], writes=[('tb', tb)])
        return tb

    def attention_prompt(self, S, grp, l):
        NTL = grp['NTL']
        st = {}

        def scores(t):
            first = grp['first'] and t == 0
            chunks = [1] if first else [0, 1]
            pts = {}
            for gI in range(2):
                for c in chunks:
                    kT_ap = self.kT[:, (t + c) * 128:(t + c + 1) * 128]
                    E_ap = self.E[:, c, gI * 512:(gI + 1) * 512]
                    pts[(gI, c)] = self.attn_scores(S, t, gI, kT_ap, [('kT', t + c)], E_ap, ('E', c))
            st[t] = (chunks, pts)

        def pv(t):
            chunks, pts = st[t]

            def pv_fn(e, pA, pB):
                r = None
                for h in range(8):
                    gI, hh = h // 4, h % 4
                    pbk = pA if gI == 0 else pB
                    for ci, c in enumerate(chunks):
                        r = e.matmul(pbk[:, hh * 65:(hh + 1) * 65], lhsT=self.tb[:, pts[(gI, c)], hh * 128:(hh + 1) * 128],
                                     rhs=self.vaug[:, t + c, gI, :], start=(ci == 0), stop=(ci == len(chunks) - 1))
                return r
            reads = [('tb', v) for v in pts.values()] + [('vaug', t + c) for c in chunks] + ['vaug_ones']
            st[t] = self.attn_pv_norm(S, l, t, pv_fn, reads, t * 128)

        for t in range(NTL + 2):
            if t < NTL:
                scores(t)
                yield None
            if 0 <= t - 1 < NTL:
                pv(t - 1)
                yield None
            if 0 <= t - 2 < NTL:
                at, tb = st[t - 2]
                self.attn_tr(S, at, tb, (t - 2) * 128)
                yield None

    def attention_sample_prep(self, S, grp, l):
        ckst = self.x_tok[:, 1:3, :].rearrange("p a (s c) -> p (a s) c", c=128)
        cvst = self.x_tok[:, 3:5, :].rearrange("p a (s c) -> p (a s) c", c=128)
        for (dst_, src_, ch_, keys_) in ((ckst, self.ck, 'ckin', ['ckst', ('xtok', 1), ('xtok', 2)]),
                                         (cvst, self.cv, 'cvin', ['cvst', ('xtok', 3), ('xtok', 4)])):
            tok = None
            for q4 in range(4):
                tok = S.dma('sp', dst_[:, q4 * 4:(q4 + 1) * 4, :], src_[l][q4 * 4:(q4 + 1) * 4].rearrange("q s c -> s q c"),
                            chan=ch_, writes=keys_)
            self._patch(S, keys_, tok)
        S.op('dve', lambda e: e.tensor_copy(self.vaugc[:, :, :, 0:64], cvst.rearrange("p q (k d) -> p q k d", k=2)),
             reads=['cvst'], writes=['vaugc'])
        S.dma('sp', self.stg1[0:32, :], self.smix[l], chan='smixin', writes=[('stg', 1)])

    def attention_sample_prep_tr(self, S, grp, l):
        ckst = self.x_tok[:, 1:3, :].rearrange("p a (s c) -> p (a s) c", c=128)
        for q4 in range(4):
            b = self.bank()
            pb = self.pb[b]

            def tr(e, q4=q4, pb=pb):
                r = None
                for j in range(4):
                    r = e.transpose(pb[:, j * 128:(j + 1) * 128], ckst[:, q4 * 4 + j, :], self.identf[:])
                return r
            S.op('pe', tr, reads=['ckst', 'identf'], writes=[('pb', b)])
            S.op('act', lambda e, q4=q4, pb=pb: e.copy(self.kcT[:, q4 * 4:(q4 + 1) * 4, :], pb[:].rearrange("p (j s) -> p j s", j=4)),
                 reads=[('pb', b)], writes=[('kcT', q4)])

    def attention_sample(self, S, grp, l):
        pt0 = {}
        for gI in range(2):
            b = self.bank()
            pb = self.pb[b]
            lo, hi = gI * 64, (gI + 1) * 64

            def mm(e, pb=pb, lo=lo, hi=hi):
                r = None
                for q in range(16):
                    r = e.matmul(pb[:, q * 32:(q + 1) * 32], lhsT=self.kcT[lo:hi, q, :],
                                 rhs=self.ar[lo:hi, 0:4, q * 8:(q + 1) * 8], start=True, stop=True)
                return r
            S.op('pe', mm, reads=[('kcT', i) for i in range(4)] + self.akeys(0, 4, 0, 128), writes=[('pb', b)])
            tb = self.tbs()
            S.op('act', lambda e, tb=tb, pb=pb: e.activation(self.tb[:, tb, :], pb[:, 0:512], AF.Exp, scale=0.125),
                 reads=[('pb', b)], writes=[('tb', tb)])
            S.op('dve', lambda e, tb=tb, gI=gI: e.tensor_tensor(self.tb[:, tb, :], self.tb[:, tb, :],
                                                               self.E[:, 2, gI * 512:(gI + 1) * 512], ALU.mult),
                 reads=[('tb', tb), ('E', 2)], writes=[('tb', tb)])
            pt0[gI] = tb
        o0T = self.x_tok[0:65, 7, :]
        bo = [self.bank(), self.bank()]

        def pv0(e):
            r = None
            for q in range(16):
                for gI in range(2):
                    col = ((q % 8) * 2 + gI) * 32
                    r = e.matmul(self.pb[bo[q // 8]][0:65, col:col + 32], lhsT=self.vaugc[:, q, gI, :],
                                 rhs=self.tb[:, pt0[gI], q * 32:(q + 1) * 32], start=True, stop=True)
            return r
        S.op('pe', pv0, reads=[('tb', pt0[0]), ('tb', pt0[1]), 'vaugc', 'vaugc_ones'], writes=[('pb', bo[0]), ('pb', bo[1])])
        o0w = o0T.rearrange("p (g h q t) -> p g h q t", g=2, h=4, q=16)
        for hq in range(2):
            for gI in range(2):
                S.op('act', lambda e, hq=hq, gI=gI: e.copy(
                    o0w[:, gI, :, hq * 8:(hq + 1) * 8, :].rearrange("p h q t -> p q h t"),
                    self.pb[bo[hq]][0:65, 0:512].rearrange("p (q g h t) -> p q g h t", q=8, g=2, h=4)[:, :, gI, :, :]),
                    reads=[('pb', bo[hq])], writes=['o0T', ('xtok', 7)])
        pts = {}
        for gI in range(2):
            pts[gI] = self.attn_scores(S, 0, gI, self.kT[:, 128:256], [('kT', 1)], self.E[:, 3, gI * 512:(gI + 1) * 512], ('E', 3))

        def pv_fn(e, pA, pB):
            r = None
            for h in range(8):
                gI, hh = h // 4, h % 4
                pbk = pA if gI == 0 else pB
                e.matmul(pbk[:, hh * 65:(hh + 1) * 65], lhsT=self.tb[:, pts[gI], hh * 128:(hh + 1) * 128],
                         rhs=self.vaug[:, 1, gI, :], start=True, stop=False)
                r = e.matmul(pbk[:, hh * 65:(hh + 1) * 65], lhsT=o0w[:, gI, hh, :, :].rearrange("p q t -> p (q t)"),
                             rhs=self.identf[0:65, 0:65], start=False, stop=True)
            return r
        reads = [('tb', pts[0]), ('tb', pts[1]), ('vaug', 1), 'vaug_ones', 'o0T', 'identf']
        at, tb = self.attn_pv_norm(S, l, 0, pv_fn, reads, 0)
        self.attn_tr(S, at, tb, 0)
        yield None

    def gmlp_prep(self, S, grp, l):
        kind = grp['kind']
        ti = self.tfs()
        wld = self.tf[:, ti, :].rearrange("p (g s) -> p g s", g=4)
        if kind == 'p':
            S.dma('sp', wld, self.gws[l].rearrange("g t s -> t g s"), chan='gwin', writes=[('tf', ti)])
            tok = None
        else:
            ti0 = self.tfs()
            stage = self.tf[:, ti0, 0:32]
            tok = None
            for q in range(16):
                tok = S.dma('sp', self.tf[q * 8:(q + 1) * 8, ti0, 0:32].rearrange("p (g s) -> p g s", g=4),
                            self.gws[l][:, 0:8, 0:8].rearrange("g t s -> t g s"), chan='gwin', writes=[('tf', ti0)])
            self._patch(S, [('tf', ti0)], tok)
            S.op('dve', lambda e: e.tensor_copy(self.tf[:, ti, :].rearrange("p (g q s) -> p g q s", g=4, q=16),
                                                stage.rearrange("p (g s) -> p g s", g=4).unsqueeze(2).to_broadcast([128, 4, 16, 8])),
                 reads=[('tf', ti0)], writes=[('tf', ti)])
        b = self.bank()
        pb = self.pb[b]

        def tr(e):
            r = None
            for g in range(4):
                r = e.transpose(pb[:, g * 128:(g + 1) * 128], wld[:, g, :], self.identf[:])
            return r
        S.op('pe', tr, reads=[('tf', ti), 'identf'], writes=[('pb', b)])
        mi = 0 if kind == 'p' else 1
        S.op('dve', lambda e: e.tensor_tensor(self.wst[:], pb[:].rearrange("p (g t) -> p g t", g=4),
                                              self.maskt[:, mi, :].unsqueeze(1).to_broadcast([128, 4, 128]), ALU.mult),
             reads=[('pb', b), 'maskt'], writes=['wst'])
        if kind == 'p':
            S.dma('sp', self.gbias[:].rearrange("p g t -> p (g t)"), self.gbs[l].partition_broadcast(128), chan='gbin', writes=['gbias'])
        else:
            ti2 = self.tfs()
            tmp = self.tf[:, ti2, :]
            S.dma('sp', tmp, self.gbs[l].partition_broadcast(128), chan='gbin', writes=[('tf', ti2)])
            S.op('dve', lambda e: e.tensor_copy(self.gbias[:].rearrange("p g (q s) -> p g q s", q=16),
                                                tmp.rearrange("p (g t) -> p g t", g=4)[:, :, 0:8].unsqueeze(2).to_broadcast([128, 4, 16, 8])),
                 reads=[('tf', ti2)], writes=['gbias'])
        S.dma('sp', self.glnbc[:, 0, :], self.gln_g[l].partition_broadcast(128), chan='glg', writes=[('glnbc', l)])
        S.dma('sp', self.glnbc[:, 1, :], self.gln_b[l].partition_broadcast(128), chan='glb', writes=[('glnbc', l)])

    def gmlp_block(self, S, grp, l, t):
        if True:
            u, c0 = 12 + t // 2, (t % 2) * 512
            b = self.bank()
            pb = self.pb[b]

            def mm(e, pb=pb, u=u, c0=c0):
                r = None
                for g in range(4):
                    r = e.matmul(pb[:, g * 128:(g + 1) * 128], lhsT=self.ar[:, u, c0 + g * 128:c0 + (g + 1) * 128],
                                 rhs=self.wst[:, g, :], start=True, stop=True)
                return r
            S.op('pe', mm, reads=self.akeys(u, u + 1, c0, 512) + ['wst'], writes=[('pb', b)])
            ti = self.tfs()
            S.op('dve', lambda e, pb=pb, ti=ti: e.tensor_tensor(self.tf[:, ti, :], pb[:, 0:512], self.gbias[:].rearrange("p g t -> p (g t)"), ALU.add),
                 reads=[('pb', b), 'gbias'], writes=[('tf', ti)])
            gu = self.ar[:, 4:8, t * 128:(t + 1) * 128]
            S.op('dve', lambda e, ti=ti, gu=gu: e.tensor_tensor(gu, gu, self.tf[:, ti, :].rearrange("p (g t) -> p g t", g=4), ALU.mult),
                 reads=[('tf', ti)] + self.akeys(4, 8, t * 128, 128), writes=self.akeys(4, 8, t * 128, 128))

    def conv_carry_in(self, S, grp, l):
        kind, GT, halves = grp['kind'], grp['GT'], grp['halves']
        if kind == 'p':
            if grp['first']:
                S.op('dve', lambda e: e.memset(self.uT[:, :, 0:2], 0.0), writes=[('uTc',)])
            else:
                S.op('dve', lambda e: e.tensor_copy(self.uT[:, :, 0:2], self.ucar[:, l, :, :]), reads=[('ucar', l)], writes=[('uTc',)])
        else:
            b = self.bank()
            pb = self.pb[b]

            def tr(e, pb=pb):
                r = None
                for c in range(4):
                    r = e.transpose(pb[:, c * 32:(c + 1) * 32], self.stg1[0:32, c * 128:(c + 1) * 128], self.identf[0:32, 0:32])
                return r
            S.op('pe', tr, reads=[('stg', 1), 'identf'], writes=[('pb', b)])
            S.op('dve', lambda e, pb=pb: e.tensor_copy(self.uT[:, :, 0:160].rearrange("p c (s j) -> p c s j", j=10)[:, :, :, 0:2],
                                                pb[:, 0:128].rearrange("p (c s r) -> p c s r", c=4, r=2)),
                 reads=[('pb', b)], writes=[('uTc',)])

    def conv_chunk(self, S, grp, l, c):
        kind, GT, halves = grp['kind'], grp['GT'], grp['halves']
        if True:
            for hi, (h0, n) in enumerate(halves):
                ti = self.tfs()
                if kind == 'p':
                    tmp = self.tf[:, ti, 0:n]
                    u2 = self.uT[:, c, 2 + h0:2 + h0 + n]
                    u1 = self.uT[:, c, 1 + h0:1 + h0 + n]
                    u0 = self.uT[:, c, h0:h0 + n]
                    sbv = self.ar[:, 8 + c, h0:h0 + n]
                else:
                    uv = self.uT[:, c, 0:160].rearrange("p (s j) -> p s j", j=10)
                    tmp = self.tf[:, ti, 0:128].rearrange("p (s j) -> p s j", j=8)
                    u2, u1, u0 = uv[:, :, 2:10], uv[:, :, 1:9], uv[:, :, 0:8]
                    sbv = self.ar[:, 8 + c, 0:128].rearrange("p (s j) -> p s j", j=8)
                rk = [('uT', c, hi), ('uTc',)] + ([('uT', c, hi - 1)] if hi > 0 else [])
                S.op('dve', lambda e, tmp=tmp, u2=u2, c=c: e.tensor_scalar(tmp, u2, self.p_mixw(l, 2, c), None, ALU.mult),
                     reads=rk + [('prm', l)], writes=[('tf', ti)])
                S.op('dve', lambda e, tmp=tmp, u1=u1, c=c: e.scalar_tensor_tensor(tmp, u1, self.p_mixw(l, 1, c), tmp, ALU.mult, ALU.add),
                     reads=rk + [('tf', ti)], writes=[('tf', ti)])
                S.op('dve', lambda e, tmp=tmp, u0=u0, c=c: e.scalar_tensor_tensor(tmp, u0, self.p_mixw(l, 0, c), tmp, ALU.mult, ALU.add),
                     reads=rk + [('tf', ti)], writes=[('tf', ti)])
                S.op('dve', lambda e, tmp=tmp, sbv=sbv: e.tensor_tensor(sbv, sbv, tmp, ALU.mult),
                     reads=[('tf', ti)] + self.akeys(8 + c, 9 + c, h0, n), writes=self.akeys(8 + c, 9 + c, h0, n))

    def conv_carry_out(self, S, grp, l):
        kind, GT, halves = grp['kind'], grp['GT'], grp['halves']
        if kind == 'p' and not grp['last']:
            S.op('dve', lambda e: e.tensor_copy(self.ucar[:, l, :, :], self.uT[:, :, GT:GT + 2]),
                 reads=[('uT', c, len(halves) - 1) for c in range(4)], writes=[('ucar', l)])

    def carry_in(self, S, grp, l):
        if grp['kind'] == 'p' and not grp['first']:
            S.op('dve', lambda e: e.tensor_copy(self.kT[:, 0:128], self.kcar[:, l, :]), reads=[('kcar', l)], writes=[('kT', 0)])
            S.op('dve', lambda e: e.tensor_copy(self.vaug[:, 0, :, 0:64], self.vcar[:, l, :, 0:64]), reads=[('vcar', l)], writes=[('vaug', 0)])

    def carry_out(self, S, grp, l):
        if grp['kind'] == 'p' and not grp['last']:
            n = grp['NTL']
            S.op('dve', lambda e: e.tensor_copy(self.kcar[:, l, :], self.kT[:, n * 128:(n + 1) * 128]), reads=[('kT', n)], writes=[('kcar', l)])
            S.op('dve', lambda e: e.tensor_copy(self.vcar[:, l, :, 0:64], self.vaug[:, n, :, 0:64]), reads=[('vaug', n)], writes=[('vcar', l)])

    def merge(self, S, grp, l):
        GT, halves = grp['GT'], grp['halves']
        allx = [('xT', t) for t in range(grp['NTL'])]
        xsrc = lambda kc, h0, n: self.xT[:, kc, h0:h0 + n]
        for dp in range(4):
            for i in range(3):
                slot = self.w_get(S)
                Wg = self.wv(slot, 0, 8, 256)
                Wp = self.wv(slot, 2048, 4, 256)
                for d2 in range(2):
                    dc = dp * 2 + d2
                    gts = {}

                    def ev_g(hi, h0, n, b, gts=gts, dc=dc, i=i):
                        tb = self.tbs()
                        gts[hi] = tb
                        S.op('act', lambda e: e.activation(self.tb[:, tb, 0:n], self.pb[b][:, 0:n], AF.Sigmoid,
                                                           bias=self.p_bg(l, i * 8 + dc), scale=1.0),
                             reads=[('pb', b), ('prm', l)], writes=[('tb', tb)])

                    def ev_p(hi, h0, n, b, gts=gts, dc=dc, d2=d2, i=i):
                        tb = gts[hi]
                        mk = ('macc', d2 * 2 + hi)
                        mac = self.macc[:, d2 * 2 + hi, 0:n]
                        g_ap = self.tb[:, tb, 0:n]
                        p_ap = self.pb[b][:, 0:n]
                        if i == 0:
                            S.op('dve', lambda e: e.tensor_tensor(mac, p_ap, g_ap, ALU.mult),
                                 reads=[('pb', b), ('tb', tb)], writes=[mk])
                        else:
                            ti = self.tfs()
                            tmp = self.tf[:, ti, 0:n]
                            S.op('dve', lambda e: e.tensor_tensor(tmp, p_ap, g_ap, ALU.mult),
                                 reads=[('pb', b), ('tb', tb)], writes=[('tf', ti)])
                            if i == 1:
                                S.op('dve', lambda e: e.tensor_tensor(mac, mac, tmp, ALU.add), reads=[mk, ('tf', ti)], writes=[mk])
                            else:
                                S.op('dve', lambda e: e.tensor_tensor(self.ar[:, 16 + dc, h0:h0 + n], mac, tmp, ALU.add),
                                     reads=[mk, ('tf', ti)], writes=self.akeys(16 + dc, 17 + dc, h0, n))
                    self.fm(S, slot, Wg, d2 * 128, xsrc, 8, halves, allx, ev_g)
                    bsrc = lambda kc, h0, n, i=i: self.ar[:, i * 4 + kc, h0:h0 + n]
                    self.fm(S, slot, Wp, d2 * 128, bsrc, 4, halves, self.akeys(i * 4, i * 4 + 4, 0, GT), ev_p)
                self.w_done(S, slot)

    def ln_bc_load(self, S, g_ap, b_ap, tag):
        S.dma('sp', self.lnbc[:, 0, :], g_ap.partition_broadcast(128), chan='lng', writes=['lnbc'])
        S.dma('sp', self.lnbc[:, 1, :], b_ap.partition_broadcast(128), chan='lnb', writes=['lnbc'])

    def ln_stats(self, S, t):
        xr = self.x_tok[:, t, :]
        st, k = self.layer_norm(S, xr, None, 1024, None, None, [('xtok', t)], None, None, recip=False)

        def affine():
            S.op('dve', lambda e: e.reciprocal(st[:, 15:16], st[:, 14:15]), reads=[k], writes=[k])
            S.op('dve', lambda e: e.scalar_tensor_tensor(xr, xr, st[:, 12:13], self.lnbc[:, 0, :], ALU.subtract, ALU.mult),
                 reads=[('xtok', t), 'lnbc', k], writes=[('xtok', t)])
            S.op('dve', lambda e: e.scalar_tensor_tensor(xr, xr, st[:, 15:16], self.lnbc[:, 1, :], ALU.mult, ALU.add),
                 reads=[('xtok', t), 'lnbc', k], writes=[('xtok', t)])
        return affine

    def wo_ln1(self, S, grp, l):
        NTL = grp['NTL']
        self.ln_bc_load(S, self.ln1_g[l], self.ln1_b[l], 'ln1')
        prev_aff = None
        for ch in range(2):
            slot = self.w_get(S)
            W = self.wv(slot, 0, 8, 512)
            for t in range(NTL):
                b = self.bank()
                pb = self.pb[b]

                def mm(e, t=t, pb=pb, W=W):
                    r = None
                    for kc in range(8):
                        r = e.matmul(pb[:, 0:512], lhsT=self.ar[:, 16 + kc, t * 128:(t + 1) * 128], rhs=W[:, kc, 0:512],
                                     start=(kc == 0), stop=(kc == 7))
                    return r
                S.op('pe', mm, reads=[('wr', slot)] + self.akeys(16, 24, t * 128, 128), writes=[('pb', b)])
                xs = self.x_tok[:, t, ch * 512:(ch + 1) * 512]
                S.op('dve', lambda e, xs=xs, pb=pb: e.scalar_tensor_tensor(xs, xs, ALPHA, pb[:, 0:512], ALU.mult, ALU.add),
                     reads=[('pb', b), ('xtok', t)], writes=[('xtok', t)])
                if ch == 1:
                    aff = self.ln_stats(S, t)
                    if prev_aff:
                        prev_aff()
                    prev_aff = aff
            self.w_done(S, slot)
        if prev_aff:
            prev_aff()
        for t in range(NTL):
            self.build_xT(S, t)

    def wd_load(self, S, l, hf):
        first = self._cur_gi == 0
        multi = self._ngroups > 1
        keys = self.akeys(0, 11, 0, 1024)
        skey = ('scrwd', l, hf)
        if first or not multi:
            S.dma('pool', self.ar[:, 0:11, :], self.w_down[l, hf * 1408:(hf + 1) * 1408, :].rearrange("(j p) c -> p j c", p=128),
                  chan='wd', writes=keys)
            if multi:
                for (u0, u1, ch) in ((0, 6, 'wdb0'), (6, 11, 'wdb1')):
                    S.dma('sp', self.scr_wd[l, hf, :, u0 * 1024:u1 * 1024], self.ar[:, u0:u1, :].rearrange("p u c -> p (u c)"),
                          chan=ch, reads=keys, writes=[skey])
        else:
            def fn(e):
                r = []
                for (u0, u1) in ((0, 6), (6, 11)):
                    r.append(e.dma_start(out=self.ar[:, u0:u1, :].rearrange("p u c -> p (u c)"), in_=self.scr_wd[l, hf, :, u0 * 1024:u1 * 1024]))
                return r
            S.dma('pool', None, None, chan='wd', reads=[skey], writes=keys, n=2, fn=fn)

    def ffn(self, S, grp, l, last_layer):
        kind, GT, halves, NTL = grp['kind'], grp['GT'], grp['halves'], grp['NTL']
        allx = [('xT', t) for t in range(NTL)]
        xsrc = lambda kc, h0, n: self.xT[:, kc, h0:h0 + n]
        state_out = (kind == 'p' and grp['last']) or kind == 's'
        stT = getattr(self, '_stT', None)
        self.ln_bc_load(S, self.ln2_g[l], self.ln2_b[l], 'ln2')
        return self.ffn_body(S, grp, l, last_layer, stT, state_out)

    def sample_ffn_prep(self, S, grp, l):
        kind = grp['kind']
        if kind == 's':
            stT = self.x_tok[:, 5:7, :].rearrange("p a b -> p (a b)")[:, 0:44 * 32].rearrange("p (c s r) -> p c s r", c=44, r=2)
            tis = [self.tfs(), self.tfs(), self.tfs(), self.tfs()]
            for q in range(11):
                ti = tis[q % 4]
                S.dma('sp', self.tf[0:32, ti, :], self.sffn[l][:, q * 512:(q + 1) * 512], chan='sffnin%d' % (q % 4), writes=[('tf', ti)])
                b = self.bank()
                pb = self.pb[b]

                def tr(e, ti=ti, pb=pb):
                    r = None
                    for c in range(4):
                        r = e.transpose(pb[:, c * 32:(c + 1) * 32], self.tf[0:32, ti, c * 128:(c + 1) * 128], self.identf[0:32, 0:32])
                    return r
                S.op('pe', tr, reads=[('tf', ti), 'identf'], writes=[('pb', b)])
                S.op('dve', lambda e, q=q, pb=pb: e.tensor_copy(stT[:, q * 4:(q + 1) * 4, :, :],
                                                               pb[:, 0:128].rearrange("p (c s r) -> p c s r", c=4, r=2)),
                     reads=[('pb', b)], writes=['stT', ('xtok', 5), ('xtok', 6)])
            self._stT = stT

    def ffn_body(self, S, grp, l, last_layer, stT, state_out):
        kind, GT, halves, NTL = grp['kind'], grp['GT'], grp['halves'], grp['NTL']
        allx = [('xT', t) for t in range(NTL)]
        xsrc = lambda kc, h0, n: self.xT[:, kc, h0:h0 + n]
        pend = []
        FDEPTH = 2
        for hf in range(2):
            for (j0, nj) in self.up_pieces(hf):
                slot = self.w_get(S)
                n_ = nj * 128
                Wa = self.wv(slot, 0, 8, n_)
                Wg = self.wv(slot, 8 * n_, 8, n_)
                for jj in range(nj):
                    j = j0 + jj
                    jl = j - hf * 11
                    hold = {}
                    for which, W_ in (('a', Wa), ('g', Wg)):
                        col = j if which == 'a' else 22 + j

                        def ev(hi, h0, n, b, which=which, col=col, hold=hold, jl=jl):
                            pbv = self.pb[b]
                            tfa, tk = self.ffs()
                            w0, w1, w2, bb = self.p_fcw(l, 0, col), self.p_fcw(l, 1, col), self.p_fcw(l, 2, col), self.p_fcb(l, col)
                            if kind == 'p':
                                c0 = tfa[:, 0:n]
                                P = pbv[:, 0:n]
                                hidx = grp['g'] * 2 + hi
                                pw = hidx % 2
                                cyw = self.upc[:, l, col, pw, :]
                                cyr = self.upc[:, l, col, 1 - pw, :]
                                S.op('act', lambda e: e.activation(c0, P, AF.Identity, bias=bb, scale=w2),
                                     reads=[('pb', b), ('prm', l)], writes=[tk])
                                if hidx < 2 * self.ngrp - 1:
                                    def carry(e):
                                        e.activation(cyw[:, 0:2], P[:, n - 2:n], AF.Identity, bias=0.0, scale=w0)
                                        return e.activation(cyw[:, 2:3], P[:, n - 1:n], AF.Identity, bias=0.0, scale=w1)
                                    S.op('act', carry, reads=[('pb', b), ('prm', l)], writes=[('upc', l, col, pw)])
                                S.op('dve', lambda e: e.scalar_tensor_tensor(c0[:, 1:n], P[:, 0:n - 1], w1, c0[:, 1:n], ALU.mult, ALU.add),
                                     reads=[('pb', b), tk], writes=[tk])
                                S.op('dve', lambda e: e.scalar_tensor_tensor(c0[:, 2:n], P[:, 0:n - 2], w0, c0[:, 2:n], ALU.mult, ALU.add),
                                     reads=[('pb', b), tk], writes=[tk])
                                if hidx > 0:
                                    ck_ = ('upc', l, col, 1 - pw)
                                    S.op('pool', lambda e: e.tensor_tensor(c0[:, 0:2], c0[:, 0:2], cyr[:, 0:2], ALU.add),
                                         reads=[ck_, tk], writes=[tk])
                                    S.op('pool', lambda e: e.tensor_tensor(c0[:, 0:1], c0[:, 0:1], cyr[:, 2:3], ALU.add),
                                         reads=[ck_, tk], writes=[tk])
                            else:
                                c0 = tfa[:, 0:n].rearrange("p (s j) -> p s j", j=8)
                                P = pbv[:, 0:n].rearrange("p (s j) -> p s j", j=8)
                                cy = stT[:, col, :, :]
                                S.op('act', lambda e: e.activation(tfa[:, 0:n], pbv[:, 0:n], AF.Identity, bias=bb, scale=w2),
                                     reads=[('pb', b), ('prm', l)], writes=[tk])
                                S.op('dve', lambda e: e.scalar_tensor_tensor(c0[:, :, 1:8], P[:, :, 0:7], w1, c0[:, :, 1:8], ALU.mult, ALU.add),
                                     reads=[('pb', b), tk], writes=[tk])
                                S.op('dve', lambda e: e.scalar_tensor_tensor(c0[:, :, 2:8], P[:, :, 0:6], w0, c0[:, :, 2:8], ALU.mult, ALU.add),
                                     reads=[('pb', b), tk], writes=[tk])
                                S.op('dve', lambda e: e.scalar_tensor_tensor(c0[:, :, 0:1], cy[:, :, 1:2], w1, c0[:, :, 0:1], ALU.mult, ALU.add),
                                     reads=['stT', tk], writes=[tk])
                                S.op('dve', lambda e: e.scalar_tensor_tensor(c0[:, :, 0:2], cy[:, :, 0:2], w0, c0[:, :, 0:2], ALU.mult, ALU.add),
                                     reads=['stT', tk], writes=[tk])
                            full = tfa[:, 0:n]
                            if which == 'a':
                                hold[hi] = (tfa, tk)
                            else:
                                tfa_a, tk_a = hold[hi]

                                def fin():
                                    S.op('act', lambda e: e.activation(full, full, AF.Silu), reads=[tk], writes=[tk])
                                    S.op('pool', lambda e: e.tensor_tensor(self.ar[:, 11 + jl, h0:h0 + n], tfa_a[:, 0:n], full, ALU.mult),
                                         reads=[tk, tk_a], writes=self.akeys(11 + jl, 12 + jl, h0, n))
                                pend.append(fin)
                                while len(pend) > FDEPTH:
                                    pend.pop(0)()
                        self.fm(S, slot, W_, jj * 128, xsrc, 8, halves, allx, ev)
                if state_out:
                    t = NTL - 1
                    for which, W_ in (('a', Wa), ('g', Wg)):
                        b = self.bank()
                        pb = self.pb[b]

                        def mm(e, W_=W_, pb=pb, n_=n_, t=t):
                            r = None
                            for kc in range(8):
                                r = e.matmul(pb[:, 0:n_], lhsT=self.xT[:, kc, t * 128:(t + 1) * 128], rhs=W_[:, kc, 0:n_],
                                             start=(kc == 0), stop=(kc == 7))
                            return r
                        S.op('pe', mm, reads=[('wr', slot), ('xT', t)], writes=[('pb', b)])
                        si = 0 if which == 'a' else 1
                        S.op('act', lambda e, pb=pb, si=si, n_=n_: e.copy((self.stg0 if si == 0 else self.stg1)[:, 0:n_], pb[:, 0:n_]), reads=[('pb', b)], writes=[('stg', si)])
                        cc = (j0 if which == 'a' else 22 + j0) * 128
                        if kind == 'p':
                            S.dma('sp', self.fcp[l, grp['b']][:, cc:cc + n_], (self.stg0 if si == 0 else self.stg1)[126:128, 0:n_], chan='st%d' % si,
                                  reads=[('stg', si)], out_final=True)
                        else:
                            for r in range(2):
                                dst = self.fcs[l].rearrange("(s r) c -> s r c", r=2)[:, r, cc:cc + n_]
                                S.dma('sp', dst, (self.stg0 if si == 0 else self.stg1)[6 + r:128:8, 0:n_], chan='st%d%d' % (si, r), reads=[('stg', si)], out_final=True)
                self.w_done(S, slot)
            while pend:
                pend.pop(0)()
            prev_fin = None
            JSPLIT = 9
            NEARLY = min(NTL, 4)
            tbanks = {}

            def dmm(t, j0_, j1_):
                banks = tbanks[t]

                def mm(e):
                    r = None
                    for ch in range(2):
                        for jl in range(j0_, j1_):
                            r = e.matmul(self.pb[banks[ch]][:, 0:512], lhsT=self.ar[:, 11 + jl, t * 128:(t + 1) * 128],
                                         rhs=self.ar[:, jl, ch * 512:(ch + 1) * 512], start=(jl == 0), stop=(jl == 10))
                    return r
                S.op('pe', mm, reads=self.akeys(11 + j0_, 11 + j1_, t * 128, 128) + self.akeys(j0_, j1_, 0, 1024),
                     writes=[('pb', b) for b in banks])
            for t in range(NEARLY):
                tbanks[t] = [self.bank(), self.bank()]
                dmm(t, 0, JSPLIT)
            for t in range(NTL):
                if t < NEARLY:
                    dmm(t, JSPLIT, 11)
                else:
                    tbanks[t] = [self.bank(), self.bank()]
                    dmm(t, 0, 11)
                banks = tbanks[t]
                for ch in range(2):
                    xs = self.x_tok[:, t, ch * 512:(ch + 1) * 512]
                    pbv = self.pb[banks[ch]][:, 0:512]
                    if hf == 0:
                        S.op('dve', lambda e, xs=xs, pbv=pbv: e.scalar_tensor_tensor(xs, xs, ALPHA, pbv, ALU.mult, ALU.add),
                             reads=[('pb', banks[ch]), ('xtok', t)], writes=[('xtok', t)])
                    else:
                        S.op('dve', lambda e, xs=xs, pbv=pbv: e.tensor_tensor(xs, xs, pbv, ALU.add),
                             reads=[('pb', banks[ch]), ('xtok', t)], writes=[('xtok', t)])
                if hf == 1:
                    aff = self.ln_stats(S, t)

                    def fin(t=t, aff=aff):
                        aff()
                        if last_layer:
                            if kind == 'p':
                                r0 = grp['g'] * 1024 + t * 128
                                S.dma('sp', self.y_p[grp['b'], r0:r0 + 128, :], self.x_tok[:, t, :], chan='yo%d' % t,
                                      reads=[('xtok', t)], out_final=True)
                                ng = self._groups[self._cur_gi + 1] if self._cur_gi + 1 < len(self._groups) else None
                                if ng is not None and ng['kind'] == 'p':
                                    r0n = ng['g'] * 1024 + t * 128
                                    S.dma('sp', self.x_tok[:, t, :], self.xp[ng['b'], r0n:r0n + 128, :], chan='xpre%d' % t, writes=[('xtok', t)])
                                    ng['preloaded'] = True
                                elif ng is not None and ng['kind'] == 's' and t == 0:
                                    S.dma('sp', self.x_tok[:, 0, :], self.xs, chan='xpre0', writes=[('xtok', 0)])
                                    ng['preloaded'] = True
                            else:
                                S.dma('sp', self.y_s, self.x_tok[:, 0, :], chan='yo0', reads=[('xtok', 0)], out_final=True)
                    if prev_fin:
                        prev_fin()
                    prev_fin = fin
            if hf == 1 and prev_fin:
                prev_fin()
            if hf == 0:
                self.wd_load(S, l, 1)
            elif not last_layer:
                for t in range(NTL):
                    self.build_xT(S, t)

    def proj_attn(self, S, grp, l):
        if grp['kind'] == 's':
            self.attention_sample_prep(S, grp, l)
        gp = self.proj(S, grp, l)
        for v in gp:
            if v == 'KV_DONE':
                break
        if grp['kind'] == 's':
            self.attention_sample_prep_tr(S, grp, l)
        ga = self.attention_prompt(S, grp, l) if grp['kind'] == 'p' else self.attention_sample(S, grp, l)
        alive_a = alive_p = True
        while alive_a or alive_p:
            if alive_p:
                try:
                    next(gp)
                except StopIteration:
                    alive_p = False
            if alive_a:
                try:
                    next(ga)
                except StopIteration:
                    alive_a = False

    def build(self):
        nc = self.nc
        self.declare()
        groups = []
        for b in range(self.nseq):
            for g in range(self.ngrp):
                groups.append(dict(kind='p', b=b, g=g, GT=1024, NTL=8, halves=[(0, 512), (512, 512)],
                                   first=(g == 0), last=(g == self.ngrp - 1)))
        if self.with_sample:
            groups.append(dict(kind='s', b=0, g=0, GT=128, NTL=1, halves=[(0, 128)], first=True, last=True))
        with ExitStack() as es:
            S = Sched(nc, es)
            self.alloc(S)
            self.init_consts(S)
            self.w_init(S, self.make_pieces(groups))
            import os
            stage = int(os.environ.get("KSTAGE", "999"))
            cnt = [0]

            def go():
                cnt[0] += 1
                return cnt[0] <= stage
            try:
                self._ngroups = len(groups)
                self._groups = groups
                for grp in groups:
                    self._cur_gi = groups.index(grp)
                    if not go(): raise StopIteration
                    self.load_x(S, grp)
                    for t in range(grp['NTL']):
                        self.build_xT(S, t)
                    for l in range(L):
                        gi = groups.index(grp)
                        nxt = (grp, l + 1) if l + 1 < L else ((groups[gi + 1], 0) if gi + 1 < len(groups) else None)
                        steps = [
                            lambda: ((self.gmlp_prep(S, grp, l) if (gi == 0 and l == 0) else None), self.carry_in(S, grp, l)),
                            lambda: self.proj_attn(S, grp, l),
                            lambda: self.carry_out(S, grp, l),
                            lambda: (self.sample_ffn_prep(S, grp, l), self.merge(S, grp, l), self.wd_load(S, l, 0)),
                            lambda: self.wo_ln1(S, grp, l),
                            lambda: ((self.gmlp_prep(S, nxt[0], nxt[1]) if nxt else None), self.ffn(S, grp, l, l == L - 1)),
                        ]
                        for st_ in steps:
                            if not go(): raise StopIteration
                            st_()
            except StopIteration:
                pass
            print("stages emitted:", cnt[0], "tasks:", S.ntask)
            S.finish()
            self.ntask = S.ntask
        return nc


def _tables():
    slopes = 2.0 ** (-(np.arange(8) + 1.0))
    s = np.arange(128)[:, None, None]
    t = np.arange(128)[None, None, :]
    sl = slopes[None, :, None]
    NEG = -30000.0
    B0 = np.where(s >= t, -sl * (t + 128 - s), NEG) + 0.0 * sl
    B1 = np.where(s <= t, -sl * (t - s), NEG) + 0.0 * sl
    B0 = np.broadcast_to(B0, (128, 8, 128)).astype(np.float32)
    B1 = np.broadcast_to(B1, (128, 8, 128)).astype(np.float32)
    B0s = np.zeros((128, 2, 16, 4, 8), np.float32)
    for gI in range(2):
        for hh in range(4):
            B0s[:, gI, :, hh, :] = B0[:, gI * 4 + hh, None, 0:8]
    B1bd = np.full((16, 8, 8, 16, 8), NEG, np.float32)
    m_bd = np.zeros((16, 8, 16, 8), np.float32)
    for q in range(16):
        B1bd[q, :, :, q, :] = B1[0:8, :, 0:8]
        m_bd[q, :, q, :] = (np.arange(8)[:, None] <= np.arange(8)[None, :])
    tabs = np.stack([B0.reshape(128, 1024), B1.reshape(128, 1024), B0s.reshape(128, 1024), B1bd.reshape(128, 1024)])
    m_tril = (np.arange(128)[:, None] <= np.arange(128)[None, :]).astype(np.float32)
    masks = np.stack([m_tril, m_bd.reshape(128, 128)])
    return np.ascontiguousarray(tabs, dtype=np.float32), np.ascontiguousarray(masks, dtype=np.float32)


_QPERM = np.concatenate([np.concatenate([np.arange(c * 64, c * 64 + 64), np.arange((4 + c) * 64, (4 + c) * 64 + 64)]) for c in range(4)])

_NC_CACHE = {}


def _get_nc(nseq, seqlen, with_sample=True):
    key = (nseq, seqlen, with_sample)
    if key not in _NC_CACHE:
        _NC_CACHE[key] = Builder(nseq, seqlen, with_sample).build()
    return _NC_CACHE[key]


def _shared_maps(inp):
    f = lambda a: np.ascontiguousarray(np.asarray(a), dtype=np.float32)
    w_in = f(inp["w_in"])
    w_in_p = w_in.copy()
    w_in_p[:, :, 0:512] = w_in[:, :, _QPERM]
    tabs, masks = _tables()
    return {
        "w_in": w_in_p, "w_gate": f(inp["w_gate"]), "b_gate": f(inp["b_gate"]).reshape(L, 24, 128),
        "gln_g": f(inp["gmlp_ln_g"]), "gln_b": f(inp["gmlp_ln_b"]), "gws": f(inp["gmlp_ws"]),
        "gbs": f(inp["gmlp_bs"]).reshape(L, 512), "mixw": f(inp["mixconv_w"]).reshape(L, 12, 128),
        "sinks": f(inp["attn_sinks"]), "p_attn": f(inp["p_attn"]), "p_gmlp": f(inp["p_gmlp"]), "p_conv": f(inp["p_conv"]),
        "w_o": f(inp["w_o"]), "ln1_g": f(inp["ln1_g"]), "ln1_b": f(inp["ln1_b"]), "w_up": f(inp["w_up"]),
        "fcw": f(inp["ffn_conv_w"]).reshape(L, 132, 128), "fcb": f(inp["ffn_conv_b"]).reshape(L, 44, 128),
        "w_down": f(inp["w_down"]), "ln2_g": f(inp["ln2_g"]), "ln2_b": f(inp["ln2_b"]),
        "tabs": tabs, "masks": masks,
    }


def run_cores(inp, ncores, nseq, seqlen):
    f = lambda a: np.ascontiguousarray(np.asarray(a), dtype=np.float32)
    shared = _shared_maps(inp)
    xp, xs = f(inp["x_prompt"]), f(inp["x_sample"])
    ck, cv = f(inp["cache_k_win"]), f(inp["cache_v_win"])
    sm, sf = f(inp["state_mixconv"]), f(inp["state_ffnconv"])
    in_maps = []
    for i in range(ncores):
        m = dict(shared)
        m["xp"] = np.ascontiguousarray(xp[i * nseq:(i + 1) * nseq])
        m["xs"] = np.ascontiguousarray(xs[i * 16:(i + 1) * 16].reshape(128, D))
        m["ck"] = np.ascontiguousarray(ck[:, i * 16:(i + 1) * 16].reshape(L, 16, 128, 128))
        m["cv"] = np.ascontiguousarray(cv[:, i * 16:(i + 1) * 16].reshape(L, 16, 128, 128))
        m["smix"] = np.ascontiguousarray(sm[:, i * 16:(i + 1) * 16].reshape(L, 32, 512))
        m["sffn"] = np.ascontiguousarray(sf[:, i * 16:(i + 1) * 16].reshape(L, 32, 2 * DFF))
        in_maps.append(m)
    nc = _get_nc(nseq, seqlen)
    res = run_bass_kernel_spmd(nc, in_maps, core_ids=list(range(ncores)))
    R = res.results
    cat = lambda name, ax: np.concatenate([np.asarray(r[name]) for r in R], axis=ax)
    nb = ncores * nseq
    ns = ncores * 16
    outs = (
        cat("y_p", 0).reshape(nb, seqlen, D),
        cat("y_s", 0).reshape(ns, 8, D),
        cat("kwp", 1).reshape(L, nb, 128, 2, 64),
        cat("vwp", 1).reshape(L, nb, 128, 2, 64),
        cat("mcp", 1).reshape(L, nb, 2, 512),
        cat("fcp", 1).reshape(L, nb, 2, 2 * DFF),
        cat("kws", 1).reshape(L, ns, 128, 2, 64),
        cat("vws", 1).reshape(L, ns, 128, 2, 64),
        cat("mcs", 1).reshape(L, ns, 2, 512),
        cat("fcs", 1).reshape(L, ns, 2, 2 * DFF),
        cat("gvs", 1).reshape(L, ns, 8, 512),
    )
    return tuple(np.ascontiguousarray(o, dtype=np.float32) for o in outs)


def kernel(**inputs):
    return run_cores(inputs, 8, 2, 2048)
```

```python
import numpy as np
from contextlib import ExitStack
import concourse.bass as bass
import concourse.mybir as mybir
from concourse.bass_utils import run_bass_kernel_spmd

F32 = mybir.dt.float32
BF16 = mybir.dt.bfloat16
AF = mybir.ActivationFunctionType
ALU = mybir.AluOpType

ENGS = ['pe', 'act', 'dve', 'pool', 'sp']


class Sched:
    def __init__(self, nc, es):
        self.nc = nc
        self.es = es
        self.q = {e: [] for e in ENGS}
        self.esem = {}
        for e in ENGS:
            if e != 'sp':
                self.esem[e] = es.enter_context(nc.semaphore("prog_" + e))
        self.ecnt = {e: 0 for e in ENGS}
        self.seen = {e: {} for e in ENGS}
        self.lastw = {}
        self.readers = {}
        self.chans = {}
        self.out_chans = set()
        self.semname = {}
        self.ntask = 0

    def sb(self, name, shape, dtype):
        return self.es.enter_context(self.nc.sbuf_tensor(name, shape, dtype))

    def ps(self, name, shape, dtype):
        return self.es.enter_context(self.nc.psum_tensor(name, shape, dtype))

    def _deps(self, eng, reads, writes):
        deps = []
        for k in reads:
            t = self.lastw.get(k)
            if t is not None:
                deps.append(t)
            if isinstance(k, tuple) and k[0] == 'pb':
                r = self.readers.get(k)
                if r:
                    deps.extend(v for s_, v in r.items() if s_ != eng)
        for k in writes:
            t = self.lastw.get(k)
            if t is not None:
                deps.append(t)
            r = self.readers.get(k)
            if r:
                deps.extend(r.values())
        waits = {}
        seen = self.seen[eng]
        for (sid, sem, v) in deps:
            if eng == 'pe' and sid == 'pe':
                continue
            if seen.get(sid, 0) >= v:
                continue
            if sid not in waits or waits[sid][1] < v:
                waits[sid] = (sem, v)
        for sid, (sem, v) in waits.items():
            seen[sid] = v
        return list(waits.values())

    def _commit(self, tok, reads, writes):
        sid = tok[0]
        for k in reads:
            r = self.readers.setdefault(k, {})
            r[sid] = tok
        for k in writes:
            self.lastw[k] = tok
            self.readers[k] = {}

    def op(self, eng, fn, reads=(), writes=()):
        waits = self._deps(eng, reads, writes)
        self.ecnt[eng] += 1
        tok = (eng, self.esem[eng], self.ecnt[eng])
        self.q[eng].append((waits, fn, tok, 1))
        self._commit(tok, reads, writes)
        self.ntask += 1
        return tok

    def dma(self, eng, out, in_, chan, reads=(), writes=(), out_final=False, n=1, fn=None, **kw):
        if chan not in self.chans:
            self.chans[chan] = [self.es.enter_context(self.nc.semaphore("c_" + chan)), 0]
        c = self.chans[chan]
        waits = self._deps(eng, reads, writes)
        c[1] += 16 * n
        tok = ('c_' + chan, c[0], c[1])
        if fn is None:
            def fn(e, out=out, in_=in_, kw=kw):
                return [e.dma_start(out=out, in_=in_, **kw)]
        self.q[eng].append((waits, fn, tok, 16))
        self._commit(tok, reads, writes)
        if out_final:
            self.out_chans.add(chan)
        self.ntask += 1
        return tok

    def finish(self):
        nc = self.nc
        engmap = {'pe': 'tensor', 'act': 'scalar', 'dve': 'vector', 'pool': 'gpsimd', 'sp': 'sync'}
        finals = [(self.chans[c][0], self.chans[c][1]) for c in sorted(self.chans)]

        def run(e, name):
            for waits, fn, tok, inc in self.q[name]:
                for (sem, v) in waits:
                    e.wait_ge(sem, v)
                r = fn(e)
                if inc == 16:
                    for ins in r:
                        ins.then_inc(tok[1], 16)
                else:
                    r.then_inc(tok[1], 1)
            if name == 'sp':
                for (sem, v) in finals:
                    e.wait_ge(sem, v)

        with nc.Block() as block:
            for name in ENGS:
                getattr(block, engmap[name])(lambda e, name=name: run(e, name))


D = 1024
KC = 8
DFF = 2816
NPAIR = 22
NUP = 44
L = 2
ALPHA = float((2.0 * L) ** 0.25)
EPS = 1e-5
C_Q, C_K, C_V, C_GU, C_GV, C_SB, C_SC, C_SH = 0, 512, 640, 768, 1280, 1792, 2304, 2816
NSLOT = 3
SLOT_EL = 4096


class Builder:
    def __init__(self, nseq=2, seqlen=2048, with_sample=True):
        self.nseq = nseq
        self.seqlen = seqlen
        self.ngrp = seqlen // 1024
        self.with_sample = with_sample
        self.nc = bass.Bass("TRN2", target_bir_lowering=False)
        self.nb = 0
        self.ntf = 0
        self.ntb = 0
        self.nln = 0

    def declare(self):
        nc = self.nc

        def din(name, shape):
            return nc.dram_tensor(name, list(shape), F32, kind="ExternalInput").ap()

        def dout(name, shape):
            return nc.dram_tensor(name, list(shape), F32, kind="ExternalOutput").ap()

        ns, sl = self.nseq, self.seqlen
        self.xp = din("xp", [ns, sl, D])
        self.xs = din("xs", [128, D])
        self.ck = din("ck", [L, 16, 128, 128])
        self.cv = din("cv", [L, 16, 128, 128])
        self.smix = din("smix", [L, 32, 512])
        self.sffn = din("sffn", [L, 32, 2 * DFF])
        self.w_in = din("w_in", [L, D, 3328])
        self.w_gate = din("w_gate", [L, D, 3072])
        self.b_gate = din("b_gate", [L, 24, 128])
        self.gln_g = din("gln_g", [L, 512])
        self.gln_b = din("gln_b", [L, 512])
        self.gws = din("gws", [L, 4, 128, 128])
        self.gbs = din("gbs", [L, 512])
        self.mixw = din("mixw", [L, 12, 128])
        self.sinks = din("sinks", [L, 8])
        self.p_br = [din("p_attn", [L, 512, D]), din("p_gmlp", [L, 512, D]), din("p_conv", [L, 512, D])]
        self.w_o = din("w_o", [L, D, D])
        self.ln1_g = din("ln1_g", [L, D])
        self.ln1_b = din("ln1_b", [L, D])
        self.w_up = din("w_up", [L, D, 2 * DFF])
        self.fcw = din("fcw", [L, 132, 128])
        self.fcb = din("fcb", [L, 44, 128])
        self.w_down = din("w_down", [L, DFF, D])
        self.ln2_g = din("ln2_g", [L, D])
        self.ln2_b = din("ln2_b", [L, D])
        self.scr = nc.dram_tensor("wscr", [L, 33, 128, SLOT_EL], BF16, kind="Internal").ap()
        self.scr_wd = nc.dram_tensor("wdscr", [L, 2, 128, 11 * 1024], BF16, kind="Internal").ap()
        self.tabs = din("tabs", [4, 128, 1024])
        self.masks = din("masks", [2, 128, 128])
        self.y_p = dout("y_p", [ns, sl, D])
        self.y_s = dout("y_s", [128, D])
        self.kwp = dout("kwp", [L, ns, 128, 128])
        self.vwp = dout("vwp", [L, ns, 128, 128])
        self.mcp = dout("mcp", [L, ns, 2, 512])
        self.fcp = dout("fcp", [L, ns, 2, 2 * DFF])
        self.kws = dout("kws", [L, 16, 128, 128])
        self.vws = dout("vws", [L, 16, 128, 128])
        self.mcs = dout("mcs", [L, 32, 512])
        self.fcs = dout("fcs", [L, 32, 2 * DFF])
        self.gvs = dout("gvs", [L, 128, 512])

    def alloc(self, S):
        sb = S.sb
        self.x_tok = sb("x_tok", [128, 8, D], F32)
        self.xT = sb("xT", [128, 8, 1024], BF16)
        self.ar = sb("ar", [128, 24, 1024], BF16)
        self.uT = sb("uT", [128, 4, 1026], BF16)
        self.kT = sb("kT", [128, 1152], BF16)
        self.vaug = sb("vaug", [128, 9, 2, 65], BF16)
        self.wr = [sb("wr%d" % i, [128, SLOT_EL], BF16) for i in range(NSLOT)]
        self.tf = sb("tf", [128, 8, 512], F32)
        self.tb = sb("tb", [128, 10, 512], BF16)
        self.macc = sb("macc", [128, 4, 512], F32)
        self.lnbc = sb("lnbc", [128, 2, 1024], F32)
        self.glnbc = sb("glnbc", [128, 2, 512], F32)
        self.E = sb("E", [128, 4, 1024], BF16)
        self.wst = sb("wst", [128, 4, 128], BF16)
        self.gbias = sb("gbias", [128, 4, 128], F32)
        self.identf = sb("identf", [128, 128], F32)
        self.identb = sb("identb", [128, 128], BF16)
        self.maskt = sb("maskt", [128, 2, 128], F32)
        self.prm = sb("prm", [128, L, 212], F32)
        self.esink = sb("esink", [128, L, 8], F32)
        self.lnst = sb("lnst", [128, 4, 16], F32)
        self.att_s = sb("att_s", [128, 4, 16], F32)
        self.kcar = sb("kcar", [128, L, 128], BF16)
        self.vcar = sb("vcar", [128, L, 2, 65], BF16)
        self.ucar = sb("ucar", [128, L, 4, 2], BF16)
        self.upc = sb("upc", [128, L, NUP, 2, 3], F32)
        self.stg0 = sb("stg0", [128, 256], F32)
        self.stg1 = sb("stg1", [128, 512], F32)
        self.kcT = sb("kcT", [128, 16, 128], BF16)
        self.vaugc = sb("vaugc", [128, 16, 2, 65], BF16)
        self.pb = [S.ps("pb%d" % i, [128, 512], F32) for i in range(8)]

    def bank(self):
        b = self.nb % 8
        self.nb += 1
        return b

    def tfs(self):
        i = self.ntf % 8
        self.ntf += 1
        return i

    def ffs(self):
        i = getattr(self, '_nff', 0) % 12
        self._nff = getattr(self, '_nff', 0) + 1
        if i < 8:
            return self.tf[:, i, :], ('tf', i)
        return self.macc[:, i - 8, :], ('macc', i - 8)

    def tbs(self):
        i = self.ntb % 10
        self.ntb += 1
        return i

    @staticmethod
    def akeys(u0, u1, c0, n):
        return [('ar', u, cb) for u in range(u0, u1) for cb in range(c0 // 128, (c0 + n + 127) // 128)]

    def make_pieces(self, groups):
        pcs = []
        for gi_, grp in enumerate(groups):
            for l in range(L):
                base_ = len(pcs)
                def win(c0, n, l=l):
                    return [(0, (8, n), self.w_in[l, :, c0:c0 + n].rearrange("(k p) c -> p k c", p=128))]
                pcs.append(win(C_Q, 512))
                pcs.append(win(C_K, 256))
                pcs.append(win(C_GU, 512))
                pcs.append(win(C_GV, 512))
                pcs.append(win(C_SB, 512))
                pcs.append(win(C_SC, 512))
                pcs.append(win(C_SH, 512))
                for dp in range(4):
                    for i in range(3):
                        c0 = i * 1024 + dp * 256
                        pcs.append([
                            (0, (8, 256), self.w_gate[l, :, c0:c0 + 256].rearrange("(k p) c -> p k c", p=128)),
                            (2048, (4, 256), self.p_br[i][l, :, dp * 256:(dp + 1) * 256].rearrange("(k p) c -> p k c", p=128)),
                        ])
                for ch in range(2):
                    pcs.append([(0, (8, 512), self.w_o[l, :, ch * 512:(ch + 1) * 512].rearrange("(k p) c -> p k c", p=128))])
                for hf in range(2):
                    for (j0, nj) in self.up_pieces(hf):
                        n = nj * 128
                        pcs.append([
                            (0, (8, n), self.w_up[l, :, j0 * 128:j0 * 128 + n].rearrange("(k p) c -> p k c", p=128)),
                            (8 * n, (8, n), self.w_up[l, :, DFF + j0 * 128:DFF + j0 * 128 + n].rearrange("(k p) c -> p k c", p=128)),
                        ])
                for i_ in range(base_, len(pcs)):
                    parts_ = pcs[i_]
                    nel_ = max(off + k * n for (off, (k, n), _) in parts_)
                    pcs[i_] = dict(parts=parts_, l=l, pidx=i_ - base_, nel=nel_, first=(gi_ == 0), multi=(len(groups) > 1))
                assert len(pcs) - base_ == 33
        return pcs

    @staticmethod
    def up_pieces(hf):
        j = hf * 11
        out = []
        for nj in (2, 2, 2, 2, 2, 1):
            out.append((j, nj))
            j += nj
        return out

    def w_init(self, S, pieces):
        self.pieces = pieces
        self.p_loaded = 0
        self.p_next = 0
        self.slot_free = [True] * NSLOT
        self._w_pump(S)

    def _w_pump(self, S):
        while self.p_loaded < len(self.pieces):
            slot = self.p_loaded % NSLOT
            if not self.slot_free[slot]:
                break
            pc = self.pieces[self.p_loaded]
            parts = pc['parts']
            wr = self.wr[slot]
            skey = ('scr', pc['l'], pc['pidx'])
            sap = self.scr[pc['l'], pc['pidx'], :, 0:pc['nel']]
            if pc['first'] or not pc['multi']:
                def fn(e, parts=parts, wr=wr):
                    r = []
                    for (off, (k, n), src) in parts:
                        dst = wr[:, off:off + k * n].rearrange("p (k c) -> p k c", k=k)
                        r.append(e.dma_start(out=dst, in_=src))
                    return r
                S.dma('pool', None, None, chan='wr%d' % slot, writes=[('wr', slot)], n=len(parts), fn=fn)
                if pc['multi']:
                    S.dma('sp', sap, wr[:, 0:pc['nel']], chan='wb%d' % slot, reads=[('wr', slot)], writes=[skey])
            else:
                S.dma('pool', wr[:, 0:pc['nel']], sap, chan='wr%d' % slot, reads=[skey], writes=[('wr', slot)])
            self.slot_free[slot] = False
            self.p_loaded += 1

    def w_get(self, S):
        assert self.p_next < self.p_loaded, "weight piece not loaded (ring too small)"
        slot = self.p_next % NSLOT
        self.p_next += 1
        return slot

    def w_done(self, S, slot):
        import os
        if 'nopump' in os.environ.get('KDBG', ''):
            return
        self.slot_free[slot] = True
        self._w_pump(S)

    def wv(self, slot, off, k, n):
        return self.wr[slot][:, off:off + k * n].rearrange("p (k c) -> p k c", k=k)

    def _patch(self, S, keys, tok):
        for k in keys:
            S.lastw[k] = tok

    def init_consts(self, S):
        S.op('pool', lambda e: e.memset(self.identf[:], 0.0), writes=['identf'])
        S.op('pool', lambda e: e.affine_select(out=self.identf[:], in_=self.identf[:], pattern=[[-1, 128]],
                                               compare_op=ALU.not_equal, fill=1.0, base=0, channel_multiplier=1),
             reads=['identf'], writes=['identf'])
        S.op('dve', lambda e: e.tensor_copy(self.identb[:], self.identf[:]), reads=['identf'], writes=['identb'])
        S.op('dve', lambda e: e.memset(self.vaug[:, :, :, 64:65], 1.0), writes=['vaug_ones'])
        S.op('dve', lambda e: e.memset(self.vaugc[:, :, :, 64:65], 1.0), writes=['vaugc_ones'])
        S.op('dve', lambda e: e.memset(self.vcar[:, :, :, 64:65], 1.0), writes=['vcar_ones'])
        keys = ['maskt', 'esink']
        tok = S.dma('sp', self.maskt[:], self.masks.rearrange("m s t -> s m t"), chan='init', writes=['maskt'])
        tmps = []
        for i in range(4):
            s0 = self.tfs(); s1 = self.tfs()
            assert s1 == s0 + 1
            tmp = self.tf[:, s0:s0 + 2, :].rearrange("p a b -> p (a b)")
            tok = S.dma('sp', tmp, self.tabs[i], chan='init', writes=[('tf', s0), ('tf', s1)])
            keys += [('tf', s0), ('tf', s1)]
            tmps.append((tmp, s0, s1))
        tok = S.dma('sp', self.esink[:].rearrange("p l h -> p (l h)"),
                    self.sinks.rearrange("l h -> (l h)").partition_broadcast(128), chan='init', writes=['esink'])
        self._patch(S, keys, tok)
        for i in range(4):
            tmp, s0, s1 = tmps[i]
            S.op('act', lambda e, i=i, tmp=tmp: e.activation(self.E[:, i, :], tmp, AF.Exp),
                 reads=[('tf', s0), ('tf', s1)], writes=[('E', i)])
        S.op('act', lambda e: e.activation(self.esink[:], self.esink[:], AF.Exp), reads=['esink'], writes=['esink'])
        sts = []
        keys = []
        for l in range(L):
            s0 = self.tfs()
            st = self.tf[:, s0, :]
            k = [('tf', s0)]
            S.dma('sp', st[0:24, 0:128], self.b_gate[l], chan='init2', writes=k)
            S.dma('sp', st[24:36, 0:128], self.mixw[l], chan='init2', writes=k)
            S.dma('sp', st[36:80, 0:128], self.fcb[l], chan='init2', writes=k)
            S.dma('sp', st[0:128, 128:256], self.fcw[l, 0:128, :], chan='init2', writes=k)
            tok = S.dma('sp', st[0:4, 256:384], self.fcw[l, 128:132, :], chan='init2', writes=k)
            keys += k
            sts.append((st, s0))
        self._patch(S, keys, tok)
        for l in range(L):
            st, s0 = sts[l]
            b = self.bank()
            pb = self.pb[b]

            def tr(e, st=st, pb=pb):
                e.transpose(pb[:, 0:80], st[0:80, 0:128], self.identf[0:80, 0:80])
                e.transpose(pb[:, 80:208], st[0:128, 128:256], self.identf[:])
                return e.transpose(pb[:, 208:212], st[0:4, 256:384], self.identf[0:4, 0:4])
            S.op('pe', tr, reads=[('tf', s0), 'identf'], writes=[('pb', b)])
            S.op('dve', lambda e, l=l, pb=pb: e.tensor_copy(self.prm[:, l, :], pb[:, 0:212]),
                 reads=[('pb', b)], writes=[('prm', l)])

    def p_bg(self, l, j):
        return self.prm[:, l, j:j + 1]

    def p_mixw(self, l, k, c):
        return self.prm[:, l, 24 + k * 4 + c:24 + k * 4 + c + 1]

    def p_fcb(self, l, col):
        return self.prm[:, l, 36 + col:36 + col + 1]

    def p_fcw(self, l, k, col):
        return self.prm[:, l, 80 + k * 44 + col:80 + k * 44 + col + 1]

    def fm(self, S, slot, wview, c0, src_fn, nk, halves, rkeys, evac, mcols=128, prow=None):
        banks = [self.bank() for _ in halves]

        import os
        seq = 'seq' in os.environ.get('KDBG', '')

        def mm(e):
            r = None
            if seq:
                for hi, (h0, n) in enumerate(halves):
                    for kc in range(nk):
                        r = e.matmul(self.pb[banks[hi]][0:mcols, 0:n], lhsT=wview[:, kc, c0:c0 + mcols],
                                     rhs=src_fn(kc, h0, n), start=(kc == 0), stop=(kc == nk - 1))
                return r
            for kc in range(nk):
                for hi, (h0, n) in enumerate(halves):
                    r = e.matmul(self.pb[banks[hi]][0:mcols, 0:n], lhsT=wview[:, kc, c0:c0 + mcols],
                                 rhs=src_fn(kc, h0, n), start=(kc == 0), stop=(kc == nk - 1))
            return r
        S.op('pe', mm, reads=[('wr', slot)] + rkeys, writes=[('pb', b) for b in banks])
        import os
        if 'noevac' in os.environ.get('KDBG', ''):
            return
        for hi, (h0, n) in enumerate(halves):
            evac(hi, h0, n, banks[hi])

    def sub(self):
        import os
        self._subc = getattr(self, '_subc', 0) + 1
        if self._subc > int(os.environ.get("KSUB", "999")):
            raise StopIteration

    def xt_keys(self, h0, n):
        return [('xT', t) for t in range(h0 // 128, (h0 + n) // 128)]

    def load_x(self, S, grp):
        if grp.get('preloaded'):
            return
        if grp['kind'] == 'p':
            b, g = grp['b'], grp['g']
            tok = None
            for t in range(8):
                r0 = g * 1024 + t * 128
                tok = S.dma('sp', self.x_tok[:, t, :], self.xp[b, r0:r0 + 128, :], chan='xin', writes=[('xtok', t)])
            self._patch(S, [('xtok', t) for t in range(8)], tok)
        else:
            S.dma('sp', self.x_tok[:, 0, :], self.xs, chan='xin', writes=[('xtok', 0)])

    def build_xT(self, S, t):
        for kh in range(2):
            b = self.bank()
            pb = self.pb[b]

            def tr(e, kh=kh, pb=pb):
                r = None
                for k in range(4):
                    kc = kh * 4 + k
                    r = e.transpose(pb[:, k * 128:(k + 1) * 128], self.x_tok[:, t, kc * 128:(kc + 1) * 128], self.identf[:])
                return r
            S.op('pe', tr, reads=[('xtok', t), 'identf'], writes=[('pb', b)])
            S.op('act', lambda e, kh=kh, pb=pb: e.copy(self.xT[:, kh * 4:(kh + 1) * 4, t * 128:(t + 1) * 128],
                                                      pb[:].rearrange("p (k t) -> p k t", k=4)),
                 reads=[('pb', b)], writes=[('xT', t)])

    def layer_norm(self, S, src, dst, width, gam, bet, rkeys, wkeys, bckeys, src_keys_extra=(), recip=True):
        i = self.nln % 4
        self.nln += 1
        st = self.lnst[:, i, :]
        nchunk = width // 512
        tfi = None

        def bn(e):
            r = None
            for c in range(nchunk):
                r = e.bn_stats(st[:, c * 6:(c + 1) * 6], src[:, c * 512:(c + 1) * 512])
            return r
        k = ('lnst', i)
        S.op('dve', bn, reads=rkeys, writes=[k])
        S.op('dve', lambda e: e.bn_aggr(st[:, 12:14], st[:, 0:6 * nchunk]), reads=[k], writes=[k])
        S.op('act', lambda e: e.activation(st[:, 14:15], st[:, 13:14], AF.Sqrt, bias=EPS, scale=1.0), reads=[k], writes=[k])
        if recip:
            S.op('dve', lambda e: e.reciprocal(st[:, 15:16], st[:, 14:15]), reads=[k], writes=[k])
        return st, k

    def proj(self, S, grp, l):
        GT, halves, NTL = grp['GT'], grp['halves'], grp['NTL']
        kind = grp['kind']
        xsrc = lambda kc, h0, n: self.xT[:, kc, h0:h0 + n]
        allx = [('xT', t) for t in range(NTL)]

        def evac_copy(unit):
            def ev(hi, h0, n, b):
                S.op('act', lambda e: e.copy(self.ar[:, unit, h0:h0 + n], self.pb[b][:, 0:n]),
                     reads=[('pb', b)], writes=self.akeys(unit, unit + 1, h0, n))
            return ev
        slot = self.w_get(S)
        W = self.wv(slot, 0, 8, 512)
        import os
        dbg = os.environ.get('KDBG', '')
        for c in range(4):
            self.fm(S, slot, W, c * 128, xsrc, 8, halves, allx, evac_copy(c))
            yield None
        self.w_done(S, slot)
        slot = self.w_get(S)
        W = self.wv(slot, 0, 8, 256)

        def ev_k(hi, h0, n, b):
            S.op('act', lambda e: e.copy(self.kT[:, 128 + h0:128 + h0 + n], self.pb[b][:, 0:n]),
                 reads=[('pb', b)], writes=[('kT', 1 + t) for t in range(h0 // 128, (h0 + n) // 128)])
        self.fm(S, slot, W, 0, xsrc, 8, halves, allx, ev_k)
        yield None
        for t in range(NTL):
            b = self.bank()
            pb = self.pb[b]

            def mm(e, t=t, pb=pb, W=W):
                r = None
                for kc in range(8):
                    r = e.matmul(pb[:, 0:256], lhsT=self.xT[:, kc, t * 128:(t + 1) * 128], rhs=W[:, kc, 0:256],
                                 start=(kc == 0), stop=(kc == 7))
                return r
            S.op('pe', mm, reads=[('wr', slot), ('xT', t)], writes=[('pb', b)])
            if 'novaug' not in dbg:
              S.op(os.environ.get('KVENG', 'dve'), lambda e, t=t, pb=pb: (e.tensor_copy if os.environ.get('KVENG', 'dve') == 'dve' else e.copy)(self.vaug[:, t + 1, :, 0:64],
                                                             pb[:, 128:256].rearrange("p (k d) -> p k d", k=2)),
                 reads=[('pb', b)], writes=[('vaug', t + 1)])
            if 'nostg' in dbg:
                continue
            if (kind == 'p' and grp['last'] and t == NTL - 1) or kind == 's':
                S.op('act', lambda e, pb=pb: e.copy(self.stg0[:, 0:256], pb[:, 0:256]), reads=[('pb', b)], writes=[('stg', 0)])
                if kind == 'p':
                    S.dma('sp', self.kwp[l, grp['b']], self.stg0[:, 0:128], chan='st0', reads=[('stg', 0)], out_final=True)
                    S.dma('sp', self.vwp[l, grp['b']], self.stg0[:, 128:256], chan='st0b', reads=[('stg', 0)], out_final=True)
                else:
                    S.dma('sp', self.kws[l][:, 120:128, :], self.stg0[:, 0:128], chan='st0', reads=[('stg', 0)], out_final=True)
                    S.dma('sp', self.vws[l][:, 120:128, :], self.stg0[:, 128:256], chan='st0b', reads=[('stg', 0)], out_final=True)
                    S.dma('sp', self.kws[l][:, 0:120, :], self.ck[l][:, 8:128, :], chan='cc0', out_final=True)
                    S.dma('sp', self.vws[l][:, 0:120, :], self.cv[l][:, 8:128, :], chan='cc1', out_final=True)
            yield None
        self.w_done(S, slot)
        yield 'KV_DONE'
        slot = self.w_get(S)
        W = self.wv(slot, 0, 8, 512)
        for c in range(4):
            self.fm(S, slot, W, c * 128, xsrc, 8, halves, allx, evac_copy(4 + c))
            yield None
        self.w_done(S, slot)
        slot = self.w_get(S)
        W = self.wv(slot, 0, 8, 512)
        for t in range(NTL):
            b = self.bank()
            pb = self.pb[b]

            def mm(e, t=t, pb=pb, W=W):
                r = None
                for kc in range(8):
                    r = e.matmul(pb[:, 0:512], lhsT=self.xT[:, kc, t * 128:(t + 1) * 128], rhs=W[:, kc, 0:512],
                                 start=(kc == 0), stop=(kc == 7))
                return r
            S.op('pe', mm, reads=[('wr', slot), ('xT', t)], writes=[('pb', b)])
            st, k = self.layer_norm(S, pb[:, 0:512], None, 512, None, None, [('pb', b)], None, None)
            ti = self.tfs()
            tmp = self.tf[:, ti, :]
            S.op('dve', lambda e, pb=pb, tmp=tmp, st=st: e.scalar_tensor_tensor(tmp, pb[:, 0:512], st[:, 12:13], self.glnbc[:, 0, :], ALU.subtract, ALU.mult),
                 reads=[('pb', b), k, ('glnbc', l)], writes=[('tf', ti)])
            u, c0 = 12 + t // 2, (t % 2) * 512
            gv_dst = self.ar[:, u, c0:c0 + 512]
            if kind == 's':
                S.op('dve', lambda e, tmp=tmp, st=st: e.scalar_tensor_tensor(tmp, tmp, st[:, 15:16], self.glnbc[:, 1, :], ALU.mult, ALU.add),
                     reads=[('tf', ti), ('glnbc', l), k], writes=[('tf', ti)])
                S.op('act', lambda e, tmp=tmp, gv_dst=gv_dst: e.copy(gv_dst, tmp), reads=[('tf', ti)], writes=self.akeys(u, u + 1, c0, 512))
                S.dma('sp', self.gvs[l], tmp, chan='gvs', reads=[('tf', ti)], out_final=True)
            else:
                S.op('dve', lambda e, tmp=tmp, gv_dst=gv_dst, st=st: e.scalar_tensor_tensor(gv_dst, tmp, st[:, 15:16], self.glnbc[:, 1, :], ALU.mult, ALU.add),
                     reads=[('tf', ti), ('glnbc', l), k], writes=self.akeys(u, u + 1, c0, 512))
            if t >= 2:
                self.gmlp_block(S, grp, l, t - 2)
            yield None
        self.w_done(S, slot)
        slot = self.w_get(S)
        W = self.wv(slot, 0, 8, 512)
        for c in range(4):
            self.fm(S, slot, W, c * 128, xsrc, 8, halves, allx, evac_copy(8 + c))
            if c < 2 and NTL - 2 + c >= 0 and NTL >= 2:
                self.gmlp_block(S, grp, l, NTL - 2 + c)
            yield None
        if NTL < 2:
            self.gmlp_block(S, grp, l, 0)
        self.w_done(S, slot)
        self.conv_carry_in(S, grp, l)
        slot_c = self.w_get(S)
        slot_h = self.w_get(S)
        Wc = self.wv(slot_c, 0, 8, 512)
        Wh = self.wv(slot_h, 0, 8, 512)
        for c in range(4):
            tmps = {}

            def ev_c(hi, h0, n, b, tmps=tmps):
                ti = self.tfs()
                tmps[hi] = ti
                S.op('act', lambda e: e.copy(self.tf[:, ti, 0:n], self.pb[b][:, 0:n]), reads=[('pb', b)], writes=[('tf', ti)])

            def ev_h(hi, h0, n, b, tmps=tmps, c=c):
                ti = tmps[hi]
                if kind == 'p':
                    dst = self.uT[:, c, 2 + h0:2 + h0 + n]
                    in0 = self.pb[b][:, 0:n]
                    in1 = self.tf[:, ti, 0:n]
                else:
                    dst = self.uT[:, c, 0:160].rearrange("p (s j) -> p s j", j=10)[:, :, 2:10]
                    in0 = self.pb[b][:, 0:n].rearrange("p (s j) -> p s j", j=8)
                    in1 = self.tf[:, ti, 0:n].rearrange("p (s j) -> p s j", j=8)
                S.op('dve', lambda e: e.tensor_tensor(dst, in0, in1, ALU.mult),
                     reads=[('pb', b), ('tf', ti)], writes=[('uT', c, hi)])
            self.fm(S, slot_c, Wc, c * 128, xsrc, 8, halves, allx, ev_c)
            yield None
            self.fm(S, slot_h, Wh, c * 128, xsrc, 8, halves, allx, ev_h)
            self.conv_chunk(S, grp, l, c)
            yield None
        self.conv_carry_out(S, grp, l)
        if (kind == 'p' and grp['last']) or kind == 's':
            t = NTL - 1
            bc = self.bank(); bh = self.bank()

            def mm(e, t=t, bc=bc, bh=bh, Wc=Wc, Wh=Wh):
                r = None
                for (W_, bb) in ((Wc, bc), (Wh, bh)):
                    for kc in range(8):
                        r = e.matmul(self.pb[bb][:, 0:512], lhsT=self.xT[:, kc, t * 128:(t + 1) * 128], rhs=W_[:, kc, 0:512],
                                     start=(kc == 0), stop=(kc == 7))
                return r
            S.op('pe', mm, reads=[('wr', slot_c), ('wr', slot_h), ('xT', t)], writes=[('pb', bc), ('pb', bh)])
            ti = self.tfs()
            S.op('act', lambda e, ti=ti, bc=bc: e.copy(self.tf[:, ti, :], self.pb[bc][:, 0:512]), reads=[('pb', bc)], writes=[('tf', ti)])
            S.op('dve', lambda e, ti=ti, bh=bh: e.tensor_tensor(self.stg1[:, :], self.pb[bh][:, 0:512], self.tf[:, ti, :], ALU.mult),
                 reads=[('pb', bh), ('tf', ti)], writes=[('stg', 1)])
            if kind == 'p':
                S.dma('sp', self.mcp[l, grp['b']], self.stg1[126:128, :], chan='st1', reads=[('stg', 1)], out_final=True)
            else:
                for r in range(2):
                    S.dma('sp', self.mcs[l].rearrange("(s r) c -> s r c", r=2)[:, r, :],
                          self.stg1[:, :], chan='st1%d' % r, reads=[('stg', 1)], out_final=True,
                          fn=(lambda e, r=r: [e.dma_start(out=self.mcs[l].rearrange("(s r) c -> s r c", r=2)[:, r, :],
                                                          in_=self.sel_rows(self.stg1[:, :], 6 + r))]))
        self.w_done(S, slot_c)
        self.w_done(S, slot_h)

    def sel_rows(self, ap2d, r):
        return ap2d[r:128:8, :]

    def attn_pv_norm(self, S, l, t, pv_fn, extra_reads, outcols):
        bA = self.bank(); bB = self.bank()
        banks = (bA, bB)
        S.op('pe', lambda e: pv_fn(e, self.pb[bA], self.pb[bB]), reads=extra_reads, writes=[('pb', bA), ('pb', bB)])
        i = self.nln % 4
        self.nln += 1
        sc = self.att_s[:, i, :]
        k = ('atts', i)
        for gi in range(2):
            pv = self.pb[banks[gi]][:, 0:260].rearrange("p (h d) -> p h d", h=4)
            S.op('dve', lambda e, pv=pv, gi=gi: e.tensor_tensor(sc[:, gi * 4:(gi + 1) * 4].unsqueeze(2), pv[:, :, 64:65],
                                                               self.esink[:, l, gi * 4:(gi + 1) * 4].unsqueeze(2), ALU.add),
                 reads=[('pb', banks[gi]), 'esink'], writes=[k])
        S.op('dve', lambda e: e.reciprocal(sc[:, 8:16], sc[:, 0:8]), reads=[k], writes=[k])
        tb = self.tbs()
        at = self.tb[:, tb, :]
        for gi in range(2):
            pv = self.pb[banks[gi]][:, 0:260].rearrange("p (h d) -> p h d", h=4)
            S.op('dve', lambda e, pv=pv, gi=gi: e.tensor_tensor(
                at[:, gi * 256:(gi + 1) * 256].rearrange("p (h d) -> p h d", h=4), pv[:, :, 0:64],
                sc[:, 8 + gi * 4:8 + (gi + 1) * 4].unsqueeze(2).to_broadcast([128, 4, 64]), ALU.mult),
                reads=[('pb', banks[gi]), k], writes=[('tb', tb)])
        return at, tb

    def attn_tr(self, S, at, tb, outcols):
        bT = self.bank()
        pbt = self.pb[bT][:].bitcast(BF16)

        def tr(e):
            r = None
            for c in range(4):
                r = e.transpose(pbt[:, c * 128:(c + 1) * 128], at[:, c * 128:(c + 1) * 128], self.identb[:])
            return r
        S.op('pe', tr, reads=[('tb', tb), 'identb'], writes=[('pb', bT)])
        S.op('act', lambda e: e.copy(self.ar[:, 0:4, outcols:outcols + 128], pbt[:, 0:512].rearrange("p (c t) -> p c t", c=4)),
             reads=[('pb', bT)], writes=self.akeys(0, 4, outcols, 128))

    def attn_scores(self, S, t, gI, kT_ap, kkeys, E_ap, ekey, qcols=128, mrows=128):
        b = self.bank()
        pb = self.pb[b]
        lo, hi = gI * 64, (gI + 1) * 64
        S.op('pe', lambda e: e.matmul(pb[0:mrows, 0:512], lhsT=kT_ap[lo:hi, :],
                                      rhs=self.ar[lo:hi, 0:4, t * 128:t * 128 + 128], start=True, stop=True),
             reads=kkeys + self.akeys(0, 4, t * 128, 128), writes=[('pb', b)])
        tb = self.tbs()
        S.op('act', lambda e: e.activation(self.tb[0:mrows, tb, :], pb[0:mrows, 0:512], AF.Exp, scale=0.125),
             reads=[('pb', b)], writes=[('tb', tb)])
        S.op('dve', lambda e: e.tensor_tensor(self.tb[0:mrows, tb, :], self.tb[0:mrows, tb, :], E_ap, ALU.mult),
             reads=[('tb', tb), ekey], writes=[('tb', tb)])
        return tb

    def attention_prompt(self, S, grp, l):
        NTL = grp['NTL']
        st = {}

        def scores(t):
            first = grp['first'] and t == 0
            chunks = [1] if first else [0, 1]
            pts = {}
            for gI in range(2):
                for c in chunks:
                    kT_ap = self.kT[:, (t + c) * 128:(t + c + 1) * 128]
                    E_ap = self.E[:, c, gI * 512:(gI + 1) * 512]
                    pts[(gI, c)] = self.attn_scores(S, t, gI, kT_ap, [('kT', t + c)], E_ap, ('E', c))
            st[t] = (chunks, pts)

        def pv(t):
            chunks, pts = st[t]

            def pv_fn(e, pA, pB):
                r = None
                for h in range(8):
                    gI, hh = h // 4, h % 4
                    pbk = pA if gI == 0 else pB
                    for ci, c in enumerate(chunks):
                        r = e.matmul(pbk[:, hh * 65:(hh + 1) * 65], lhsT=self.tb[:, pts[(gI, c)], hh * 128:(hh + 1) * 128],
                                     rhs=self.vaug[:, t + c, gI, :], start=(ci == 0), stop=(ci == len(chunks) - 1))
                return r
            reads = [('tb', v) for v in pts.values()] + [('vaug', t + c) for c in chunks] + ['vaug_ones']
            st[t] = self.attn_pv_norm(S, l, t, pv_fn, reads, t * 128)

        for t in range(NTL + 2):
            if t < NTL:
                scores(t)
                yield None
            if 0 <= t - 1 < NTL:
                pv(t - 1)
                yield None
            if 0 <= t - 2 < NTL:
                at, tb = st[t - 2]
                self.attn_tr(S, at, tb, (t - 2) * 128)
                yield None

    def attention_sample_prep(self, S, grp, l):
        ckst = self.x_tok[:, 1:3, :].rearrange("p a (s c) -> p (a s) c", c=128)
        cvst = self.x_tok[:, 3:5, :].rearrange("p a (s c) -> p (a s) c", c=128)
        for (dst_, src_, ch_, keys_) in ((ckst, self.ck, 'ckin', ['ckst', ('xtok', 1), ('xtok', 2)]),
                                         (cvst, self.cv, 'cvin', ['cvst', ('xtok', 3), ('xtok', 4)])):
            tok = None
            for q4 in range(4):
                tok = S.dma('sp', dst_[:, q4 * 4:(q4 + 1) * 4, :], src_[l][q4 * 4:(q4 + 1) * 4].rearrange("q s c -> s q c"),
                            chan=ch_, writes=keys_)
            self._patch(S, keys_, tok)
        S.op('dve', lambda e: e.tensor_copy(self.vaugc[:, :, :, 0:64], cvst.rearrange("p q (k d) -> p q k d", k=2)),
             reads=['cvst'], writes=['vaugc'])
        S.dma('sp', self.stg1[0:32, :], self.smix[l], chan='smixin', writes=[('stg', 1)])

    def attention_sample_prep_tr(self, S, grp, l):
        ckst = self.x_tok[:, 1:3, :].rearrange("p a (s c) -> p (a s) c", c=128)
        for q4 in range(4):
            b = self.bank()
            pb = self.pb[b]

            def tr(e, q4=q4, pb=pb):
                r = None
                for j in range(4):
                    r = e.transpose(pb[:, j * 128:(j + 1) * 128], ckst[:, q4 * 4 + j, :], self.identf[:])
                return r
            S.op('pe', tr, reads=['ckst', 'identf'], writes=[('pb', b)])
            S.op('act', lambda e, q4=q4, pb=pb: e.copy(self.kcT[:, q4 * 4:(q4 + 1) * 4, :], pb[:].rearrange("p (j s) -> p j s", j=4)),
                 reads=[('pb', b)], writes=[('kcT', q4)])

    def attention_sample(self, S, grp, l):
        pt0 = {}
        for gI in range(2):
            b = self.bank()
            pb = self.pb[b]
            lo, hi = gI * 64, (gI + 1) * 64

            def mm(e, pb=pb, lo=lo, hi=hi):
                r = None
                for q in range(16):
                    r = e.matmul(pb[:, q * 32:(q + 1) * 32], lhsT=self.kcT[lo:hi, q, :],
                                 rhs=self.ar[lo:hi, 0:4, q * 8:(q + 1) * 8], start=True, stop=True)
                return r
            S.op('pe', mm, reads=[('kcT', i) for i in range(4)] + self.akeys(0, 4, 0, 128), writes=[('pb', b)])
            tb = self.tbs()
            S.op('act', lambda e, tb=tb, pb=pb: e.activation(self.tb[:, tb, :], pb[:, 0:512], AF.Exp, scale=0.125),
                 reads=[('pb', b)], writes=[('tb', tb)])
            S.op('dve', lambda e, tb=tb, gI=gI: e.tensor_tensor(self.tb[:, tb, :], self.tb[:, tb, :],
                                                               self.E[:, 2, gI * 512:(gI + 1) * 512], ALU.mult),
                 reads=[('tb', tb), ('E', 2)], writes=[('tb', tb)])
            pt0[gI] = tb
        o0T = self.x_tok[0:65, 7, :]
        bo = [self.bank(), self.bank()]

        def pv0(e):
            r = None
            for q in range(16):
                for gI in range(2):
                    col = ((q % 8) * 2 + gI) * 32
                    r = e.matmul(self.pb[bo[q // 8]][0:65, col:col + 32], lhsT=self.vaugc[:, q, gI, :],
                                 rhs=self.tb[:, pt0[gI], q * 32:(q + 1) * 32], start=True, stop=True)
            return r
        S.op('pe', pv0, reads=[('tb', pt0[0]), ('tb', pt0[1]), 'vaugc', 'vaugc_ones'], writes=[('pb', bo[0]), ('pb', bo[1])])
        o0w = o0T.rearrange("p (g h q t) -> p g h q t", g=2, h=4, q=16)
        for hq in range(2):
            for gI in range(2):
                S.op('act', lambda e, hq=hq, gI=gI: e.copy(
                    o0w[:, gI, :, hq * 8:(hq + 1) * 8, :].rearrange("p h q t -> p q h t"),
                    self.pb[bo[hq]][0:65, 0:512].rearrange("p (q g h t) -> p q g h t", q=8, g=2, h=4)[:, :, gI, :, :]),
                    reads=[('pb', bo[hq])], writes=['o0T', ('xtok', 7)])
        pts = {}
        for gI in range(2):
            pts[gI] = self.attn_scores(S, 0, gI, self.kT[:, 128:256], [('kT', 1)], self.E[:, 3, gI * 512:(gI + 1) * 512], ('E', 3))

        def pv_fn(e, pA, pB):
            r = None
            for h in range(8):
                gI, hh = h // 4, h % 4
                pbk = pA if gI == 0 else pB
                e.matmul(pbk[:, hh * 65:(hh + 1) * 65], lhsT=self.tb[:, pts[gI], hh * 128:(hh + 1) * 128],
                         rhs=self.vaug[:, 1, gI, :], start=True, stop=False)
                r = e.matmul(pbk[:, hh * 65:(hh + 1) * 65], lhsT=o0w[:, gI, hh, :, :].rearrange("p q t -> p (q t)"),
                             rhs=self.identf[0:65, 0:65], start=False, stop=True)
            return r
        reads = [('tb', pts[0]), ('tb', pts[1]), ('vaug', 1), 'vaug_ones', 'o0T', 'identf']
        at, tb = self.attn_pv_norm(S, l, 0, pv_fn, reads, 0)
        self.attn_tr(S, at, tb, 0)
        yield None

    def gmlp_prep(self, S, grp, l):
        kind = grp['kind']
        ti = self.tfs()
        wld = self.tf[:, ti, :].rearrange("p (g s) -> p g s", g=4)
        if kind == 'p':
            S.dma('sp', wld, self.gws[l].rearrange("g t s -> t g s"), chan='gwin', writes=[('tf', ti)])
            tok = None
        else:
            ti0 = self.tfs()
            stage = self.tf[:, ti0, 0:32]
            tok = None
            for q in range(16):
                tok = S.dma('sp', self.tf[q * 8:(q + 1) * 8, ti0, 0:32].rearrange("p (g s) -> p g s", g=4),
                            self.gws[l][:, 0:8, 0:8].rearrange("g t s -> t g s"), chan='gwin', writes=[('tf', ti0)])
            self._patch(S, [('tf', ti0)], tok)
            S.op('dve', lambda e: e.tensor_copy(self.tf[:, ti, :].rearrange("p (g q s) -> p g q s", g=4, q=16),
                                                stage.rearrange("p (g s) -> p g s", g=4).unsqueeze(2).to_broadcast([128, 4, 16, 8])),
                 reads=[('tf', ti0)], writes=[('tf', ti)])
        b = self.bank()
        pb = self.pb[b]

        def tr(e):
            r = None
            for g in range(4):
                r = e.transpose(pb[:, g * 128:(g + 1) * 128], wld[:, g, :], self.identf[:])
            return r
        S.op('pe', tr, reads=[('tf', ti), 'identf'], writes=[('pb', b)])
        mi = 0 if kind == 'p' else 1
        S.op('dve', lambda e: e.tensor_tensor(self.wst[:], pb[:].rearrange("p (g t) -> p g t", g=4),
                                              self.maskt[:, mi, :].unsqueeze(1).to_broadcast([128, 4, 128]), ALU.mult),
             reads=[('pb', b), 'maskt'], writes=['wst'])
        if kind == 'p':
            S.dma('sp', self.gbias[:].rearrange("p g t -> p (g t)"), self.gbs[l].partition_broadcast(128), chan='gbin', writes=['gbias'])
        else:
            ti2 = self.tfs()
            tmp = self.tf[:, ti2, :]
            S.dma('sp', tmp, self.gbs[l].partition_broadcast(128), chan='gbin', writes=[('tf', ti2)])
            S.op('dve', lambda e: e.tensor_copy(self.gbias[:].rearrange("p g (q s) -> p g q s", q=16),
                                                tmp.rearrange("p (g t) -> p g t", g=4)[:, :, 0:8].unsqueeze(2).to_broadcast([128, 4, 16, 8])),
                 reads=[('tf', ti2)], writes=['gbias'])
        S.dma('sp', self.glnbc[:, 0, :], self.gln_g[l].partition_broadcast(128), chan='glg', writes=[('glnbc', l)])
        S.dma('sp', self.glnbc[:, 1, :], self.gln_b[l].partition_broadcast(128), chan='glb', writes=[('glnbc', l)])

    def gmlp_block(self, S, grp, l, t):
        if True:
            u, c0 = 12 + t // 2, (t % 2) * 512
            b = self.bank()
            pb = self.pb[b]

            def mm(e, pb=pb, u=u, c0=c0):
                r = None
                for g in range(4):
                    r = e.matmul(pb[:, g * 128:(g + 1) * 128], lhsT=self.ar[:, u, c0 + g * 128:c0 + (g + 1) * 128],
                                 rhs=self.wst[:, g, :], start=True, stop=True)
                return r
            S.op('pe', mm, reads=self.akeys(u, u + 1, c0, 512) + ['wst'], writes=[('pb', b)])
            ti = self.tfs()
            S.op('dve', lambda e, pb=pb, ti=ti: e.tensor_tensor(self.tf[:, ti, :], pb[:, 0:512], self.gbias[:].rearrange("p g t -> p (g t)"), ALU.add),
                 reads=[('pb', b), 'gbias'], writes=[('tf', ti)])
            gu = self.ar[:, 4:8, t * 128:(t + 1) * 128]
            S.op('dve', lambda e, ti=ti, gu=gu: e.tensor_tensor(gu, gu, self.tf[:, ti, :].rearrange("p (g t) -> p g t", g=4), ALU.mult),
                 reads=[('tf', ti)] + self.akeys(4, 8, t * 128, 128), writes=self.akeys(4, 8, t * 128, 128))

    def conv_carry_in(self, S, grp, l):
        kind, GT, halves = grp['kind'], grp['GT'], grp['halves']
        if kind == 'p':
            if grp['first']:
                S.op('dve', lambda e: e.memset(self.uT[:, :, 0:2], 0.0), writes=[('uTc',)])
            else:
                S.op('dve', lambda e: e.tensor_copy(self.uT[:, :, 0:2], self.ucar[:, l, :, :]), reads=[('ucar', l)], writes=[('uTc',)])
        else:
            b = self.bank()
            pb = self.pb[b]

            def tr(e, pb=pb):
                r = None
                for c in range(4):
                    r = e.transpose(pb[:, c * 32:(c + 1) * 32], self.stg1[0:32, c * 128:(c + 1) * 128], self.identf[0:32, 0:32])
                return r
            S.op('pe', tr, reads=[('stg', 1), 'identf'], writes=[('pb', b)])
            S.op('dve', lambda e, pb=pb: e.tensor_copy(self.uT[:, :, 0:160].rearrange("p c (s j) -> p c s j", j=10)[:, :, :, 0:2],
                                                pb[:, 0:128].rearrange("p (c s r) -> p c s r", c=4, r=2)),
                 reads=[('pb', b)], writes=[('uTc',)])

    def conv_chunk(self, S, grp, l, c):
        kind, GT, halves = grp['kind'], grp['GT'], grp['halves']
        if True:
            for hi, (h0, n) in enumerate(halves):
                ti = self.tfs()
                if kind == 'p':
                    tmp = self.tf[:, ti, 0:n]
                    u2 = self.uT[:, c, 2 + h0:2 + h0 + n]
                    u1 = self.uT[:, c, 1 + h0:1 + h0 + n]
                    u0 = self.uT[:, c, h0:h0 + n]
                    sbv = self.ar[:, 8 + c, h0:h0 + n]
                else:
                    uv = self.uT[:, c, 0:160].rearrange("p (s j) -> p s j", j=10)
                    tmp = self.tf[:, ti, 0:128].rearrange("p (s j) -> p s j", j=8)
                    u2, u1, u0 = uv[:, :, 2:10], uv[:, :, 1:9], uv[:, :, 0:8]
                    sbv = self.ar[:, 8 + c, 0:128].rearrange("p (s j) -> p s j", j=8)
                rk = [('uT', c, hi), ('uTc',)] + ([('uT', c, hi - 1)] if hi > 0 else [])
                S.op('dve', lambda e, tmp=tmp, u2=u2, c=c: e.tensor_scalar(tmp, u2, self.p_mixw(l, 2, c), None, ALU.mult),
                     reads=rk + [('prm', l)], writes=[('tf', ti)])
                S.op('dve', lambda e, tmp=tmp, u1=u1, c=c: e.scalar_tensor_tensor(tmp, u1, self.p_mixw(l, 1, c), tmp, ALU.mult, ALU.add),
                     reads=rk + [('tf', ti)], writes=[('tf', ti)])
                S.op('dve', lambda e, tmp=tmp, u0=u0, c=c: e.scalar_tensor_tensor(tmp, u0, self.p_mixw(l, 0, c), tmp, ALU.mult, ALU.add),
                     reads=rk + [('tf', ti)], writes=[('tf', ti)])
                S.op('dve', lambda e, tmp=tmp, sbv=sbv: e.tensor_tensor(sbv, sbv, tmp, ALU.mult),
                     reads=[('tf', ti)] + self.akeys(8 + c, 9 + c, h0, n), writes=self.akeys(8 + c, 9 + c, h0, n))

    def conv_carry_out(self, S, grp, l):
        kind, GT, halves = grp['kind'], grp['GT'], grp['halves']
        if kind == 'p' and not grp['last']:
            S.op('dve', lambda e: e.tensor_copy(self.ucar[:, l, :, :], self.uT[:, :, GT:GT + 2]),
                 reads=[('uT', c, len(halves) - 1) for c in range(4)], writes=[('ucar', l)])

    def carry_in(self, S, grp, l):
        if grp['kind'] == 'p' and not grp['first']:
            S.op('dve', lambda e: e.tensor_copy(self.kT[:, 0:128], self.kcar[:, l, :]), reads=[('kcar', l)], writes=[('kT', 0)])
            S.op('dve', lambda e: e.tensor_copy(self.vaug[:, 0, :, 0:64], self.vcar[:, l, :, 0:64]), reads=[('vcar', l)], writes=[('vaug', 0)])

    def carry_out(self, S, grp, l):
        if grp['kind'] == 'p' and not grp['last']:
            n = grp['NTL']
            S.op('dve', lambda e: e.tensor_copy(self.kcar[:, l, :], self.kT[:, n * 128:(n + 1) * 128]), reads=[('kT', n)], writes=[('kcar', l)])
            S.op('dve', lambda e: e.tensor_copy(self.vcar[:, l, :, 0:64], self.vaug[:, n, :, 0:64]), reads=[('vaug', n)], writes=[('vcar', l)])

    def merge(self, S, grp, l):
        GT, halves = grp['GT'], grp['halves']
        allx = [('xT', t) for t in range(grp['NTL'])]
        xsrc = lambda kc, h0, n: self.xT[:, kc, h0:h0 + n]
        for dp in range(4):
            for i in range(3):
                slot = self.w_get(S)
                Wg = self.wv(slot, 0, 8, 256)
                Wp = self.wv(slot, 2048, 4, 256)
                for d2 in range(2):
                    dc = dp * 2 + d2
                    gts = {}

                    def ev_g(hi, h0, n, b, gts=gts, dc=dc, i=i):
                        tb = self.tbs()
                        gts[hi] = tb
                        S.op('act', lambda e: e.activation(self.tb[:, tb, 0:n], self.pb[b][:, 0:n], AF.Sigmoid,
                                                           bias=self.p_bg(l, i * 8 + dc), scale=1.0),
                             reads=[('pb', b), ('prm', l)], writes=[('tb', tb)])

                    def ev_p(hi, h0, n, b, gts=gts, dc=dc, d2=d2, i=i):
                        tb = gts[hi]
                        mk = ('macc', d2 * 2 + hi)
                        mac = self.macc[:, d2 * 2 + hi, 0:n]
                        g_ap = self.tb[:, tb, 0:n]
                        p_ap = self.pb[b][:, 0:n]
                        if i == 0:
                            S.op('dve', lambda e: e.tensor_tensor(mac, p_ap, g_ap, ALU.mult),
                                 reads=[('pb', b), ('tb', tb)], writes=[mk])
                        else:
                            ti = self.tfs()
                            tmp = self.tf[:, ti, 0:n]
                            S.op('dve', lambda e: e.tensor_tensor(tmp, p_ap, g_ap, ALU.mult),
                                 reads=[('pb', b), ('tb', tb)], writes=[('tf', ti)])
                            if i == 1:
                                S.op('dve', lambda e: e.tensor_tensor(mac, mac, tmp, ALU.add), reads=[mk, ('tf', ti)], writes=[mk])
                            else:
                                S.op('dve', lambda e: e.tensor_tensor(self.ar[:, 16 + dc, h0:h0 + n], mac, tmp, ALU.add),
                                     reads=[mk, ('tf', ti)], writes=self.akeys(16 + dc, 17 + dc, h0, n))
                    self.fm(S, slot, Wg, d2 * 128, xsrc, 8, halves, allx, ev_g)
                    bsrc = lambda kc, h0, n, i=i: self.ar[:, i * 4 + kc, h0:h0 + n]
                    self.fm(S, slot, Wp, d2 * 128, bsrc, 4, halves, self.akeys(i * 4, i * 4 + 4, 0, GT), ev_p)
                self.w_done(S, slot)

    def ln_bc_load(self, S, g_ap, b_ap, tag):
        S.dma('sp', self.lnbc[:, 0, :], g_ap.partition_broadcast(128), chan='lng', writes=['lnbc'])
        S.dma('sp', self.lnbc[:, 1, :], b_ap.partition_broadcast(128), chan='lnb', writes=['lnbc'])

    def ln_stats(self, S, t):
        xr = self.x_tok[:, t, :]
        st, k = self.layer_norm(S, xr, None, 1024, None, None, [('xtok', t)], None, None, recip=False)

        def affine():
            S.op('dve', lambda e: e.reciprocal(st[:, 15:16], st[:, 14:15]), reads=[k], writes=[k])
            S.op('dve', lambda e: e.scalar_tensor_tensor(xr, xr, st[:, 12:13], self.lnbc[:, 0, :], ALU.subtract, ALU.mult),
                 reads=[('xtok', t), 'lnbc', k], writes=[('xtok', t)])
            S.op('dve', lambda e: e.scalar_tensor_tensor(xr, xr, st[:, 15:16], self.lnbc[:, 1, :], ALU.mult, ALU.add),
                 reads=[('xtok', t), 'lnbc', k], writes=[('xtok', t)])
        return affine

    def wo_ln1(self, S, grp, l):
        NTL = grp['NTL']
        self.ln_bc_load(S, self.ln1_g[l], self.ln1_b[l], 'ln1')
        prev_aff = None
        for ch in range(2):
            slot = self.w_get(S)
            W = self.wv(slot, 0, 8, 512)
            for t in range(NTL):
                b = self.bank()
                pb = self.pb[b]

                def mm(e, t=t, pb=pb, W=W):
                    r = None
                    for kc in range(8):
                        r = e.matmul(pb[:, 0:512], lhsT=self.ar[:, 16 + kc, t * 128:(t + 1) * 128], rhs=W[:, kc, 0:512],
                                     start=(kc == 0), stop=(kc == 7))
                    return r
                S.op('pe', mm, reads=[('wr', slot)] + self.akeys(16, 24, t * 128, 128), writes=[('pb', b)])
                xs = self.x_tok[:, t, ch * 512:(ch + 1) * 512]
                S.op('dve', lambda e, xs=xs, pb=pb: e.scalar_tensor_tensor(xs, xs, ALPHA, pb[:, 0:512], ALU.mult, ALU.add),
                     reads=[('pb', b), ('xtok', t)], writes=[('xtok', t)])
                if ch == 1:
                    aff = self.ln_stats(S, t)
                    if prev_aff:
                        prev_aff()
                    prev_aff = aff
            self.w_done(S, slot)
        if prev_aff:
            prev_aff()
        if grp['kind'] == 'p':
            for t in range(NTL // 2):
                self.build_xT(S, t)
            self._xt_deferred = True
        else:
            for t in range(NTL):
                self.build_xT(S, t)

    def wd_load(self, S, l, hf):
        first = self._cur_gi == 0
        multi = self._ngroups > 1
        keys = self.akeys(0, 11, 0, 1024)
        skey = ('scrwd', l, hf)
        if first or not multi:
            S.dma('pool', self.ar[:, 0:11, :], self.w_down[l, hf * 1408:(hf + 1) * 1408, :].rearrange("(j p) c -> p j c", p=128),
                  chan='wd', writes=keys)
            if multi:
                for (u0, u1, ch) in ((0, 6, 'wdb0'), (6, 11, 'wdb1')):
                    S.dma('sp', self.scr_wd[l, hf, :, u0 * 1024:u1 * 1024], self.ar[:, u0:u1, :].rearrange("p u c -> p (u c)"),
                          chan=ch, reads=keys, writes=[skey])
        else:
            def fn(e):
                r = []
                for (u0, u1) in ((0, 6), (6, 11)):
                    r.append(e.dma_start(out=self.ar[:, u0:u1, :].rearrange("p u c -> p (u c)"), in_=self.scr_wd[l, hf, :, u0 * 1024:u1 * 1024]))
                return r
            S.dma('pool', None, None, chan='wd', reads=[skey], writes=keys, n=2, fn=fn)

    def ffn(self, S, grp, l, last_layer):
        kind, GT, halves, NTL = grp['kind'], grp['GT'], grp['halves'], grp['NTL']
        allx = [('xT', t) for t in range(NTL)]
        xsrc = lambda kc, h0, n: self.xT[:, kc, h0:h0 + n]
        state_out = (kind == 'p' and grp['last']) or kind == 's'
        stT = getattr(self, '_stT', None)
        self.ln_bc_load(S, self.ln2_g[l], self.ln2_b[l], 'ln2')
        return self.ffn_body(S, grp, l, last_layer, stT, state_out)

    def sample_ffn_prep(self, S, grp, l):
        kind = grp['kind']
        if kind == 's':
            stT = self.x_tok[:, 5:7, :].rearrange("p a b -> p (a b)")[:, 0:44 * 32].rearrange("p (c s r) -> p c s r", c=44, r=2)
            tis = [self.tfs(), self.tfs(), self.tfs(), self.tfs()]
            for q in range(11):
                ti = tis[q % 4]
                S.dma('sp', self.tf[0:32, ti, :], self.sffn[l][:, q * 512:(q + 1) * 512], chan='sffnin%d' % (q % 4), writes=[('tf', ti)])
                b = self.bank()
                pb = self.pb[b]

                def tr(e, ti=ti, pb=pb):
                    r = None
                    for c in range(4):
                        r = e.transpose(pb[:, c * 32:(c + 1) * 32], self.tf[0:32, ti, c * 128:(c + 1) * 128], self.identf[0:32, 0:32])
                    return r
                S.op('pe', tr, reads=[('tf', ti), 'identf'], writes=[('pb', b)])
                S.op('dve', lambda e, q=q, pb=pb: e.tensor_copy(stT[:, q * 4:(q + 1) * 4, :, :],
                                                               pb[:, 0:128].rearrange("p (c s r) -> p c s r", c=4, r=2)),
                     reads=[('pb', b)], writes=['stT', ('xtok', 5), ('xtok', 6)])
            self._stT = stT

    def ffn_body(self, S, grp, l, last_layer, stT, state_out):
        kind, GT, halves, NTL = grp['kind'], grp['GT'], grp['halves'], grp['NTL']
        allx = [('xT', t) for t in range(NTL)]
        xsrc = lambda kc, h0, n: self.xT[:, kc, h0:h0 + n]
        pend = []
        FDEPTH = 2
        holds = {}

        def up_chunks(slot, j0, nj, hf, hsel):
            n_ = nj * 128
            Wa = self.wv(slot, 0, 8, n_)
            Wg = self.wv(slot, 8 * n_, 8, n_)
            for jj in range(nj):
                j = j0 + jj
                jl = j - hf * 11
                hold = holds.setdefault(j, {})
                for which, W_ in (('a', Wa), ('g', Wg)):
                    col = j if which == 'a' else 22 + j

                    def ev(hi_l, h0, n, b, which=which, col=col, hold=hold, jl=jl):
                        hi = hsel[hi_l][0]
                        pbv = self.pb[b]
                        tfa, tk = self.ffs()
                        w0, w1, w2, bb = self.p_fcw(l, 0, col), self.p_fcw(l, 1, col), self.p_fcw(l, 2, col), self.p_fcb(l, col)
                        if kind == 'p':
                            c0 = tfa[:, 0:n]
                            P = pbv[:, 0:n]
                            hidx = grp['g'] * 2 + hi
                            pw = hidx % 2
                            cyw = self.upc[:, l, col, pw, :]
                            cyr = self.upc[:, l, col, 1 - pw, :]
                            S.op('act', lambda e: e.activation(c0, P, AF.Identity, bias=bb, scale=w2),
                                 reads=[('pb', b), ('prm', l)], writes=[tk])
                            if hidx < 2 * self.ngrp - 1:
                                def carry(e):
                                    e.activation(cyw[:, 0:2], P[:, n - 2:n], AF.Identity, bias=0.0, scale=w0)
                                    return e.activation(cyw[:, 2:3], P[:, n - 1:n], AF.Identity, bias=0.0, scale=w1)
                                S.op('act', carry, reads=[('pb', b), ('prm', l)], writes=[('upc', l, col, pw)])
                            S.op('dve', lambda e: e.scalar_tensor_tensor(c0[:, 1:n], P[:, 0:n - 1], w1, c0[:, 1:n], ALU.mult, ALU.add),
                                 reads=[('pb', b), tk], writes=[tk])
                            S.op('dve', lambda e: e.scalar_tensor_tensor(c0[:, 2:n], P[:, 0:n - 2], w0, c0[:, 2:n], ALU.mult, ALU.add),
                                 reads=[('pb', b), tk], writes=[tk])
                            if hidx > 0:
                                ck_ = ('upc', l, col, 1 - pw)
                                S.op('pool', lambda e: e.tensor_tensor(c0[:, 0:2], c0[:, 0:2], cyr[:, 0:2], ALU.add),
                                     reads=[ck_, tk], writes=[tk])
                                S.op('pool', lambda e: e.tensor_tensor(c0[:, 0:1], c0[:, 0:1], cyr[:, 2:3], ALU.add),
                                     reads=[ck_, tk], writes=[tk])
                        else:
                            c0 = tfa[:, 0:n].rearrange("p (s j) -> p s j", j=8)
                            P = pbv[:, 0:n].rearrange("p (s j) -> p s j", j=8)
                            cy = stT[:, col, :, :]
                            S.op('act', lambda e: e.activation(tfa[:, 0:n], pbv[:, 0:n], AF.Identity, bias=bb, scale=w2),
                                 reads=[('pb', b), ('prm', l)], writes=[tk])
                            S.op('dve', lambda e: e.scalar_tensor_tensor(c0[:, :, 1:8], P[:, :, 0:7], w1, c0[:, :, 1:8], ALU.mult, ALU.add),
                                 reads=[('pb', b), tk], writes=[tk])
                            S.op('dve', lambda e: e.scalar_tensor_tensor(c0[:, :, 2:8], P[:, :, 0:6], w0, c0[:, :, 2:8], ALU.mult, ALU.add),
                                 reads=[('pb', b), tk], writes=[tk])
                            S.op('dve', lambda e: e.scalar_tensor_tensor(c0[:, :, 0:1], cy[:, :, 1:2], w1, c0[:, :, 0:1], ALU.mult, ALU.add),
                                 reads=['stT', tk], writes=[tk])
                            S.op('dve', lambda e: e.scalar_tensor_tensor(c0[:, :, 0:2], cy[:, :, 0:2], w0, c0[:, :, 0:2], ALU.mult, ALU.add),
                                 reads=['stT', tk], writes=[tk])
                        full = tfa[:, 0:n]
                        if which == 'a':
                            hold[hi] = (tfa, tk)
                        else:
                            tfa_a, tk_a = hold[hi]

                            def fin():
                                S.op('act', lambda e: e.activation(full, full, AF.Silu), reads=[tk], writes=[tk])
                                S.op('pool', lambda e: e.tensor_tensor(self.ar[:, 11 + jl, h0:h0 + n], tfa_a[:, 0:n], full, ALU.mult),
                                     reads=[tk, tk_a], writes=self.akeys(11 + jl, 12 + jl, h0, n))
                            pend.append(fin)
                            while len(pend) > FDEPTH:
                                pend.pop(0)()
                    self.fm(S, slot, W_, jj * 128, xsrc, 8, [h_ for _, h_ in hsel], [k_ for _, (h0_, n_h) in hsel for k_ in self.xt_keys(h0_, n_h)], ev)

        def up_state(slot, j0, nj):
            n_ = nj * 128
            Wa = self.wv(slot, 0, 8, n_)
            Wg = self.wv(slot, 8 * n_, 8, n_)
            if state_out:
                t = NTL - 1
                for which, W_ in (('a', Wa), ('g', Wg)):
                    b = self.bank()
                    pb = self.pb[b]

                    def mm(e, W_=W_, pb=pb, n_=n_, t=t):
                        r = None
                        for kc in range(8):
                            r = e.matmul(pb[:, 0:n_], lhsT=self.xT[:, kc, t * 128:(t + 1) * 128], rhs=W_[:, kc, 0:n_],
                                         start=(kc == 0), stop=(kc == 7))
                        return r
                    S.op('pe', mm, reads=[('wr', slot), ('xT', t)], writes=[('pb', b)])
                    si = 0 if which == 'a' else 1
                    S.op('act', lambda e, pb=pb, si=si, n_=n_: e.copy((self.stg0 if si == 0 else self.stg1)[:, 0:n_], pb[:, 0:n_]), reads=[('pb', b)], writes=[('stg', si)])
                    cc = (j0 if which == 'a' else 22 + j0) * 128
                    if kind == 'p':
                        S.dma('sp', self.fcp[l, grp['b']][:, cc:cc + n_], (self.stg0 if si == 0 else self.stg1)[126:128, 0:n_], chan='st%d' % si,
                              reads=[('stg', si)], out_final=True)
                    else:
                        for r in range(2):
                            dst = self.fcs[l].rearrange("(s r) c -> s r c", r=2)[:, r, cc:cc + n_]
                            S.dma('sp', dst, (self.stg0 if si == 0 else self.stg1)[6 + r:128:8, 0:n_], chan='st%d%d' % (si, r), reads=[('stg', si)], out_final=True)

        allh = list(enumerate(halves))
        for hf in range(2):
            plist = list(self.up_pieces(hf))
            if hf == 0 and kind == 'p' and getattr(self, '_xt_deferred', False):
                (ja, na) = plist[0]
                sa = self.w_get(S)
                up_chunks(sa, ja, na, hf, allh[0:1])
                for t in range(NTL // 2, NTL):
                    self.build_xT(S, t)
                self._xt_deferred = False
                up_chunks(sa, ja, na, hf, allh[1:2])
                up_state(sa, ja, na)
                self.w_done(S, sa)
                plist = plist[1:]
            for (j0, nj) in plist:
                slot = self.w_get(S)
                up_chunks(slot, j0, nj, hf, allh)
                up_state(slot, j0, nj)
                self.w_done(S, slot)
            while pend:
                pend.pop(0)()
            prev_fin = None
            JSPLIT = 9
            NEARLY = min(NTL, 4)
            tbanks = {}

            def dmm(t, j0_, j1_):
                banks = tbanks[t]

                def mm(e):
                    r = None
                    for ch in range(2):
                        for jl in range(j0_, j1_):
                            r = e.matmul(self.pb[banks[ch]][:, 0:512], lhsT=self.ar[:, 11 + jl, t * 128:(t + 1) * 128],
                                         rhs=self.ar[:, jl, ch * 512:(ch + 1) * 512], start=(jl == 0), stop=(jl == 10))
                    return r
                S.op('pe', mm, reads=self.akeys(11 + j0_, 11 + j1_, t * 128, 128) + self.akeys(j0_, j1_, 0, 1024),
                     writes=[('pb', b) for b in banks])
            for t in range(NEARLY):
                tbanks[t] = [self.bank(), self.bank()]
                dmm(t, 0, JSPLIT)
            for t in range(NTL):
                if t < NEARLY:
                    dmm(t, JSPLIT, 11)
                else:
                    tbanks[t] = [self.bank(), self.bank()]
                    dmm(t, 0, 11)
                banks = tbanks[t]
                for ch in range(2):
                    xs = self.x_tok[:, t, ch * 512:(ch + 1) * 512]
                    pbv = self.pb[banks[ch]][:, 0:512]
                    if hf == 0:
                        S.op('dve', lambda e, xs=xs, pbv=pbv: e.scalar_tensor_tensor(xs, xs, ALPHA, pbv, ALU.mult, ALU.add),
                             reads=[('pb', banks[ch]), ('xtok', t)], writes=[('xtok', t)])
                    else:
                        S.op('dve', lambda e, xs=xs, pbv=pbv: e.tensor_tensor(xs, xs, pbv, ALU.add),
                             reads=[('pb', banks[ch]), ('xtok', t)], writes=[('xtok', t)])
                if hf == 1:
                    aff = self.ln_stats(S, t)

                    def fin(t=t, aff=aff):
                        aff()
                        if last_layer:
                            if kind == 'p':
                                r0 = grp['g'] * 1024 + t * 128
                                S.dma('sp', self.y_p[grp['b'], r0:r0 + 128, :], self.x_tok[:, t, :], chan='yo%d' % t,
                                      reads=[('xtok', t)], out_final=True)
                                ng = self._groups[self._cur_gi + 1] if self._cur_gi + 1 < len(self._groups) else None
                                if ng is not None and ng['kind'] == 'p':
                                    r0n = ng['g'] * 1024 + t * 128
                                    S.dma('sp', self.x_tok[:, t, :], self.xp[ng['b'], r0n:r0n + 128, :], chan='xpre%d' % t, writes=[('xtok', t)])
                                    ng['preloaded'] = True
                                elif ng is not None and ng['kind'] == 's' and t == 0:
                                    S.dma('sp', self.x_tok[:, 0, :], self.xs, chan='xpre0', writes=[('xtok', 0)])
                                    ng['preloaded'] = True
                            else:
                                S.dma('sp', self.y_s, self.x_tok[:, 0, :], chan='yo0', reads=[('xtok', 0)], out_final=True)
                    if prev_fin:
                        prev_fin()
                    prev_fin = fin
            if hf == 1 and prev_fin:
                prev_fin()
            if hf == 0:
                self.wd_load(S, l, 1)
            elif not last_layer:
                for t in range(NTL):
                    self.build_xT(S, t)

    def proj_attn(self, S, grp, l):
        if grp['kind'] == 's':
            self.attention_sample_prep(S, grp, l)
        gp = self.proj(S, grp, l)
        for v in gp:
            if v == 'KV_DONE':
                break
        if grp['kind'] == 's':
            self.attention_sample_prep_tr(S, grp, l)
        ga = self.attention_prompt(S, grp, l) if grp['kind'] == 'p' else self.attention_sample(S, grp, l)
        alive_a = alive_p = True
        while alive_a or alive_p:
            if alive_p:
                try:
                    next(gp)
                except StopIteration:
                    alive_p = False
            if alive_a:
                try:
                    next(ga)
                except StopIteration:
                    alive_a = False

    def build(self):
        nc = self.nc
        self.declare()
        groups = []
        for b in range(self.nseq):
            for g in range(self.ngrp):
                groups.append(dict(kind='p', b=b, g=g, GT=1024, NTL=8, halves=[(0, 512), (512, 512)],
                                   first=(g == 0), last=(g == self.ngrp - 1)))
        if self.with_sample:
            groups.append(dict(kind='s', b=0, g=0, GT=128, NTL=1, halves=[(0, 128)], first=True, last=True))
        with ExitStack() as es:
            S = Sched(nc, es)
            self.alloc(S)
            self.init_consts(S)
            self.w_init(S, self.make_pieces(groups))
            import os
            stage = int(os.environ.get("KSTAGE", "999"))
            cnt = [0]

            def go():
                cnt[0] += 1
                return cnt[0] <= stage
            try:
                self._ngroups = len(groups)
                self._groups = groups
                for grp in groups:
                    self._cur_gi = groups.index(grp)
                    if not go(): raise StopIteration
                    self.load_x(S, grp)
                    for t in range(grp['NTL']):
                        self.build_xT(S, t)
                    for l in range(L):
                        gi = groups.index(grp)
                        nxt = (grp, l + 1) if l + 1 < L else ((groups[gi + 1], 0) if gi + 1 < len(groups) else None)
                        steps = [
                            lambda: ((self.gmlp_prep(S, grp, l) if (gi == 0 and l == 0) else None), self.carry_in(S, grp, l)),
                            lambda: self.proj_attn(S, grp, l),
                            lambda: self.carry_out(S, grp, l),
                            lambda: (self.sample_ffn_prep(S, grp, l), self.merge(S, grp, l), self.wd_load(S, l, 0)),
                            lambda: self.wo_ln1(S, grp, l),
                            lambda: ((self.gmlp_prep(S, nxt[0], nxt[1]) if nxt else None), self.ffn(S, grp, l, l == L - 1)),
                        ]
                        for st_ in steps:
                            if not go(): raise StopIteration
                            st_()
            except StopIteration:
                pass
            print("stages emitted:", cnt[0], "tasks:", S.ntask)
            S.finish()
            self.ntask = S.ntask
        return nc


def _tables():
    slopes = 2.0 ** (-(np.arange(8) + 1.0))
    s = np.arange(128)[:, None, None]
    t = np.arange(128)[None, None, :]
    sl = slopes[None, :, None]
    NEG = -30000.0
    B0 = np.where(s >= t, -sl * (t + 128 - s), NEG) + 0.0 * sl
    B1 = np.where(s <= t, -sl * (t - s), NEG) + 0.0 * sl
    B0 = np.broadcast_to(B0, (128, 8, 128)).astype(np.float32)
    B1 = np.broadcast_to(B1, (128, 8, 128)).astype(np.float32)
    B0s = np.zeros((128, 2, 16, 4, 8), np.float32)
    for gI in range(2):
        for hh in range(4):
            B0s[:, gI, :, hh, :] = B0[:, gI * 4 + hh, None, 0:8]
    B1bd = np.full((16, 8, 8, 16, 8), NEG, np.float32)
    m_bd = np.zeros((16, 8, 16, 8), np.float32)
    for q in range(16):
        B1bd[q, :, :, q, :] = B1[0:8, :, 0:8]
        m_bd[q, :, q, :] = (np.arange(8)[:, None] <= np.arange(8)[None, :])
    tabs = np.stack([B0.reshape(128, 1024), B1.reshape(128, 1024), B0s.reshape(128, 1024), B1bd.reshape(128, 1024)])
    m_tril = (np.arange(128)[:, None] <= np.arange(128)[None, :]).astype(np.float32)
    masks = np.stack([m_tril, m_bd.reshape(128, 128)])
    return np.ascontiguousarray(tabs, dtype=np.float32), np.ascontiguousarray(masks, dtype=np.float32)


_QPERM = np.concatenate([np.concatenate([np.arange(c * 64, c * 64 + 64), np.arange((4 + c) * 64, (4 + c) * 64 + 64)]) for c in range(4)])

_NC_CACHE = {}


def _get_nc(nseq, seqlen, with_sample=True):
    key = (nseq, seqlen, with_sample)
    if key not in _NC_CACHE:
        _NC_CACHE[key] = Builder(nseq, seqlen, with_sample).build()
    return _NC_CACHE[key]


def _shared_maps(inp):
    f = lambda a: np.ascontiguousarray(np.asarray(a), dtype=np.float32)
    w_in = f(inp["w_in"])
    w_in_p = w_in.copy()
    w_in_p[:, :, 0:512] = w_in[:, :, _QPERM]
    tabs, masks = _tables()
    return {
        "w_in": w_in_p, "w_gate": f(inp["w_gate"]), "b_gate": f(inp["b_gate"]).reshape(L, 24, 128),
        "gln_g": f(inp["gmlp_ln_g"]), "gln_b": f(inp["gmlp_ln_b"]), "gws": f(inp["gmlp_ws"]),
        "gbs": f(inp["gmlp_bs"]).reshape(L, 512), "mixw": f(inp["mixconv_w"]).reshape(L, 12, 128),
        "sinks": f(inp["attn_sinks"]), "p_attn": f(inp["p_attn"]), "p_gmlp": f(inp["p_gmlp"]), "p_conv": f(inp["p_conv"]),
        "w_o": f(inp["w_o"]), "ln1_g": f(inp["ln1_g"]), "ln1_b": f(inp["ln1_b"]), "w_up": f(inp["w_up"]),
        "fcw": f(inp["ffn_conv_w"]).reshape(L, 132, 128), "fcb": f(inp["ffn_conv_b"]).reshape(L, 44, 128),
        "w_down": f(inp["w_down"]), "ln2_g": f(inp["ln2_g"]), "ln2_b": f(inp["ln2_b"]),
        "tabs": tabs, "masks": masks,
    }


def run_cores(inp, ncores, nseq, seqlen):
    f = lambda a: np.ascontiguousarray(np.asarray(a), dtype=np.float32)
    shared = _shared_maps(inp)
    xp, xs = f(inp["x_prompt"]), f(inp["x_sample"])
    ck, cv = f(inp["cache_k_win"]), f(inp["cache_v_win"])
    sm, sf = f(inp["state_mixconv"]), f(inp["state_ffnconv"])
    in_maps = []
    for i in range(ncores):
        m = dict(shared)
        m["xp"] = np.ascontiguousarray(xp[i * nseq:(i + 1) * nseq])
        m["xs"] = np.ascontiguousarray(xs[i * 16:(i + 1) * 16].reshape(128, D))
        m["ck"] = np.ascontiguousarray(ck[:, i * 16:(i + 1) * 16].reshape(L, 16, 128, 128))
        m["cv"] = np.ascontiguousarray(cv[:, i * 16:(i + 1) * 16].reshape(L, 16, 128, 128))
        m["smix"] = np.ascontiguousarray(sm[:, i * 16:(i + 1) * 16].reshape(L, 32, 512))
        m["sffn"] = np.ascontiguousarray(sf[:, i * 16:(i + 1) * 16].reshape(L, 32, 2 * DFF))
        in_maps.append(m)
    nc = _get_nc(nseq, seqlen)
    res = run_bass_kernel_spmd(nc, in_maps, core_ids=list(range(ncores)))
    R = res.results
    cat = lambda name, ax: np.concatenate([np.asarray(r[name]) for r in R], axis=ax)
    nb = ncores * nseq
    ns = ncores * 16
    outs = (
        cat("y_p", 0).reshape(nb, seqlen, D),
        cat("y_s", 0).reshape(ns, 8, D),
        cat("kwp", 1).reshape(L, nb, 128, 2, 64),
        cat("vwp", 1).reshape(L, nb, 128, 2, 64),
        cat("mcp", 1).reshape(L, nb, 2, 512),
        cat("fcp", 1).reshape(L, nb, 2, 2 * DFF),
        cat("kws", 1).reshape(L, ns, 128, 2, 64),
        cat("vws", 1).reshape(L, ns, 128, 2, 64),
        cat("mcs", 1).reshape(L, ns, 2, 512),
        cat("fcs", 1).reshape(L, ns, 2, 2 * DFF),
        cat("gvs", 1).reshape(L, ns, 8, 512),
    )
    return tuple(np.ascontiguousarray(o, dtype=np.float32) for o in outs)


def kernel(**inputs):
    return run_cores(inputs, 8, 2, 2048)
```

```python
import numpy as np
from contextlib import ExitStack
import concourse.bass as bass
import concourse.mybir as mybir
from concourse.bass_utils import run_bass_kernel_spmd

F32 = mybir.dt.float32
BF16 = mybir.dt.bfloat16
AF = mybir.ActivationFunctionType
ALU = mybir.AluOpType

ENGS = ['pe', 'act', 'dve', 'pool', 'sp']


class Sched:
    def __init__(self, nc, es):
        self.nc = nc
        self.es = es
        self.q = {e: [] for e in ENGS}
        self.esem = {}
        for e in ENGS:
            if e != 'sp':
                self.esem[e] = es.enter_context(nc.semaphore("prog_" + e))
        self.ecnt = {e: 0 for e in ENGS}
        self.seen = {e: {} for e in ENGS}
        self.lastw = {}
        self.readers = {}
        self.chans = {}
        self.out_chans = set()
        self.semname = {}
        self.ntask = 0

    def sb(self, name, shape, dtype):
        return self.es.enter_context(self.nc.sbuf_tensor(name, shape, dtype))

    def ps(self, name, shape, dtype):
        return self.es.enter_context(self.nc.psum_tensor(name, shape, dtype))

    def _deps(self, eng, reads, writes):
        deps = []
        for k in reads:
            t = self.lastw.get(k)
            if t is not None:
                deps.append(t)
            if isinstance(k, tuple) and k[0] == 'pb':
                r = self.readers.get(k)
                if r:
                    deps.extend(v for s_, v in r.items() if s_ != eng)
        for k in writes:
            t = self.lastw.get(k)
            if t is not None:
                deps.append(t)
            r = self.readers.get(k)
            if r:
                deps.extend(r.values())
        waits = {}
        seen = self.seen[eng]
        for (sid, sem, v) in deps:
            if eng == 'pe' and sid == 'pe':
                continue
            if seen.get(sid, 0) >= v:
                continue
            if sid not in waits or waits[sid][1] < v:
                waits[sid] = (sem, v)
        for sid, (sem, v) in waits.items():
            seen[sid] = v
        return list(waits.values())

    def _commit(self, tok, reads, writes):
        sid = tok[0]
        for k in reads:
            r = self.readers.setdefault(k, {})
            r[sid] = tok
        for k in writes:
            self.lastw[k] = tok
            self.readers[k] = {}

    def op(self, eng, fn, reads=(), writes=()):
        waits = self._deps(eng, reads, writes)
        self.ecnt[eng] += 1
        tok = (eng, self.esem[eng], self.ecnt[eng])
        self.q[eng].append((waits, fn, tok, 1))
        self._commit(tok, reads, writes)
        self.ntask += 1
        return tok

    def dma(self, eng, out, in_, chan, reads=(), writes=(), out_final=False, n=1, fn=None, **kw):
        if chan not in self.chans:
            self.chans[chan] = [self.es.enter_context(self.nc.semaphore("c_" + chan)), 0]
        c = self.chans[chan]
        waits = self._deps(eng, reads, writes)
        c[1] += 16 * n
        tok = ('c_' + chan, c[0], c[1])
        if fn is None:
            def fn(e, out=out, in_=in_, kw=kw):
                return [e.dma_start(out=out, in_=in_, **kw)]
        self.q[eng].append((waits, fn, tok, 16))
        self._commit(tok, reads, writes)
        if out_final:
            self.out_chans.add(chan)
        self.ntask += 1
        return tok

    def finish(self):
        nc = self.nc
        engmap = {'pe': 'tensor', 'act': 'scalar', 'dve': 'vector', 'pool': 'gpsimd', 'sp': 'sync'}
        finals = [(self.chans[c][0], self.chans[c][1]) for c in sorted(self.chans)]

        def run(e, name):
            for waits, fn, tok, inc in self.q[name]:
                for (sem, v) in waits:
                    e.wait_ge(sem, v)
                r = fn(e)
                if inc == 16:
                    for ins in r:
                        ins.then_inc(tok[1], 16)
                else:
                    r.then_inc(tok[1], 1)
            if name == 'sp':
                for (sem, v) in finals:
                    e.wait_ge(sem, v)

        with nc.Block() as block:
            for name in ENGS:
                getattr(block, engmap[name])(lambda e, name=name: run(e, name))


D = 1024
KC = 8
DFF = 2816
NPAIR = 22
NUP = 44
L = 2
ALPHA = float((2.0 * L) ** 0.25)
EPS = 1e-5
C_Q, C_K, C_V, C_GU, C_GV, C_SB, C_SC, C_SH = 0, 512, 640, 768, 1280, 1792, 2304, 2816
NSLOT = 3
SLOT_EL = 4096


class Builder:
    def __init__(self, nseq=2, seqlen=2048, with_sample=True):
        self.nseq = nseq
        self.seqlen = seqlen
        self.ngrp = seqlen // 1024
        self.with_sample = with_sample
        self.nc = bass.Bass("TRN2", target_bir_lowering=False)
        self.nb = 0
        self.ntf = 0
        self.ntb = 0
        self.nln = 0

    def declare(self):
        nc = self.nc

        def din(name, shape):
            return nc.dram_tensor(name, list(shape), F32, kind="ExternalInput").ap()

        def dout(name, shape):
            return nc.dram_tensor(name, list(shape), F32, kind="ExternalOutput").ap()

        ns, sl = self.nseq, self.seqlen
        self.xp = din("xp", [ns, sl, D])
        self.xs = din("xs", [128, D])
        self.ck = din("ck", [L, 16, 128, 128])
        self.cv = din("cv", [L, 16, 128, 128])
        self.smix = din("smix", [L, 32, 512])
        self.sffn = din("sffn", [L, 32, 2 * DFF])
        self.w_in = din("w_in", [L, D, 3328])
        self.w_gate = din("w_gate", [L, D, 3072])
        self.b_gate = din("b_gate", [L, 24, 128])
        self.gln_g = din("gln_g", [L, 512])
        self.gln_b = din("gln_b", [L, 512])
        self.gws = din("gws", [L, 4, 128, 128])
        self.gbs = din("gbs", [L, 512])
        self.mixw = din("mixw", [L, 12, 128])
        self.sinks = din("sinks", [L, 8])
        self.p_br = [din("p_attn", [L, 512, D]), din("p_gmlp", [L, 512, D]), din("p_conv", [L, 512, D])]
        self.w_o = din("w_o", [L, D, D])
        self.ln1_g = din("ln1_g", [L, D])
        self.ln1_b = din("ln1_b", [L, D])
        self.w_up = din("w_up", [L, D, 2 * DFF])
        self.fcw = din("fcw", [L, 132, 128])
        self.fcb = din("fcb", [L, 44, 128])
        self.w_down = din("w_down", [L, DFF, D])
        self.ln2_g = din("ln2_g", [L, D])
        self.ln2_b = din("ln2_b", [L, D])
        self.scr = nc.dram_tensor("wscr", [L, 33, 128, SLOT_EL], BF16, kind="Internal").ap()
        self.scr_wd = nc.dram_tensor("wdscr", [L, 2, 128, 11 * 1024], BF16, kind="Internal").ap()
        self.tabs = din("tabs", [4, 128, 1024])
        self.masks = din("masks", [2, 128, 128])
        self.y_p = dout("y_p", [ns, sl, D])
        self.y_s = dout("y_s", [128, D])
        self.kwp = dout("kwp", [L, ns, 128, 128])
        self.vwp = dout("vwp", [L, ns, 128, 128])
        self.mcp = dout("mcp", [L, ns, 2, 512])
        self.fcp = dout("fcp", [L, ns, 2, 2 * DFF])
        self.kws = dout("kws", [L, 16, 128, 128])
        self.vws = dout("vws", [L, 16, 128, 128])
        self.mcs = dout("mcs", [L, 32, 512])
        self.fcs = dout("fcs", [L, 32, 2 * DFF])
        self.gvs = dout("gvs", [L, 128, 512])

    def alloc(self, S):
        sb = S.sb
        self.x_tok = sb("x_tok", [128, 8, D], F32)
        self.xT = sb("xT", [128, 8, 1024], BF16)
        self.ar = sb("ar", [128, 24, 1024], BF16)
        self.uT = sb("uT", [128, 4, 1026], BF16)
        self.kT = sb("kT", [128, 1152], BF16)
        self.vaug = sb("vaug", [128, 9, 2, 65], BF16)
        self.wr = [sb("wr%d" % i, [128, SLOT_EL], BF16) for i in range(NSLOT)]
        self.tf = sb("tf", [128, 8, 512], F32)
        self.tb = sb("tb", [128, 10, 512], BF16)
        self.macc = sb("macc", [128, 4, 512], F32)
        self.lnbc = sb("lnbc", [128, 2, 1024], F32)
        self.glnbc = sb("glnbc", [128, 2, 512], F32)
        self.E = sb("E", [128, 4, 1024], BF16)
        self.wst = sb("wst", [128, 4, 128], BF16)
        self.gbias = sb("gbias", [128, 4, 128], F32)
        self.identf = sb("identf", [128, 128], F32)
        self.identb = sb("identb", [128, 128], BF16)
        self.maskt = sb("maskt", [128, 2, 128], F32)
        self.prm = sb("prm", [128, L, 212], F32)
        self.esink = sb("esink", [128, L, 8], F32)
        self.lnst = sb("lnst", [128, 4, 16], F32)
        self.att_s = sb("att_s", [128, 4, 16], F32)
        self.kcar = sb("kcar", [128, L, 128], BF16)
        self.vcar = sb("vcar", [128, L, 2, 65], BF16)
        self.ucar = sb("ucar", [128, L, 4, 2], BF16)
        self.upc = sb("upc", [128, L, NUP, 2, 3], F32)
        self.stg0 = sb("stg0", [128, 256], F32)
        self.stg1 = sb("stg1", [128, 512], F32)
        self.kcT = sb("kcT", [128, 16, 128], BF16)
        self.vaugc = sb("vaugc", [128, 16, 2, 65], BF16)
        self.pb = [S.ps("pb%d" % i, [128, 512], F32) for i in range(8)]

    def bank(self):
        b = self.nb % 8
        self.nb += 1
        return b

    def tfs(self):
        i = self.ntf % 8
        self.ntf += 1
        return i

    def ffs(self):
        i = getattr(self, '_nff', 0) % 12
        self._nff = getattr(self, '_nff', 0) + 1
        if i < 8:
            return self.tf[:, i, :], ('tf', i)
        return self.macc[:, i - 8, :], ('macc', i - 8)

    def tbs(self):
        i = self.ntb % 10
        self.ntb += 1
        return i

    @staticmethod
    def akeys(u0, u1, c0, n):
        return [('ar', u, cb) for u in range(u0, u1) for cb in range(c0 // 128, (c0 + n + 127) // 128)]

    def make_pieces(self, groups):
        pcs = []
        for gi_, grp in enumerate(groups):
            for l in range(L):
                base_ = len(pcs)
                def win(c0, n, l=l):
                    return [(0, (8, n), self.w_in[l, :, c0:c0 + n].rearrange("(k p) c -> p k c", p=128))]
                pcs.append(win(C_Q, 512))
                pcs.append(win(C_K, 256))
                pcs.append(win(C_GU, 512))
                pcs.append(win(C_GV, 512))
                pcs.append(win(C_SB, 512))
                pcs.append(win(C_SC, 512))
                pcs.append(win(C_SH, 512))
                for dp in range(4):
                    for i in range(3):
                        c0 = i * 1024 + dp * 256
                        pcs.append([
                            (0, (8, 256), self.w_gate[l, :, c0:c0 + 256].rearrange("(k p) c -> p k c", p=128)),
                            (2048, (4, 256), self.p_br[i][l, :, dp * 256:(dp + 1) * 256].rearrange("(k p) c -> p k c", p=128)),
                        ])
                for ch in range(2):
                    pcs.append([(0, (8, 512), self.w_o[l, :, ch * 512:(ch + 1) * 512].rearrange("(k p) c -> p k c", p=128))])
                for hf in range(2):
                    for (j0, nj) in self.up_pieces(hf):
                        n = nj * 128
                        pcs.append([
                            (0, (8, n), self.w_up[l, :, j0 * 128:j0 * 128 + n].rearrange("(k p) c -> p k c", p=128)),
                            (8 * n, (8, n), self.w_up[l, :, DFF + j0 * 128:DFF + j0 * 128 + n].rearrange("(k p) c -> p k c", p=128)),
                        ])
                for i_ in range(base_, len(pcs)):
                    parts_ = pcs[i_]
                    nel_ = max(off + k * n for (off, (k, n), _) in parts_)
                    pcs[i_] = dict(parts=parts_, l=l, pidx=i_ - base_, nel=nel_, first=(gi_ == 0), multi=(len(groups) > 1))
                assert len(pcs) - base_ == 33
        return pcs

    @staticmethod
    def up_pieces(hf):
        j = hf * 11
        out = []
        for nj in (2, 2, 2, 2, 2, 1):
            out.append((j, nj))
            j += nj
        return out

    def w_init(self, S, pieces):
        self.pieces = pieces
        self.p_loaded = 0
        self.p_next = 0
        self.slot_free = [True] * NSLOT
        self._w_pump(S)

    def _w_pump(self, S):
        while self.p_loaded < len(self.pieces):
            slot = self.p_loaded % NSLOT
            if not self.slot_free[slot]:
                break
            pc = self.pieces[self.p_loaded]
            parts = pc['parts']
            wr = self.wr[slot]
            skey = ('scr', pc['l'], pc['pidx'])
            sap = self.scr[pc['l'], pc['pidx'], :, 0:pc['nel']]
            if pc['first'] or not pc['multi']:
                def fn(e, parts=parts, wr=wr):
                    r = []
                    for (off, (k, n), src) in parts:
                        dst = wr[:, off:off + k * n].rearrange("p (k c) -> p k c", k=k)
                        r.append(e.dma_start(out=dst, in_=src))
                    return r
                S.dma('pool', None, None, chan='wr%d' % slot, writes=[('wr', slot)], n=len(parts), fn=fn)
                if pc['multi']:
                    S.dma('sp', sap, wr[:, 0:pc['nel']], chan='wb%d' % slot, reads=[('wr', slot)], writes=[skey])
            else:
                S.dma('pool', wr[:, 0:pc['nel']], sap, chan='wr%d' % slot, reads=[skey], writes=[('wr', slot)])
            self.slot_free[slot] = False
            self.p_loaded += 1

    def w_get(self, S):
        assert self.p_next < self.p_loaded, "weight piece not loaded (ring too small)"
        slot = self.p_next % NSLOT
        self.p_next += 1
        return slot

    def w_done(self, S, slot):
        import os
        if 'nopump' in os.environ.get('KDBG', ''):
            return
        self.slot_free[slot] = True
        self._w_pump(S)

    def wv(self, slot, off, k, n):
        return self.wr[slot][:, off:off + k * n].rearrange("p (k c) -> p k c", k=k)

    def _patch(self, S, keys, tok):
        for k in keys:
            S.lastw[k] = tok

    def init_consts(self, S):
        S.op('pool', lambda e: e.memset(self.identf[:], 0.0), writes=['identf'])
        S.op('pool', lambda e: e.affine_select(out=self.identf[:], in_=self.identf[:], pattern=[[-1, 128]],
                                               compare_op=ALU.not_equal, fill=1.0, base=0, channel_multiplier=1),
             reads=['identf'], writes=['identf'])
        S.op('dve', lambda e: e.tensor_copy(self.identb[:], self.identf[:]), reads=['identf'], writes=['identb'])
        S.op('dve', lambda e: e.memset(self.vaug[:, :, :, 64:65], 1.0), writes=['vaug_ones'])
        S.op('dve', lambda e: e.memset(self.vaugc[:, :, :, 64:65], 1.0), writes=['vaugc_ones'])
        S.op('dve', lambda e: e.memset(self.vcar[:, :, :, 64:65], 1.0), writes=['vcar_ones'])
        keys = ['maskt', 'esink']
        tok = S.dma('sp', self.maskt[:], self.masks.rearrange("m s t -> s m t"), chan='init', writes=['maskt'])
        tmps = []
        for i in range(4):
            s0 = self.tfs(); s1 = self.tfs()
            assert s1 == s0 + 1
            tmp = self.tf[:, s0:s0 + 2, :].rearrange("p a b -> p (a b)")
            tok = S.dma('sp', tmp, self.tabs[i], chan='init', writes=[('tf', s0), ('tf', s1)])
            keys += [('tf', s0), ('tf', s1)]
            tmps.append((tmp, s0, s1))
        tok = S.dma('sp', self.esink[:].rearrange("p l h -> p (l h)"),
                    self.sinks.rearrange("l h -> (l h)").partition_broadcast(128), chan='init', writes=['esink'])
        self._patch(S, keys, tok)
        for i in range(4):
            tmp, s0, s1 = tmps[i]
            S.op('act', lambda e, i=i, tmp=tmp: e.activation(self.E[:, i, :], tmp, AF.Exp),
                 reads=[('tf', s0), ('tf', s1)], writes=[('E', i)])
        S.op('act', lambda e: e.activation(self.esink[:], self.esink[:], AF.Exp), reads=['esink'], writes=['esink'])
        sts = []
        keys = []
        for l in range(L):
            s0 = self.tfs()
            st = self.tf[:, s0, :]
            k = [('tf', s0)]
            S.dma('sp', st[0:24, 0:128], self.b_gate[l], chan='init2', writes=k)
            S.dma('sp', st[24:36, 0:128], self.mixw[l], chan='init2', writes=k)
            S.dma('sp', st[36:80, 0:128], self.fcb[l], chan='init2', writes=k)
            S.dma('sp', st[0:128, 128:256], self.fcw[l, 0:128, :], chan='init2', writes=k)
            tok = S.dma('sp', st[0:4, 256:384], self.fcw[l, 128:132, :], chan='init2', writes=k)
            keys += k
            sts.append((st, s0))
        self._patch(S, keys, tok)
        for l in range(L):
            st, s0 = sts[l]
            b = self.bank()
            pb = self.pb[b]

            def tr(e, st=st, pb=pb):
                e.transpose(pb[:, 0:80], st[0:80, 0:128], self.identf[0:80, 0:80])
                e.transpose(pb[:, 80:208], st[0:128, 128:256], self.identf[:])
                return e.transpose(pb[:, 208:212], st[0:4, 256:384], self.identf[0:4, 0:4])
            S.op('pe', tr, reads=[('tf', s0), 'identf'], writes=[('pb', b)])
            S.op('dve', lambda e, l=l, pb=pb: e.tensor_copy(self.prm[:, l, :], pb[:, 0:212]),
                 reads=[('pb', b)], writes=[('prm', l)])

    def p_bg(self, l, j):
        return self.prm[:, l, j:j + 1]

    def p_mixw(self, l, k, c):
        return self.prm[:, l, 24 + k * 4 + c:24 + k * 4 + c + 1]

    def p_fcb(self, l, col):
        return self.prm[:, l, 36 + col:36 + col + 1]

    def p_fcw(self, l, k, col):
        return self.prm[:, l, 80 + k * 44 + col:80 + k * 44 + col + 1]

    def fm(self, S, slot, wview, c0, src_fn, nk, halves, rkeys, evac, mcols=128, prow=None):
        banks = [self.bank() for _ in halves]

        import os
        seq = 'seq' in os.environ.get('KDBG', '')

        def mm(e):
            r = None
            if seq:
                for hi, (h0, n) in enumerate(halves):
                    for kc in range(nk):
                        r = e.matmul(self.pb[banks[hi]][0:mcols, 0:n], lhsT=wview[:, kc, c0:c0 + mcols],
                                     rhs=src_fn(kc, h0, n), start=(kc == 0), stop=(kc == nk - 1))
                return r
            for kc in range(nk):
                for hi, (h0, n) in enumerate(halves):
                    r = e.matmul(self.pb[banks[hi]][0:mcols, 0:n], lhsT=wview[:, kc, c0:c0 + mcols],
                                 rhs=src_fn(kc, h0, n), start=(kc == 0), stop=(kc == nk - 1))
            return r
        S.op('pe', mm, reads=[('wr', slot)] + rkeys, writes=[('pb', b) for b in banks])
        import os
        if 'noevac' in os.environ.get('KDBG', ''):
            return
        for hi, (h0, n) in enumerate(halves):
            evac(hi, h0, n, banks[hi])

    def sub(self):
        import os
        self._subc = getattr(self, '_subc', 0) + 1
        if self._subc > int(os.environ.get("KSUB", "999")):
            raise StopIteration

    def xt_keys(self, h0, n):
        return [('xT', t) for t in range(h0 // 128, (h0 + n) // 128)]

    def load_x(self, S, grp):
        if grp.get('preloaded'):
            return
        if grp['kind'] == 'p':
            b, g = grp['b'], grp['g']
            tok = None
            for t in range(8):
                r0 = g * 1024 + t * 128
                tok = S.dma('sp', self.x_tok[:, t, :], self.xp[b, r0:r0 + 128, :], chan='xin', writes=[('xtok', t)])
            self._patch(S, [('xtok', t) for t in range(8)], tok)
        else:
            S.dma('sp', self.x_tok[:, 0, :], self.xs, chan='xin', writes=[('xtok', 0)])

    def build_xT(self, S, t):
        for kh in range(2):
            b = self.bank()
            pb = self.pb[b]

            def tr(e, kh=kh, pb=pb):
                r = None
                for k in range(4):
                    kc = kh * 4 + k
                    r = e.transpose(pb[:, k * 128:(k + 1) * 128], self.x_tok[:, t, kc * 128:(kc + 1) * 128], self.identf[:])
                return r
            S.op('pe', tr, reads=[('xtok', t), 'identf'], writes=[('pb', b)])
            S.op('act', lambda e, kh=kh, pb=pb: e.copy(self.xT[:, kh * 4:(kh + 1) * 4, t * 128:(t + 1) * 128],
                                                      pb[:].rearrange("p (k t) -> p k t", k=4)),
                 reads=[('pb', b)], writes=[('xT', t)])

    def layer_norm(self, S, src, dst, width, gam, bet, rkeys, wkeys, bckeys, src_keys_extra=(), recip=True):
        i = self.nln % 4
        self.nln += 1
        st = self.lnst[:, i, :]
        nchunk = width // 512
        tfi = None

        def bn(e):
            r = None
            for c in range(nchunk):
                r = e.bn_stats(st[:, c * 6:(c + 1) * 6], src[:, c * 512:(c + 1) * 512])
            return r
        k = ('lnst', i)
        S.op('dve', bn, reads=rkeys, writes=[k])
        S.op('dve', lambda e: e.bn_aggr(st[:, 12:14], st[:, 0:6 * nchunk]), reads=[k], writes=[k])
        S.op('act', lambda e: e.activation(st[:, 14:15], st[:, 13:14], AF.Sqrt, bias=EPS, scale=1.0), reads=[k], writes=[k])
        if recip:
            S.op('dve', lambda e: e.reciprocal(st[:, 15:16], st[:, 14:15]), reads=[k], writes=[k])
        return st, k

    def proj(self, S, grp, l):
        GT, halves, NTL = grp['GT'], grp['halves'], grp['NTL']
        kind = grp['kind']
        xsrc = lambda kc, h0, n: self.xT[:, kc, h0:h0 + n]
        allx = [('xT', t) for t in range(NTL)]

        def evac_copy(unit):
            def ev(hi, h0, n, b):
                S.op('act', lambda e: e.copy(self.ar[:, unit, h0:h0 + n], self.pb[b][:, 0:n]),
                     reads=[('pb', b)], writes=self.akeys(unit, unit + 1, h0, n))
            return ev
        slot = self.w_get(S)
        W = self.wv(slot, 0, 8, 512)
        import os
        dbg = os.environ.get('KDBG', '')
        for c in range(4):
            self.fm(S, slot, W, c * 128, xsrc, 8, halves, allx, evac_copy(c))
            yield None
        self.w_done(S, slot)
        slot = self.w_get(S)
        W = self.wv(slot, 0, 8, 256)

        def ev_k(hi, h0, n, b):
            S.op('act', lambda e: e.copy(self.kT[:, 128 + h0:128 + h0 + n], self.pb[b][:, 0:n]),
                 reads=[('pb', b)], writes=[('kT', 1 + t) for t in range(h0 // 128, (h0 + n) // 128)])
        self.fm(S, slot, W, 0, xsrc, 8, halves, allx, ev_k)
        yield None
        for t in range(NTL):
            b = self.bank()
            pb = self.pb[b]

            def mm(e, t=t, pb=pb, W=W):
                r = None
                for kc in range(8):
                    r = e.matmul(pb[:, 0:256], lhsT=self.xT[:, kc, t * 128:(t + 1) * 128], rhs=W[:, kc, 0:256],
                                 start=(kc == 0), stop=(kc == 7))
                return r
            S.op('pe', mm, reads=[('wr', slot), ('xT', t)], writes=[('pb', b)])
            if 'novaug' not in dbg:
              S.op(os.environ.get('KVENG', 'dve'), lambda e, t=t, pb=pb: (e.tensor_copy if os.environ.get('KVENG', 'dve') == 'dve' else e.copy)(self.vaug[:, t + 1, :, 0:64],
                                                             pb[:, 128:256].rearrange("p (k d) -> p k d", k=2)),
                 reads=[('pb', b)], writes=[('vaug', t + 1)])
            if 'nostg' in dbg:
                continue
            if (kind == 'p' and grp['last'] and t == NTL - 1) or kind == 's':
                S.op('act', lambda e, pb=pb: e.copy(self.stg0[:, 0:256], pb[:, 0:256]), reads=[('pb', b)], writes=[('stg', 0)])
                if kind == 'p':
                    S.dma('sp', self.kwp[l, grp['b']], self.stg0[:, 0:128], chan='st0', reads=[('stg', 0)], out_final=True)
                    S.dma('sp', self.vwp[l, grp['b']], self.stg0[:, 128:256], chan='st0b', reads=[('stg', 0)], out_final=True)
                else:
                    S.dma('sp', self.kws[l][:, 120:128, :], self.stg0[:, 0:128], chan='st0', reads=[('stg', 0)], out_final=True)
                    S.dma('sp', self.vws[l][:, 120:128, :], self.stg0[:, 128:256], chan='st0b', reads=[('stg', 0)], out_final=True)
                    S.dma('sp', self.kws[l][:, 0:120, :], self.ck[l][:, 8:128, :], chan='cc0', out_final=True)
                    S.dma('sp', self.vws[l][:, 0:120, :], self.cv[l][:, 8:128, :], chan='cc1', out_final=True)
            yield None
        self.w_done(S, slot)
        yield 'KV_DONE'
        slot = self.w_get(S)
        W = self.wv(slot, 0, 8, 512)
        for c in range(4):
            self.fm(S, slot, W, c * 128, xsrc, 8, halves, allx, evac_copy(4 + c))
            yield None
        self.w_done(S, slot)
        slot = self.w_get(S)
        W = self.wv(slot, 0, 8, 512)
        for t in range(NTL):
            b = self.bank()
            pb = self.pb[b]

            def mm(e, t=t, pb=pb, W=W):
                r = None
                for kc in range(8):
                    r = e.matmul(pb[:, 0:512], lhsT=self.xT[:, kc, t * 128:(t + 1) * 128], rhs=W[:, kc, 0:512],
                                 start=(kc == 0), stop=(kc == 7))
                return r
            S.op('pe', mm, reads=[('wr', slot), ('xT', t)], writes=[('pb', b)])
            st, k = self.layer_norm(S, pb[:, 0:512], None, 512, None, None, [('pb', b)], None, None)
            ti = self.tfs()
            tmp = self.tf[:, ti, :]
            S.op('dve', lambda e, pb=pb, tmp=tmp, st=st: e.scalar_tensor_tensor(tmp, pb[:, 0:512], st[:, 12:13], self.glnbc[:, 0, :], ALU.subtract, ALU.mult),
                 reads=[('pb', b), k, ('glnbc', l)], writes=[('tf', ti)])
            u, c0 = 12 + t // 2, (t % 2) * 512
            gv_dst = self.ar[:, u, c0:c0 + 512]
            if kind == 's':
                S.op('dve', lambda e, tmp=tmp, st=st: e.scalar_tensor_tensor(tmp, tmp, st[:, 15:16], self.glnbc[:, 1, :], ALU.mult, ALU.add),
                     reads=[('tf', ti), ('glnbc', l), k], writes=[('tf', ti)])
                S.op('act', lambda e, tmp=tmp, gv_dst=gv_dst: e.copy(gv_dst, tmp), reads=[('tf', ti)], writes=self.akeys(u, u + 1, c0, 512))
                S.dma('sp', self.gvs[l], tmp, chan='gvs', reads=[('tf', ti)], out_final=True)
            else:
                S.op('dve', lambda e, tmp=tmp, gv_dst=gv_dst, st=st: e.scalar_tensor_tensor(gv_dst, tmp, st[:, 15:16], self.glnbc[:, 1, :], ALU.mult, ALU.add),
                     reads=[('tf', ti), ('glnbc', l), k], writes=self.akeys(u, u + 1, c0, 512))
            if t >= 2:
                self.gmlp_block(S, grp, l, t - 2)
            yield None
        self.w_done(S, slot)
        slot = self.w_get(S)
        W = self.wv(slot, 0, 8, 512)
        for c in range(4):
            self.fm(S, slot, W, c * 128, xsrc, 8, halves, allx, evac_copy(8 + c))
            if c < 2 and NTL - 2 + c >= 0 and NTL >= 2:
                self.gmlp_block(S, grp, l, NTL - 2 + c)
            yield None
        if NTL < 2:
            self.gmlp_block(S, grp, l, 0)
        self.w_done(S, slot)
        self.conv_carry_in(S, grp, l)
        slot_c = self.w_get(S)
        slot_h = self.w_get(S)
        Wc = self.wv(slot_c, 0, 8, 512)
        Wh = self.wv(slot_h, 0, 8, 512)
        for c in range(4):
            tmps = {}

            def ev_c(hi, h0, n, b, tmps=tmps):
                ti = self.tfs()
                tmps[hi] = ti
                S.op('act', lambda e: e.copy(self.tf[:, ti, 0:n], self.pb[b][:, 0:n]), reads=[('pb', b)], writes=[('tf', ti)])

            def ev_h(hi, h0, n, b, tmps=tmps, c=c):
                ti = tmps[hi]
                if kind == 'p':
                    dst = self.uT[:, c, 2 + h0:2 + h0 + n]
                    in0 = self.pb[b][:, 0:n]
                    in1 = self.tf[:, ti, 0:n]
                else:
                    dst = self.uT[:, c, 0:160].rearrange("p (s j) -> p s j", j=10)[:, :, 2:10]
                    in0 = self.pb[b][:, 0:n].rearrange("p (s j) -> p s j", j=8)
                    in1 = self.tf[:, ti, 0:n].rearrange("p (s j) -> p s j", j=8)
                S.op('dve', lambda e: e.tensor_tensor(dst, in0, in1, ALU.mult),
                     reads=[('pb', b), ('tf', ti)], writes=[('uT', c, hi)])
            self.fm(S, slot_c, Wc, c * 128, xsrc, 8, halves, allx, ev_c)
            yield None
            self.fm(S, slot_h, Wh, c * 128, xsrc, 8, halves, allx, ev_h)
            self.conv_chunk(S, grp, l, c)
            yield None
        self.conv_carry_out(S, grp, l)
        if (kind == 'p' and grp['last']) or kind == 's':
            t = NTL - 1
            bc = self.bank(); bh = self.bank()

            def mm(e, t=t, bc=bc, bh=bh, Wc=Wc, Wh=Wh):
                r = None
                for (W_, bb) in ((Wc, bc), (Wh, bh)):
                    for kc in range(8):
                        r = e.matmul(self.pb[bb][:, 0:512], lhsT=self.xT[:, kc, t * 128:(t + 1) * 128], rhs=W_[:, kc, 0:512],
                                     start=(kc == 0), stop=(kc == 7))
                return r
            S.op('pe', mm, reads=[('wr', slot_c), ('wr', slot_h), ('xT', t)], writes=[('pb', bc), ('pb', bh)])
            ti = self.tfs()
            S.op('act', lambda e, ti=ti, bc=bc: e.copy(self.tf[:, ti, :], self.pb[bc][:, 0:512]), reads=[('pb', bc)], writes=[('tf', ti)])
            S.op('dve', lambda e, ti=ti, bh=bh: e.tensor_tensor(self.stg1[:, :], self.pb[bh][:, 0:512], self.tf[:, ti, :], ALU.mult),
                 reads=[('pb', bh), ('tf', ti)], writes=[('stg', 1)])
            if kind == 'p':
                S.dma('sp', self.mcp[l, grp['b']], self.stg1[126:128, :], chan='st1', reads=[('stg', 1)], out_final=True)
            else:
                for r in range(2):
                    S.dma('sp', self.mcs[l].rearrange("(s r) c -> s r c", r=2)[:, r, :],
                          self.stg1[:, :], chan='st1%d' % r, reads=[('stg', 1)], out_final=True,
                          fn=(lambda e, r=r: [e.dma_start(out=self.mcs[l].rearrange("(s r) c -> s r c", r=2)[:, r, :],
                                                          in_=self.sel_rows(self.stg1[:, :], 6 + r))]))
        self.w_done(S, slot_c)
        self.w_done(S, slot_h)

    def sel_rows(self, ap2d, r):
        return ap2d[r:128:8, :]

    def attn_pv_norm(self, S, l, t, pv_fn, extra_reads, outcols):
        bA = self.bank(); bB = self.bank()
        banks = (bA, bB)
        S.op('pe', lambda e: pv_fn(e, self.pb[bA], self.pb[bB]), reads=extra_reads, writes=[('pb', bA), ('pb', bB)])
        i = self.nln % 4
        self.nln += 1
        sc = self.att_s[:, i, :]
        k = ('atts', i)
        for gi in range(2):
            pv = self.pb[banks[gi]][:, 0:260].rearrange("p (h d) -> p h d", h=4)
            S.op('dve', lambda e, pv=pv, gi=gi: e.tensor_tensor(sc[:, gi * 4:(gi + 1) * 4].unsqueeze(2), pv[:, :, 64:65],
                                                               self.esink[:, l, gi * 4:(gi + 1) * 4].unsqueeze(2), ALU.add),
                 reads=[('pb', banks[gi]), 'esink'], writes=[k])
        S.op('dve', lambda e: e.reciprocal(sc[:, 8:16], sc[:, 0:8]), reads=[k], writes=[k])
        tb = self.tbs()
        at = self.tb[:, tb, :]
        for gi in range(2):
            pv = self.pb[banks[gi]][:, 0:260].rearrange("p (h d) -> p h d", h=4)
            S.op('dve', lambda e, pv=pv, gi=gi: e.tensor_tensor(
                at[:, gi * 256:(gi + 1) * 256].rearrange("p (h d) -> p h d", h=4), pv[:, :, 0:64],
                sc[:, 8 + gi * 4:8 + (gi + 1) * 4].unsqueeze(2).to_broadcast([128, 4, 64]), ALU.mult),
                reads=[('pb', banks[gi]), k], writes=[('tb', tb)])
        return at, tb

    def attn_tr(self, S, at, tb, outcols):
        bT = self.bank()
        pbt = self.pb[bT][:].bitcast(BF16)

        def tr(e):
            r = None
            for c in range(4):
                r = e.transpose(pbt[:, c * 128:(c + 1) * 128], at[:, c * 128:(c + 1) * 128], self.identb[:])
            return r
        S.op('pe', tr, reads=[('tb', tb), 'identb'], writes=[('pb', bT)])
        S.op('act', lambda e: e.copy(self.ar[:, 0:4, outcols:outcols + 128], pbt[:, 0:512].rearrange("p (c t) -> p c t", c=4)),
             reads=[('pb', bT)], writes=self.akeys(0, 4, outcols, 128))

    def attn_scores(self, S, t, gI, kT_ap, kkeys, E_ap, ekey, qcols=128, mrows=128):
        b = self.bank()
        pb = self.pb[b]
        lo, hi = gI * 64, (gI + 1) * 64
        S.op('pe', lambda e: e.matmul(pb[0:mrows, 0:512], lhsT=kT_ap[lo:hi, :],
                                      rhs=self.ar[lo:hi, 0:4, t * 128:t * 128 + 128], start=True, stop=True),
             reads=kkeys + self.akeys(0, 4, t * 128, 128), writes=[('pb', b)])
        tb = self.tbs()
        S.op('act', lambda e: e.activation(self.tb[0:mrows, tb, :], pb[0:mrows, 0:512], AF.Exp, scale=0.125),
             reads=[('pb', b)], writes=[('tb', tb)])
        S.op('dve', lambda e: e.tensor_tensor(self.tb[0:mrows, tb, :], self.tb[0:mrows, tb, :], E_ap, ALU.mult),
             reads=[('tb', tb), ekey], writes=[('tb', tb)])
        return tb

    def attention_prompt(self, S, grp, l):
        NTL = grp['NTL']
        st = {}

        def scores(t):
            first = grp['first'] and t == 0
            chunks = [1] if first else [0, 1]
            pts = {}
            for gI in range(2):
                for c in chunks:
                    kT_ap = self.kT[:, (t + c) * 128:(t + c + 1) * 128]
                    E_ap = self.E[:, c, gI * 512:(gI + 1) * 512]
                    pts[(gI, c)] = self.attn_scores(S, t, gI, kT_ap, [('kT', t + c)], E_ap, ('E', c))
            st[t] = (chunks, pts)

        def pv(t):
            chunks, pts = st[t]

            def pv_fn(e, pA, pB):
                r = None
                for h in range(8):
                    gI, hh = h // 4, h % 4
                    pbk = pA if gI == 0 else pB
                    for ci, c in enumerate(chunks):
                        r = e.matmul(pbk[:, hh * 65:(hh + 1) * 65], lhsT=self.tb[:, pts[(gI, c)], hh * 128:(hh + 1) * 128],
                                     rhs=self.vaug[:, t + c, gI, :], start=(ci == 0), stop=(ci == len(chunks) - 1))
                return r
            reads = [('tb', v) for v in pts.values()] + [('vaug', t + c) for c in chunks] + ['vaug_ones']
            st[t] = self.attn_pv_norm(S, l, t, pv_fn, reads, t * 128)

        for t in range(NTL + 2):
            if t < NTL:
                scores(t)
                yield None
            if 0 <= t - 1 < NTL:
                pv(t - 1)
                yield None
            if 0 <= t - 2 < NTL:
                at, tb = st[t - 2]
                self.attn_tr(S, at, tb, (t - 2) * 128)
                yield None

    def attention_sample_prep(self, S, grp, l):
        ckst = self.x_tok[:, 1:3, :].rearrange("p a (s c) -> p (a s) c", c=128)
        cvst = self.x_tok[:, 3:5, :].rearrange("p a (s c) -> p (a s) c", c=128)
        for (dst_, src_, ch_, keys_) in ((ckst, self.ck, 'ckin', ['ckst', ('xtok', 1), ('xtok', 2)]),
                                         (cvst, self.cv, 'cvin', ['cvst', ('xtok', 3), ('xtok', 4)])):
            tok = None
            for q4 in range(4):
                tok = S.dma('sp', dst_[:, q4 * 4:(q4 + 1) * 4, :], src_[l][q4 * 4:(q4 + 1) * 4].rearrange("q s c -> s q c"),
                            chan=ch_, writes=keys_)
            self._patch(S, keys_, tok)
        S.op('dve', lambda e: e.tensor_copy(self.vaugc[:, :, :, 0:64], cvst.rearrange("p q (k d) -> p q k d", k=2)),
             reads=['cvst'], writes=['vaugc'])
        S.dma('sp', self.stg1[0:32, :], self.smix[l], chan='smixin', writes=[('stg', 1)])

    def attention_sample_prep_tr(self, S, grp, l):
        ckst = self.x_tok[:, 1:3, :].rearrange("p a (s c) -> p (a s) c", c=128)
        for q4 in range(4):
            b = self.bank()
            pb = self.pb[b]

            def tr(e, q4=q4, pb=pb):
                r = None
                for j in range(4):
                    r = e.transpose(pb[:, j * 128:(j + 1) * 128], ckst[:, q4 * 4 + j, :], self.identf[:])
                return r
            S.op('pe', tr, reads=['ckst', 'identf'], writes=[('pb', b)])
            S.op('act', lambda e, q4=q4, pb=pb: e.copy(self.kcT[:, q4 * 4:(q4 + 1) * 4, :], pb[:].rearrange("p (j s) -> p j s", j=4)),
                 reads=[('pb', b)], writes=[('kcT', q4)])

    def attention_sample(self, S, grp, l):
        pt0 = {}
        for gI in range(2):
            b = self.bank()
            pb = self.pb[b]
            lo, hi = gI * 64, (gI + 1) * 64

            def mm(e, pb=pb, lo=lo, hi=hi):
                r = None
                for q in range(16):
                    r = e.matmul(pb[:, q * 32:(q + 1) * 32], lhsT=self.kcT[lo:hi, q, :],
                                 rhs=self.ar[lo:hi, 0:4, q * 8:(q + 1) * 8], start=True, stop=True)
                return r
            S.op('pe', mm, reads=[('kcT', i) for i in range(4)] + self.akeys(0, 4, 0, 128), writes=[('pb', b)])
            tb = self.tbs()
            S.op('act', lambda e, tb=tb, pb=pb: e.activation(self.tb[:, tb, :], pb[:, 0:512], AF.Exp, scale=0.125),
                 reads=[('pb', b)], writes=[('tb', tb)])
            S.op('dve', lambda e, tb=tb, gI=gI: e.tensor_tensor(self.tb[:, tb, :], self.tb[:, tb, :],
                                                               self.E[:, 2, gI * 512:(gI + 1) * 512], ALU.mult),
                 reads=[('tb', tb), ('E', 2)], writes=[('tb', tb)])
            pt0[gI] = tb
        o0T = self.x_tok[0:65, 7, :]
        bo = [self.bank(), self.bank()]

        def pv0(e):
            r = None
            for q in range(16):
                for gI in range(2):
                    col = ((q % 8) * 2 + gI) * 32
                    r = e.matmul(self.pb[bo[q // 8]][0:65, col:col + 32], lhsT=self.vaugc[:, q, gI, :],
                                 rhs=self.tb[:, pt0[gI], q * 32:(q + 1) * 32], start=True, stop=True)
            return r
        S.op('pe', pv0, reads=[('tb', pt0[0]), ('tb', pt0[1]), 'vaugc', 'vaugc_ones'], writes=[('pb', bo[0]), ('pb', bo[1])])
        o0w = o0T.rearrange("p (g h q t) -> p g h q t", g=2, h=4, q=16)
        for hq in range(2):
            for gI in range(2):
                S.op('act', lambda e, hq=hq, gI=gI: e.copy(
                    o0w[:, gI, :, hq * 8:(hq + 1) * 8, :].rearrange("p h q t -> p q h t"),
                    self.pb[bo[hq]][0:65, 0:512].rearrange("p (q g h t) -> p q g h t", q=8, g=2, h=4)[:, :, gI, :, :]),
                    reads=[('pb', bo[hq])], writes=['o0T', ('xtok', 7)])
        pts = {}
        for gI in range(2):
            pts[gI] = self.attn_scores(S, 0, gI, self.kT[:, 128:256], [('kT', 1)], self.E[:, 3, gI * 512:(gI + 1) * 512], ('E', 3))

        def pv_fn(e, pA, pB):
            r = None
            for h in range(8):
                gI, hh = h // 4, h % 4
                pbk = pA if gI == 0 else pB
                e.matmul(pbk[:, hh * 65:(hh + 1) * 65], lhsT=self.tb[:, pts[gI], hh * 128:(hh + 1) * 128],
                         rhs=self.vaug[:, 1, gI, :], start=True, stop=False)
                r = e.matmul(pbk[:, hh * 65:(hh + 1) * 65], lhsT=o0w[:, gI, hh, :, :].rearrange("p q t -> p (q t)"),
                             rhs=self.identf[0:65, 0:65], start=False, stop=True)
            return r
        reads = [('tb', pts[0]), ('tb', pts[1]), ('vaug', 1), 'vaug_ones', 'o0T', 'identf']
        at, tb = self.attn_pv_norm(S, l, 0, pv_fn, reads, 0)
        self.attn_tr(S, at, tb, 0)
        yield None

    def gmlp_prep(self, S, grp, l):
        kind = grp['kind']
        ti = self.tfs()
        wld = self.tf[:, ti, :].rearrange("p (g s) -> p g s", g=4)
        if kind == 'p':
            S.dma('sp', wld, self.gws[l].rearrange("g t s -> t g s"), chan='gwin', writes=[('tf', ti)])
            tok = None
        else:
            ti0 = self.tfs()
            stage = self.tf[:, ti0, 0:32]
            tok = None
            for q in range(16):
                tok = S.dma('sp', self.tf[q * 8:(q + 1) * 8, ti0, 0:32].rearrange("p (g s) -> p g s", g=4),
                            self.gws[l][:, 0:8, 0:8].rearrange("g t s -> t g s"), chan='gwin', writes=[('tf', ti0)])
            self._patch(S, [('tf', ti0)], tok)
            S.op('dve', lambda e: e.tensor_copy(self.tf[:, ti, :].rearrange("p (g q s) -> p g q s", g=4, q=16),
                                                stage.rearrange("p (g s) -> p g s", g=4).unsqueeze(2).to_broadcast([128, 4, 16, 8])),
                 reads=[('tf', ti0)], writes=[('tf', ti)])
        b = self.bank()
        pb = self.pb[b]

        def tr(e):
            r = None
            for g in range(4):
                r = e.transpose(pb[:, g * 128:(g + 1) * 128], wld[:, g, :], self.identf[:])
            return r
        S.op('pe', tr, reads=[('tf', ti), 'identf'], writes=[('pb', b)])
        mi = 0 if kind == 'p' else 1
        S.op('dve', lambda e: e.tensor_tensor(self.wst[:], pb[:].rearrange("p (g t) -> p g t", g=4),
                                              self.maskt[:, mi, :].unsqueeze(1).to_broadcast([128, 4, 128]), ALU.mult),
             reads=[('pb', b), 'maskt'], writes=['wst'])
        if kind == 'p':
            S.dma('sp', self.gbias[:].rearrange("p g t -> p (g t)"), self.gbs[l].partition_broadcast(128), chan='gbin', writes=['gbias'])
        else:
            ti2 = self.tfs()
            tmp = self.tf[:, ti2, :]
            S.dma('sp', tmp, self.gbs[l].partition_broadcast(128), chan='gbin', writes=[('tf', ti2)])
            S.op('dve', lambda e: e.tensor_copy(self.gbias[:].rearrange("p g (q s) -> p g q s", q=16),
                                                tmp.rearrange("p (g t) -> p g t", g=4)[:, :, 0:8].unsqueeze(2).to_broadcast([128, 4, 16, 8])),
                 reads=[('tf', ti2)], writes=['gbias'])
        S.dma('sp', self.glnbc[:, 0, :], self.gln_g[l].partition_broadcast(128), chan='glg', writes=[('glnbc', l)])
        S.dma('sp', self.glnbc[:, 1, :], self.gln_b[l].partition_broadcast(128), chan='glb', writes=[('glnbc', l)])

    def gmlp_block(self, S, grp, l, t):
        if True:
            u, c0 = 12 + t // 2, (t % 2) * 512
            b = self.bank()
            pb = self.pb[b]

            def mm(e, pb=pb, u=u, c0=c0):
                r = None
                for g in range(4):
                    r = e.matmul(pb[:, g * 128:(g + 1) * 128], lhsT=self.ar[:, u, c0 + g * 128:c0 + (g + 1) * 128],
                                 rhs=self.wst[:, g, :], start=True, stop=True)
                return r
            S.op('pe', mm, reads=self.akeys(u, u + 1, c0, 512) + ['wst'], writes=[('pb', b)])
            ti = self.tfs()
            S.op('dve', lambda e, pb=pb, ti=ti: e.tensor_tensor(self.tf[:, ti, :], pb[:, 0:512], self.gbias[:].rearrange("p g t -> p (g t)"), ALU.add),
                 reads=[('pb', b), 'gbias'], writes=[('tf', ti)])
            gu = self.ar[:, 4:8, t * 128:(t + 1) * 128]
            S.op('dve', lambda e, ti=ti, gu=gu: e.tensor_tensor(gu, gu, self.tf[:, ti, :].rearrange("p (g t) -> p g t", g=4), ALU.mult),
                 reads=[('tf', ti)] + self.akeys(4, 8, t * 128, 128), writes=self.akeys(4, 8, t * 128, 128))

    def conv_carry_in(self, S, grp, l):
        kind, GT, halves = grp['kind'], grp['GT'], grp['halves']
        if kind == 'p':
            if grp['first']:
                S.op('dve', lambda e: e.memset(self.uT[:, :, 0:2], 0.0), writes=[('uTc',)])
            else:
                S.op('dve', lambda e: e.tensor_copy(self.uT[:, :, 0:2], self.ucar[:, l, :, :]), reads=[('ucar', l)], writes=[('uTc',)])
        else:
            b = self.bank()
            pb = self.pb[b]

            def tr(e, pb=pb):
                r = None
                for c in range(4):
                    r = e.transpose(pb[:, c * 32:(c + 1) * 32], self.stg1[0:32, c * 128:(c + 1) * 128], self.identf[0:32, 0:32])
                return r
            S.op('pe', tr, reads=[('stg', 1), 'identf'], writes=[('pb', b)])
            S.op('dve', lambda e, pb=pb: e.tensor_copy(self.uT[:, :, 0:160].rearrange("p c (s j) -> p c s j", j=10)[:, :, :, 0:2],
                                                pb[:, 0:128].rearrange("p (c s r) -> p c s r", c=4, r=2)),
                 reads=[('pb', b)], writes=[('uTc',)])

    def conv_chunk(self, S, grp, l, c):
        kind, GT, halves = grp['kind'], grp['GT'], grp['halves']
        if True:
            for hi, (h0, n) in enumerate(halves):
                ti = self.tfs()
                if kind == 'p':
                    tmp = self.tf[:, ti, 0:n]
                    u2 = self.uT[:, c, 2 + h0:2 + h0 + n]
                    u1 = self.uT[:, c, 1 + h0:1 + h0 + n]
                    u0 = self.uT[:, c, h0:h0 + n]
                    sbv = self.ar[:, 8 + c, h0:h0 + n]
                else:
                    uv = self.uT[:, c, 0:160].rearrange("p (s j) -> p s j", j=10)
                    tmp = self.tf[:, ti, 0:128].rearrange("p (s j) -> p s j", j=8)
                    u2, u1, u0 = uv[:, :, 2:10], uv[:, :, 1:9], uv[:, :, 0:8]
                    sbv = self.ar[:, 8 + c, 0:128].rearrange("p (s j) -> p s j", j=8)
                rk = [('uT', c, hi), ('uTc',)] + ([('uT', c, hi - 1)] if hi > 0 else [])
                S.op('dve', lambda e, tmp=tmp, u2=u2, c=c: e.tensor_scalar(tmp, u2, self.p_mixw(l, 2, c), None, ALU.mult),
                     reads=rk + [('prm', l)], writes=[('tf', ti)])
                S.op('dve', lambda e, tmp=tmp, u1=u1, c=c: e.scalar_tensor_tensor(tmp, u1, self.p_mixw(l, 1, c), tmp, ALU.mult, ALU.add),
                     reads=rk + [('tf', ti)], writes=[('tf', ti)])
                S.op('dve', lambda e, tmp=tmp, u0=u0, c=c: e.scalar_tensor_tensor(tmp, u0, self.p_mixw(l, 0, c), tmp, ALU.mult, ALU.add),
                     reads=rk + [('tf', ti)], writes=[('tf', ti)])
                S.op('dve', lambda e, tmp=tmp, sbv=sbv: e.tensor_tensor(sbv, sbv, tmp, ALU.mult),
                     reads=[('tf', ti)] + self.akeys(8 + c, 9 + c, h0, n), writes=self.akeys(8 + c, 9 + c, h0, n))

    def conv_carry_out(self, S, grp, l):
        kind, GT, halves = grp['kind'], grp['GT'], grp['halves']
        if kind == 'p' and not grp['last']:
            S.op('dve', lambda e: e.tensor_copy(self.ucar[:, l, :, :], self.uT[:, :, GT:GT + 2]),
                 reads=[('uT', c, len(halves) - 1) for c in range(4)], writes=[('ucar', l)])

    def carry_in(self, S, grp, l):
        if grp['kind'] == 'p' and not grp['first']:
            S.op('dve', lambda e: e.tensor_copy(self.kT[:, 0:128], self.kcar[:, l, :]), reads=[('kcar', l)], writes=[('kT', 0)])
            S.op('dve', lambda e: e.tensor_copy(self.vaug[:, 0, :, 0:64], self.vcar[:, l, :, 0:64]), reads=[('vcar', l)], writes=[('vaug', 0)])

    def carry_out(self, S, grp, l):
        if grp['kind'] == 'p' and not grp['last']:
            n = grp['NTL']
            S.op('dve', lambda e: e.tensor_copy(self.kcar[:, l, :], self.kT[:, n * 128:(n + 1) * 128]), reads=[('kT', n)], writes=[('kcar', l)])
            S.op('dve', lambda e: e.tensor_copy(self.vcar[:, l, :, 0:64], self.vaug[:, n, :, 0:64]), reads=[('vaug', n)], writes=[('vcar', l)])

    def merge(self, S, grp, l):
        GT, halves = grp['GT'], grp['halves']
        allx = [('xT', t) for t in range(grp['NTL'])]
        xsrc = lambda kc, h0, n: self.xT[:, kc, h0:h0 + n]
        for dp in range(4):
            for i in range(3):
                slot = self.w_get(S)
                Wg = self.wv(slot, 0, 8, 256)
                Wp = self.wv(slot, 2048, 4, 256)
                for d2 in range(2):
                    dc = dp * 2 + d2
                    gts = {}

                    def ev_g(hi, h0, n, b, gts=gts, dc=dc, i=i):
                        tb = self.tbs()
                        gts[hi] = tb
                        S.op('act', lambda e: e.activation(self.tb[:, tb, 0:n], self.pb[b][:, 0:n], AF.Sigmoid,
                                                           bias=self.p_bg(l, i * 8 + dc), scale=1.0),
                             reads=[('pb', b), ('prm', l)], writes=[('tb', tb)])

                    def ev_p(hi, h0, n, b, gts=gts, dc=dc, d2=d2, i=i):
                        tb = gts[hi]
                        mk = ('macc', d2 * 2 + hi)
                        mac = self.macc[:, d2 * 2 + hi, 0:n]
                        g_ap = self.tb[:, tb, 0:n]
                        p_ap = self.pb[b][:, 0:n]
                        if i == 0:
                            S.op('dve', lambda e: e.tensor_tensor(mac, p_ap, g_ap, ALU.mult),
                                 reads=[('pb', b), ('tb', tb)], writes=[mk])
                        else:
                            ti = self.tfs()
                            tmp = self.tf[:, ti, 0:n]
                            S.op('dve', lambda e: e.tensor_tensor(tmp, p_ap, g_ap, ALU.mult),
                                 reads=[('pb', b), ('tb', tb)], writes=[('tf', ti)])
                            if i == 1:
                                S.op('dve', lambda e: e.tensor_tensor(mac, mac, tmp, ALU.add), reads=[mk, ('tf', ti)], writes=[mk])
                            else:
                                S.op('dve', lambda e: e.tensor_tensor(self.ar[:, 16 + dc, h0:h0 + n], mac, tmp, ALU.add),
                                     reads=[mk, ('tf', ti)], writes=self.akeys(16 + dc, 17 + dc, h0, n))
                    self.fm(S, slot, Wg, d2 * 128, xsrc, 8, halves, allx, ev_g)
                    bsrc = lambda kc, h0, n, i=i: self.ar[:, i * 4 + kc, h0:h0 + n]
                    self.fm(S, slot, Wp, d2 * 128, bsrc, 4, halves, self.akeys(i * 4, i * 4 + 4, 0, GT), ev_p)
                self.w_done(S, slot)

    def ln_bc_load(self, S, g_ap, b_ap, tag):
        S.dma('sp', self.lnbc[:, 0, :], g_ap.partition_broadcast(128), chan='lng', writes=['lnbc'])
        S.dma('sp', self.lnbc[:, 1, :], b_ap.partition_broadcast(128), chan='lnb', writes=['lnbc'])

    def ln_stats(self, S, t):
        xr = self.x_tok[:, t, :]
        st, k = self.layer_norm(S, xr, None, 1024, None, None, [('xtok', t)], None, None, recip=False)

        def affine():
            S.op('dve', lambda e: e.reciprocal(st[:, 15:16], st[:, 14:15]), reads=[k], writes=[k])
            S.op('dve', lambda e: e.scalar_tensor_tensor(xr, xr, st[:, 12:13], self.lnbc[:, 0, :], ALU.subtract, ALU.mult),
                 reads=[('xtok', t), 'lnbc', k], writes=[('xtok', t)])
            S.op('dve', lambda e: e.scalar_tensor_tensor(xr, xr, st[:, 15:16], self.lnbc[:, 1, :], ALU.mult, ALU.add),
                 reads=[('xtok', t), 'lnbc', k], writes=[('xtok', t)])
        return affine

    def wo_ln1(self, S, grp, l):
        NTL = grp['NTL']
        self.ln_bc_load(S, self.ln1_g[l], self.ln1_b[l], 'ln1')
        prev_aff = None
        slots = [self.w_get(S), self.w_get(S)]
        Ws = [self.wv(s_, 0, 8, 512) for s_ in slots]
        for t in range(NTL):
            for ch in range(2):
                b = self.bank()
                pb = self.pb[b]

                def mm(e, t=t, pb=pb, W=Ws[ch]):
                    r = None
                    for kc in range(8):
                        r = e.matmul(pb[:, 0:512], lhsT=self.ar[:, 16 + kc, t * 128:(t + 1) * 128], rhs=W[:, kc, 0:512],
                                     start=(kc == 0), stop=(kc == 7))
                    return r
                S.op('pe', mm, reads=[('wr', slots[ch])] + self.akeys(16, 24, t * 128, 128), writes=[('pb', b)])
                xs = self.x_tok[:, t, ch * 512:(ch + 1) * 512]
                S.op('dve', lambda e, xs=xs, pb=pb: e.scalar_tensor_tensor(xs, xs, ALPHA, pb[:, 0:512], ALU.mult, ALU.add),
                     reads=[('pb', b), ('xtok', t)], writes=[('xtok', t)])
            aff = self.ln_stats(S, t)
            if prev_aff:
                prev_aff()
            prev_aff = aff
        for s_ in slots:
            self.w_done(S, s_)
        if prev_aff:
            prev_aff()
        for t in range(NTL):
            self.build_xT(S, t)

    def wd_load(self, S, l, hf):
        first = self._cur_gi == 0
        multi = self._ngroups > 1
        keys = self.akeys(0, 11, 0, 1024)
        skey = ('scrwd', l, hf)
        if first or not multi:
            S.dma('pool', self.ar[:, 0:11, :], self.w_down[l, hf * 1408:(hf + 1) * 1408, :].rearrange("(j p) c -> p j c", p=128),
                  chan='wd', writes=keys)
            if multi:
                for (u0, u1, ch) in ((0, 6, 'wdb0'), (6, 11, 'wdb1')):
                    S.dma('sp', self.scr_wd[l, hf, :, u0 * 1024:u1 * 1024], self.ar[:, u0:u1, :].rearrange("p u c -> p (u c)"),
                          chan=ch, reads=keys, writes=[skey])
        else:
            def fn(e):
                r = []
                for (u0, u1) in ((0, 6), (6, 11)):
                    r.append(e.dma_start(out=self.ar[:, u0:u1, :].rearrange("p u c -> p (u c)"), in_=self.scr_wd[l, hf, :, u0 * 1024:u1 * 1024]))
                return r
            S.dma('pool', None, None, chan='wd', reads=[skey], writes=keys, n=2, fn=fn)

    def ffn(self, S, grp, l, last_layer):
        kind, GT, halves, NTL = grp['kind'], grp['GT'], grp['halves'], grp['NTL']
        allx = [('xT', t) for t in range(NTL)]
        xsrc = lambda kc, h0, n: self.xT[:, kc, h0:h0 + n]
        state_out = (kind == 'p' and grp['last']) or kind == 's'
        stT = getattr(self, '_stT', None)
        self.ln_bc_load(S, self.ln2_g[l], self.ln2_b[l], 'ln2')
        return self.ffn_body(S, grp, l, last_layer, stT, state_out)

    def sample_ffn_prep(self, S, grp, l):
        kind = grp['kind']
        if kind == 's':
            stT = self.x_tok[:, 5:7, :].rearrange("p a b -> p (a b)")[:, 0:44 * 32].rearrange("p (c s r) -> p c s r", c=44, r=2)
            tis = [self.tfs(), self.tfs(), self.tfs(), self.tfs()]
            for q in range(11):
                ti = tis[q % 4]
                S.dma('sp', self.tf[0:32, ti, :], self.sffn[l][:, q * 512:(q + 1) * 512], chan='sffnin%d' % (q % 4), writes=[('tf', ti)])
                b = self.bank()
                pb = self.pb[b]

                def tr(e, ti=ti, pb=pb):
                    r = None
                    for c in range(4):
                        r = e.transpose(pb[:, c * 32:(c + 1) * 32], self.tf[0:32, ti, c * 128:(c + 1) * 128], self.identf[0:32, 0:32])
                    return r
                S.op('pe', tr, reads=[('tf', ti), 'identf'], writes=[('pb', b)])
                S.op('dve', lambda e, q=q, pb=pb: e.tensor_copy(stT[:, q * 4:(q + 1) * 4, :, :],
                                                               pb[:, 0:128].rearrange("p (c s r) -> p c s r", c=4, r=2)),
                     reads=[('pb', b)], writes=['stT', ('xtok', 5), ('xtok', 6)])
            self._stT = stT

    def ffn_body(self, S, grp, l, last_layer, stT, state_out):
        kind, GT, halves, NTL = grp['kind'], grp['GT'], grp['halves'], grp['NTL']
        allx = [('xT', t) for t in range(NTL)]
        xsrc = lambda kc, h0, n: self.xT[:, kc, h0:h0 + n]
        pend = []
        FDEPTH = 2
        for hf in range(2):
            for (j0, nj) in self.up_pieces(hf):
                slot = self.w_get(S)
                n_ = nj * 128
                Wa = self.wv(slot, 0, 8, n_)
                Wg = self.wv(slot, 8 * n_, 8, n_)
                for jj in range(nj):
                    j = j0 + jj
                    jl = j - hf * 11
                    hold = {}
                    for which, W_ in (('a', Wa), ('g', Wg)):
                        col = j if which == 'a' else 22 + j

                        def ev(hi, h0, n, b, which=which, col=col, hold=hold, jl=jl):
                            pbv = self.pb[b]
                            tfa, tk = self.ffs()
                            w0, w1, w2, bb = self.p_fcw(l, 0, col), self.p_fcw(l, 1, col), self.p_fcw(l, 2, col), self.p_fcb(l, col)
                            if kind == 'p':
                                c0 = tfa[:, 0:n]
                                P = pbv[:, 0:n]
                                hidx = grp['g'] * 2 + hi
                                pw = hidx % 2
                                cyw = self.upc[:, l, col, pw, :]
                                cyr = self.upc[:, l, col, 1 - pw, :]
                                S.op('act', lambda e: e.activation(c0, P, AF.Identity, bias=bb, scale=w2),
                                     reads=[('pb', b), ('prm', l)], writes=[tk])
                                if hidx < 2 * self.ngrp - 1:
                                    def carry(e):
                                        e.activation(cyw[:, 0:2], P[:, n - 2:n], AF.Identity, bias=0.0, scale=w0)
                                        return e.activation(cyw[:, 2:3], P[:, n - 1:n], AF.Identity, bias=0.0, scale=w1)
                                    S.op('act', carry, reads=[('pb', b), ('prm', l)], writes=[('upc', l, col, pw)])
                                S.op('dve', lambda e: e.scalar_tensor_tensor(c0[:, 1:n], P[:, 0:n - 1], w1, c0[:, 1:n], ALU.mult, ALU.add),
                                     reads=[('pb', b), tk], writes=[tk])
                                S.op('dve', lambda e: e.scalar_tensor_tensor(c0[:, 2:n], P[:, 0:n - 2], w0, c0[:, 2:n], ALU.mult, ALU.add),
                                     reads=[('pb', b), tk], writes=[tk])
                                if hidx > 0:
                                    ck_ = ('upc', l, col, 1 - pw)
                                    S.op('pool', lambda e: e.tensor_tensor(c0[:, 0:2], c0[:, 0:2], cyr[:, 0:2], ALU.add),
                                         reads=[ck_, tk], writes=[tk])
                                    S.op('pool', lambda e: e.tensor_tensor(c0[:, 0:1], c0[:, 0:1], cyr[:, 2:3], ALU.add),
                                         reads=[ck_, tk], writes=[tk])
                            else:
                                c0 = tfa[:, 0:n].rearrange("p (s j) -> p s j", j=8)
                                P = pbv[:, 0:n].rearrange("p (s j) -> p s j", j=8)
                                cy = stT[:, col, :, :]
                                S.op('act', lambda e: e.activation(tfa[:, 0:n], pbv[:, 0:n], AF.Identity, bias=bb, scale=w2),
                                     reads=[('pb', b), ('prm', l)], writes=[tk])
                                S.op('dve', lambda e: e.scalar_tensor_tensor(c0[:, :, 1:8], P[:, :, 0:7], w1, c0[:, :, 1:8], ALU.mult, ALU.add),
                                     reads=[('pb', b), tk], writes=[tk])
                                S.op('dve', lambda e: e.scalar_tensor_tensor(c0[:, :, 2:8], P[:, :, 0:6], w0, c0[:, :, 2:8], ALU.mult, ALU.add),
                                     reads=[('pb', b), tk], writes=[tk])
                                S.op('dve', lambda e: e.scalar_tensor_tensor(c0[:, :, 0:1], cy[:, :, 1:2], w1, c0[:, :, 0:1], ALU.mult, ALU.add),
                                     reads=['stT', tk], writes=[tk])
                                S.op('dve', lambda e: e.scalar_tensor_tensor(c0[:, :, 0:2], cy[:, :, 0:2], w0, c0[:, :, 0:2], ALU.mult, ALU.add),
                                     reads=['stT', tk], writes=[tk])
                            full = tfa[:, 0:n]
                            if which == 'a':
                                hold[hi] = (tfa, tk)
                            else:
                                tfa_a, tk_a = hold[hi]

                                def fin():
                                    S.op('act', lambda e: e.activation(full, full, AF.Silu), reads=[tk], writes=[tk])
                                    S.op('pool', lambda e: e.tensor_tensor(self.ar[:, 11 + jl, h0:h0 + n], tfa_a[:, 0:n], full, ALU.mult),
                                         reads=[tk, tk_a], writes=self.akeys(11 + jl, 12 + jl, h0, n))
                                pend.append(fin)
                                while len(pend) > FDEPTH:
                                    pend.pop(0)()
                        self.fm(S, slot, W_, jj * 128, xsrc, 8, halves, allx, ev)
                if state_out:
                    t = NTL - 1
                    for which, W_ in (('a', Wa), ('g', Wg)):
                        b = self.bank()
                        pb = self.pb[b]

                        def mm(e, W_=W_, pb=pb, n_=n_, t=t):
                            r = None
                            for kc in range(8):
                                r = e.matmul(pb[:, 0:n_], lhsT=self.xT[:, kc, t * 128:(t + 1) * 128], rhs=W_[:, kc, 0:n_],
                                             start=(kc == 0), stop=(kc == 7))
                            return r
                        S.op('pe', mm, reads=[('wr', slot), ('xT', t)], writes=[('pb', b)])
                        si = 0 if which == 'a' else 1
                        S.op('act', lambda e, pb=pb, si=si, n_=n_: e.copy((self.stg0 if si == 0 else self.stg1)[:, 0:n_], pb[:, 0:n_]), reads=[('pb', b)], writes=[('stg', si)])
                        cc = (j0 if which == 'a' else 22 + j0) * 128
                        if kind == 'p':
                            S.dma('sp', self.fcp[l, grp['b']][:, cc:cc + n_], (self.stg0 if si == 0 else self.stg1)[126:128, 0:n_], chan='st%d' % si,
                                  reads=[('stg', si)], out_final=True)
                        else:
                            for r in range(2):
                                dst = self.fcs[l].rearrange("(s r) c -> s r c", r=2)[:, r, cc:cc + n_]
                                S.dma('sp', dst, (self.stg0 if si == 0 else self.stg1)[6 + r:128:8, 0:n_], chan='st%d%d' % (si, r), reads=[('stg', si)], out_final=True)
                self.w_done(S, slot)
            while pend:
                pend.pop(0)()
            prev_fin = None
            JSPLIT = 9
            NEARLY = min(NTL, 4)
            tbanks = {}

            def dmm(t, j0_, j1_):
                banks = tbanks[t]

                def mm(e):
                    r = None
                    for ch in range(2):
                        for jl in range(j0_, j1_):
                            r = e.matmul(self.pb[banks[ch]][:, 0:512], lhsT=self.ar[:, 11 + jl, t * 128:(t + 1) * 128],
                                         rhs=self.ar[:, jl, ch * 512:(ch + 1) * 512], start=(jl == 0), stop=(jl == 10))
                    return r
                S.op('pe', mm, reads=self.akeys(11 + j0_, 11 + j1_, t * 128, 128) + self.akeys(j0_, j1_, 0, 1024),
                     writes=[('pb', b) for b in banks])
            for t in range(NEARLY):
                tbanks[t] = [self.bank(), self.bank()]
                dmm(t, 0, JSPLIT)
            for t in range(NTL):
                if t < NEARLY:
                    dmm(t, JSPLIT, 11)
                else:
                    tbanks[t] = [self.bank(), self.bank()]
                    dmm(t, 0, 11)
                banks = tbanks[t]
                for ch in range(2):
                    xs = self.x_tok[:, t, ch * 512:(ch + 1) * 512]
                    pbv = self.pb[banks[ch]][:, 0:512]
                    if hf == 0:
                        S.op('dve', lambda e, xs=xs, pbv=pbv: e.scalar_tensor_tensor(xs, xs, ALPHA, pbv, ALU.mult, ALU.add),
                             reads=[('pb', banks[ch]), ('xtok', t)], writes=[('xtok', t)])
                    else:
                        S.op('dve', lambda e, xs=xs, pbv=pbv: e.tensor_tensor(xs, xs, pbv, ALU.add),
                             reads=[('pb', banks[ch]), ('xtok', t)], writes=[('xtok', t)])
                if hf == 1:
                    aff = self.ln_stats(S, t)

                    def fin(t=t, aff=aff):
                        aff()
                        if last_layer:
                            if kind == 'p':
                                r0 = grp['g'] * 1024 + t * 128
                                S.dma('sp', self.y_p[grp['b'], r0:r0 + 128, :], self.x_tok[:, t, :], chan='yo%d' % t,
                                      reads=[('xtok', t)], out_final=True)
                                ng = self._groups[self._cur_gi + 1] if self._cur_gi + 1 < len(self._groups) else None
                                if ng is not None and ng['kind'] == 'p':
                                    r0n = ng['g'] * 1024 + t * 128
                                    S.dma('sp', self.x_tok[:, t, :], self.xp[ng['b'], r0n:r0n + 128, :], chan='xpre%d' % t, writes=[('xtok', t)])
                                    ng['preloaded'] = True
                                elif ng is not None and ng['kind'] == 's' and t == 0:
                                    S.dma('sp', self.x_tok[:, 0, :], self.xs, chan='xpre0', writes=[('xtok', 0)])
                                    ng['preloaded'] = True
                            else:
                                S.dma('sp', self.y_s, self.x_tok[:, 0, :], chan='yo0', reads=[('xtok', 0)], out_final=True)
                    if prev_fin:
                        prev_fin()
                    prev_fin = fin
            if hf == 1 and prev_fin:
                prev_fin()
            if hf == 0:
                self.wd_load(S, l, 1)
            elif not last_layer:
                for t in range(NTL):
                    self.build_xT(S, t)

    def proj_attn(self, S, grp, l):
        if grp['kind'] == 's':
            self.attention_sample_prep(S, grp, l)
        gp = self.proj(S, grp, l)
        for v in gp:
            if v == 'KV_DONE':
                break
        if grp['kind'] == 's':
            self.attention_sample_prep_tr(S, grp, l)
        ga = self.attention_prompt(S, grp, l) if grp['kind'] == 'p' else self.attention_sample(S, grp, l)
        alive_a = alive_p = True
        while alive_a or alive_p:
            if alive_p:
                try:
                    next(gp)
                except StopIteration:
                    alive_p = False
            if alive_a:
                try:
                    next(ga)
                except StopIteration:
                    alive_a = False

    def build(self):
        nc = self.nc
        self.declare()
        groups = []
        for b in range(self.nseq):
            for g in range(self.ngrp):
                groups.append(dict(kind='p', b=b, g=g, GT=1024, NTL=8, halves=[(0, 512), (512, 512)],
                                   first=(g == 0), last=(g == self.ngrp - 1)))
        if self.with_sample:
            groups.append(dict(kind='s', b=0, g=0, GT=128, NTL=1, halves=[(0, 128)], first=True, last=True))
        with ExitStack() as es:
            S = Sched(nc, es)
            self.alloc(S)
            self.init_consts(S)
            self.w_init(S, self.make_pieces(groups))
            import os
            stage = int(os.environ.get("KSTAGE", "999"))
            cnt = [0]

            def go():
                cnt[0] += 1
                return cnt[0] <= stage
            try:
                self._ngroups = len(groups)
                self._groups = groups
                for grp in groups:
                    self._cur_gi = groups.index(grp)
                    if not go(): raise StopIteration
                    self.load_x(S, grp)
                    for t in range(grp['NTL']):
                        self.build_xT(S, t)
                    for l in range(L):
                        gi = groups.index(grp)
                        nxt = (grp, l + 1) if l + 1 < L else ((groups[gi + 1], 0) if gi + 1 < len(groups) else None)
                        steps = [
                            lambda: ((self.gmlp_prep(S, grp, l) if (gi == 0 and l == 0) else None), self.carry_in(S, grp, l)),
                            lambda: self.proj_attn(S, grp, l),
                            lambda: self.carry_out(S, grp, l),
                            lambda: (self.sample_ffn_prep(S, grp, l), self.merge(S, grp, l), self.wd_load(S, l, 0)),
                            lambda: self.wo_ln1(S, grp, l),
                            lambda: ((self.gmlp_prep(S, nxt[0], nxt[1]) if nxt else None), self.ffn(S, grp, l, l == L - 1)),
                        ]
                        for st_ in steps:
                            if not go(): raise StopIteration
                            st_()
            except StopIteration:
                pass
            print("stages emitted:", cnt[0], "tasks:", S.ntask)
            S.finish()
            self.ntask = S.ntask
        return nc


def _tables():
    slopes = 2.0 ** (-(np.arange(8) + 1.0))
    s = np.arange(128)[:, None, None]
    t = np.arange(128)[None, None, :]
    sl = slopes[None, :, None]
    NEG = -30000.0
    B0 = np.where(s >= t, -sl * (t + 128 - s), NEG) + 0.0 * sl
    B1 = np.where(s <= t, -sl * (t - s), NEG) + 0.0 * sl
    B0 = np.broadcast_to(B0, (128, 8, 128)).astype(np.float32)
    B1 = np.broadcast_to(B1, (128, 8, 128)).astype(np.float32)
    B0s = np.zeros((128, 2, 16, 4, 8), np.float32)
    for gI in range(2):
        for hh in range(4):
            B0s[:, gI, :, hh, :] = B0[:, gI * 4 + hh, None, 0:8]
    B1bd = np.full((16, 8, 8, 16, 8), NEG, np.float32)
    m_bd = np.zeros((16, 8, 16, 8), np.float32)
    for q in range(16):
        B1bd[q, :, :, q, :] = B1[0:8, :, 0:8]
        m_bd[q, :, q, :] = (np.arange(8)[:, None] <= np.arange(8)[None, :])
    tabs = np.stack([B0.reshape(128, 1024), B1.reshape(128, 1024), B0s.reshape(128, 1024), B1bd.reshape(128, 1024)])
    m_tril = (np.arange(128)[:, None] <= np.arange(128)[None, :]).astype(np.float32)
    masks = np.stack([m_tril, m_bd.reshape(128, 128)])
    return np.ascontiguousarray(tabs, dtype=np.float32), np.ascontiguousarray(masks, dtype=np.float32)


_QPERM = np.concatenate([np.concatenate([np.arange(c * 64, c * 64 + 64), np.arange((4 + c) * 64, (4 + c) * 64 + 64)]) for c in range(4)])

_NC_CACHE = {}


def _get_nc(nseq, seqlen, with_sample=True):
    key = (nseq, seqlen, with_sample)
    if key not in _NC_CACHE:
        _NC_CACHE[key] = Builder(nseq, seqlen, with_sample).build()
    return _NC_CACHE[key]


def _shared_maps(inp):
    f = lambda a: np.ascontiguousarray(np.asarray(a), dtype=np.float32)
    w_in = f(inp["w_in"])
    w_in_p = w_in.copy()
    w_in_p[:, :, 0:512] = w_in[:, :, _QPERM]
    tabs, masks = _tables()
    return {
        "w_in": w_in_p, "w_gate": f(inp["w_gate"]), "b_gate": f(inp["b_gate"]).reshape(L, 24, 128),
        "gln_g": f(inp["gmlp_ln_g"]), "gln_b": f(inp["gmlp_ln_b"]), "gws": f(inp["gmlp_ws"]),
        "gbs": f(inp["gmlp_bs"]).reshape(L, 512), "mixw": f(inp["mixconv_w"]).reshape(L, 12, 128),
        "sinks": f(inp["attn_sinks"]), "p_attn": f(inp["p_attn"]), "p_gmlp": f(inp["p_gmlp"]), "p_conv": f(inp["p_conv"]),
        "w_o": f(inp["w_o"]), "ln1_g": f(inp["ln1_g"]), "ln1_b": f(inp["ln1_b"]), "w_up": f(inp["w_up"]),
        "fcw": f(inp["ffn_conv_w"]).reshape(L, 132, 128), "fcb": f(inp["ffn_conv_b"]).reshape(L, 44, 128),
        "w_down": f(inp["w_down"]), "ln2_g": f(inp["ln2_g"]), "ln2_b": f(inp["ln2_b"]),
        "tabs": tabs, "masks": masks,
    }


def run_cores(inp, ncores, nseq, seqlen):
    f = lambda a: np.ascontiguousarray(np.asarray(a), dtype=np.float32)
    shared = _shared_maps(inp)
    xp, xs = f(inp["x_prompt"]), f(inp["x_sample"])
    ck, cv = f(inp["cache_k_win"]), f(inp["cache_v_win"])
    sm, sf = f(inp["state_mixconv"]), f(inp["state_ffnconv"])
    in_maps = []
    for i in range(ncores):
        m = dict(shared)
        m["xp"] = np.ascontiguousarray(xp[i * nseq:(i + 1) * nseq])
        m["xs"] = np.ascontiguousarray(xs[i * 16:(i + 1) * 16].reshape(128, D))
        m["ck"] = np.ascontiguousarray(ck[:, i * 16:(i + 1) * 16].reshape(L, 16, 128, 128))
        m["cv"] = np.ascontiguousarray(cv[:, i * 16:(i + 1) * 16].reshape(L, 16, 128, 128))
        m["smix"] = np.ascontiguousarray(sm[:, i * 16:(i + 1) * 16].reshape(L, 32, 512))
        m["sffn"] = np.ascontiguousarray(sf[:, i * 16:(i + 1) * 16].reshape(L, 32, 2 * DFF))
        in_maps.append(m)
    nc = _get_nc(nseq, seqlen)
    res = run_bass_kernel_spmd(nc, in_maps, core_ids=list(range(ncores)))
    R = res.results
    cat = lambda name, ax: np.concatenate([np.asarray(r[name]) for r in R], axis=ax)
    nb = ncores * nseq
    ns = ncores * 16
    outs = (
        cat("y_p", 0).reshape(nb, seqlen, D),
        cat("y_s", 0).reshape(ns, 8, D),
        cat("kwp", 1).reshape(L, nb, 128, 2, 64),
        cat("vwp", 1).reshape(L, nb, 128, 2, 64),
        cat("mcp", 1).reshape(L, nb, 2, 512),
        cat("fcp", 1).reshape(L, nb, 2, 2 * DFF),
        cat("kws", 1).reshape(L, ns, 128, 2, 64),
        cat("vws", 1).reshape(L, ns, 128, 2, 64),
        cat("mcs", 1).reshape(L, ns, 2, 512),
        cat("fcs", 1).reshape(L, ns, 2, 2 * DFF),
        cat("gvs", 1).reshape(L, ns, 8, 512),
    )
    return tuple(np.ascontiguousarray(o, dtype=np.float32) for o in outs)


def kernel(**inputs):
    return run_cores(inputs, 8, 2, 2048)
```
